# Optimizing a Trainium2 kernel written in Bass

```python
import math, functools
import jax
import jax.numpy as jnp
from jax import lax
import numpy as np

D_MODEL = 1024
BATCH = 32
SEQ = 2048
DEPTH = 4

GRID_W = 64
CTX_LEN = 256
N_MIXERS = 4
CHUNK = 128
Q_BLOCK = 128
CONV_K = 5
ROPE_BASE = 10000.0
LN_EPS = 1e-5
RMS_EPS = 1e-6
ALPHA = (2.0 * DEPTH) ** 0.25
BETA = (8.0 * DEPTH) ** -0.25

A_DI = 2 * D_MODEL
A_HEADS = 4
A_HD = A_DI // A_HEADS

B_DI = 2 * D_MODEL
B_HD = 64
B_HEADS = B_DI // B_HD
B_GROUPS = 8
B_STATE = 128
B_CONV_CH = B_DI + 2 * B_GROUPS * B_STATE

C_HD = 64
C_HEADS = D_MODEL // (2 * C_HD)

D_HEADS = 4
D_QK = D_MODEL // D_HEADS
D_V = 2 * D_MODEL // D_HEADS

P_HEADS = 8
P_NKEYS = 128
P_EXPERTS = P_NKEYS * P_NKEYS
P_DK = 256
P_TOPK = 16
P_BLOCK = 128

kernel_name = 'hybrid_mlstm_ssd_diffattn_retention_peer'

F32 = jnp.float32


def n_mixer_layers(m):
    return len(range(m, DEPTH, N_MIXERS))


def layer_norm(x, g, b):
    xf = x.astype(F32)
    mu = jnp.mean(xf, axis=-1, keepdims=True)
    var = jnp.mean(jnp.square(xf - mu), axis=-1, keepdims=True)
    return ((xf - mu) * lax.rsqrt(var + LN_EPS)).astype(x.dtype) * g + b


def rms_norm(x, g):
    xf = x.astype(F32)
    return (xf * lax.rsqrt(jnp.mean(xf * xf, axis=-1, keepdims=True) + RMS_EPS)).astype(x.dtype) * g


def group_rms_norm(x, g, groups):
    bn, t, ch = x.shape
    xf = x.astype(F32).reshape(bn, t, groups, ch // groups)
    y = xf * lax.rsqrt(jnp.mean(xf * xf, axis=-1, keepdims=True) + RMS_EPS)
    return y.reshape(bn, t, ch).astype(x.dtype) * g


def head_norm(x, g):
    xf = x.astype(F32)
    mu = jnp.mean(xf, axis=-1, keepdims=True)
    var = jnp.mean(jnp.square(xf - mu), axis=-1, keepdims=True)
    y = (xf - mu) * lax.rsqrt(var + LN_EPS)
    return y.reshape(x.shape[0], x.shape[1], -1) * g.astype(F32)


def axial_positions(n_tokens):
    rows = n_tokens // GRID_W
    row = jnp.repeat(jnp.arange(rows, dtype=jnp.int32), GRID_W)
    col = jnp.tile(jnp.arange(GRID_W, dtype=jnp.int32), rows)
    return row, col


def rope_1d(x, pos):
    half = x.shape[-1] // 2
    freqs = ROPE_BASE ** (-jnp.arange(half, dtype=F32) / half)
    ang = pos.astype(F32)[:, None] * freqs
    cos = jnp.cos(ang)[:, None, :].astype(x.dtype)
    sin = jnp.sin(ang)[:, None, :].astype(x.dtype)
    x1, x2 = x[..., :half], x[..., half:]
    return jnp.concatenate([x1 * cos - x2 * sin, x1 * sin + x2 * cos], axis=-1)


def rope_2d(x):
    row, col = axial_positions(x.shape[1])
    h = x.shape[-1] // 2
    return jnp.concatenate([rope_1d(x[..., :h], row), rope_1d(x[..., h:], col)], axis=-1)


def dw_conv(x, w, b):
    ch = x.shape[-1]
    y = lax.conv_general_dilated(x, w[:, None, :].astype(x.dtype), window_strides=(1,),
                                 padding=[(CONV_K // 2, CONV_K // 2)],
                                 dimension_numbers=('NWC', 'WIO', 'NWC'), feature_group_count=ch)
    return y + b


def to_chunks(t):
    bn, n = t.shape[:2]
    return jnp.moveaxis(t.reshape((bn, n // CHUNK, CHUNK) + t.shape[2:]), 1, 0)


def from_chunks(t):
    t = jnp.moveaxis(t, 0, 1)
    return t.reshape((t.shape[0], -1) + t.shape[3:])


def flip_time(t):
    return jnp.flip(t, axis=1)


def bidirectional_prefix_scan(scan_fwd, scan_bwd, ctx_fwd, ctx_bwd, lat_fwd, lat_bwd, init):
    y_cf, s_f = scan_fwd(ctx_fwd, init)
    y_cb, s_b = scan_bwd(tuple(flip_time(t) for t in ctx_bwd), init)
    y_lf, _ = scan_fwd(lat_fwd, s_f)
    y_lb, _ = scan_bwd(tuple(flip_time(t) for t in lat_bwd), s_b)
    return y_cf + flip_time(y_cb), y_lf + flip_time(y_lb)


def mlstm_chunk_scan(inputs, state):
    causal = jnp.tril(jnp.ones((CHUNK, CHUNK), dtype=bool))

    def step(carry, chunk):
        cmat, nvec, m = carry
        q, k, v, ig, lf = chunk
        b = jnp.swapaxes(jnp.cumsum(lf, axis=1), 1, 2)
        ig = jnp.swapaxes(ig, 1, 2)
        logw = jnp.where(causal, b[..., :, None] - b[..., None, :] + ig[..., None, :], -jnp.inf)
        inter = b + m[..., None]
        m_t = jnp.maximum(inter, jnp.max(logw, axis=-1))
        s = jnp.einsum('blhd,bshd->bhls', q, k) * jnp.exp(logw - m_t[..., None])
        a = jnp.exp(inter - m_t)
        num = jnp.einsum('bhls,bshd->bhld', s, v) + a[..., None] * jnp.einsum('bhvk,blhk->bhlv', cmat, q)
        den = jnp.sum(s, axis=-1) + a * jnp.einsum('bhk,blhk->bhl', nvec, q)
        h = num / jnp.maximum(jnp.abs(den), jnp.exp(-m_t))[..., None]
        b_last = b[..., -1]
        g = b_last[..., None] - b + ig
        m_new = jnp.maximum(b_last + m, jnp.max(g, axis=-1))
        carry_decay = jnp.exp(b_last + m - m_new)
        wg = jnp.exp(g - m_new[..., None])
        cmat = carry_decay[..., None, None] * cmat + jnp.einsum('bhs,bshv,bshk->bhvk', wg, v, k)
        nvec = carry_decay[..., None] * nvec + jnp.einsum('bhs,bshk->bhk', wg, k)
        return (cmat, nvec, m_new), jnp.swapaxes(h, 1, 2)

    state, h = lax.scan(step, state, tuple(to_chunks(t) for t in inputs))
    return from_chunks(h), state


def mlstm_mixer(h_ctx, h_lat, w_up, conv_w, conv_b, w_qk, w_v, w_gate, b_gate, norm_g, skip, w_down, need_ctx):
    def prep(h):
        bn, t, _ = h.shape
        xm, z, o_pre = jnp.split(h @ w_up, 3, axis=-1)
        xc = jax.nn.silu(dw_conv(xm, conv_w, conv_b))
        q, k = jnp.split(xc @ w_qk, 2, axis=-1)
        v = xm @ w_v
        gates = (jnp.concatenate([q, k, v], axis=-1) @ w_gate + b_gate).astype(F32).reshape(bn, t, 4, A_HEADS)
        heads = lambda a: a.astype(F32).reshape(bn, t, A_HEADS, A_HD)
        qkv = (heads(q) * A_HD ** -0.5, heads(k), heads(v))
        fwd = qkv + (gates[:, :, 0], jax.nn.log_sigmoid(gates[:, :, 1]))
        bwd = qkv + (gates[:, :, 2], jax.nn.log_sigmoid(gates[:, :, 3]))
        return fwd, bwd, (xc, z, o_pre)

    fc, bc, aux_c = prep(h_ctx)
    fl, bl, aux_l = prep(h_lat)
    bn = h_lat.shape[0]
    init = (jnp.zeros((bn, A_HEADS, A_HD, A_HD), F32), jnp.zeros((bn, A_HEADS, A_HD), F32),
            jnp.zeros((bn, A_HEADS), F32))
    hc, hl = bidirectional_prefix_scan(mlstm_chunk_scan, mlstm_chunk_scan, fc, bc, fl, bl, init)

    def out(hh, aux):
        xc, z, o_pre = aux
        bn2, t = xc.shape[:2]
        hh = jax.nn.sigmoid(o_pre.astype(F32)).reshape(bn2, t, A_HEADS, A_HD) * hh
        y = head_norm(hh, norm_g).astype(xc.dtype) + skip * xc
        return (y * jax.nn.silu(z)) @ w_down

    return (out(hc, aux_c) if need_ctx else None), out(hl, aux_l)


def ssd_chunk_scan(a_rate, inputs, state):
    causal = jnp.tril(jnp.ones((CHUNK, CHUNK), dtype=bool))

    def step(s_prev, chunk):
        x, dt, bm, cm = chunk
        acs = jnp.moveaxis(jnp.cumsum(dt * a_rate, axis=1), 1, -1)
        seg = jnp.exp(jnp.where(causal, acs[..., :, None] - acs[..., None, :], -jnp.inf))
        xdt = x * dt[..., None]
        cb = jnp.einsum('blgn,bsgn->bgls', cm, bm)
        y = (jnp.einsum('bgls,bgrls,bsgrp->blgrp', cb, seg, xdt)
             + jnp.einsum('blgn,bgrpn,bgrl->blgrp', cm, s_prev, jnp.exp(acs)))
        a_last = acs[..., -1]
        s_new = (jnp.exp(a_last)[..., None, None] * s_prev
                 + jnp.einsum('bgrs,bsgrp,bsgn->bgrpn', jnp.exp(a_last[..., None] - acs), xdt, bm))
        return s_new, y

    state, y = lax.scan(step, state, tuple(to_chunks(t) for t in inputs))
    return from_chunks(y), state


def mamba2_mixer(h_ctx, h_lat, w_in, conv_w, conv_b, dt_bias, a_log, d_skip, norm_g, w_out, need_ctx):
    r = B_HEADS // B_GROUPS
    a_rate = -jnp.exp(a_log.astype(F32)).reshape(2, B_GROUPS, r)

    def prep(h):
        bn, t, _ = h.shape
        z, xbc, dt_raw = jnp.split(h @ w_in, [B_DI, B_DI + B_CONV_CH], axis=-1)
        xbc = jax.nn.silu(dw_conv(xbc, conv_w, conv_b))
        xs, bm, cm = jnp.split(xbc, [B_DI, B_DI + B_GROUPS * B_STATE], axis=-1)
        xs = xs.astype(F32).reshape(bn, t, B_GROUPS, r, B_HD)
        bm = bm.astype(F32).reshape(bn, t, B_GROUPS, B_STATE)
        cm = cm.astype(F32).reshape(bn, t, B_GROUPS, B_STATE)
        dt = jax.nn.softplus(dt_raw.astype(F32).reshape(bn, t, 2, B_GROUPS, r)
                             + dt_bias.astype(F32).reshape(2, B_GROUPS, r))
        return (xs, dt[:, :, 0], bm, cm), (xs, dt[:, :, 1], bm, cm), (xs, z)

    fc, bc, aux_c = prep(h_ctx)
    fl, bl, aux_l = prep(h_lat)
    init = jnp.zeros((h_lat.shape[0], B_GROUPS, r, B_HD, B_STATE), F32)
    yc, yl = bidirectional_prefix_scan(functools.partial(ssd_chunk_scan, a_rate[0]),
                                       functools.partial(ssd_chunk_scan, a_rate[1]), fc, bc, fl, bl, init)

    def out(y, aux):
        xs, z = aux
        bn, t = z.shape[:2]
        y = y + d_skip.astype(F32).reshape(B_GROUPS, r)[..., None] * xs
        y = y.reshape(bn, t, B_DI).astype(z.dtype) * jax.nn.silu(z)
        return group_rms_norm(y, norm_g, B_GROUPS) @ w_out

    return (out(yc, aux_c) if need_ctx else None), out(yl, aux_l)


def diff_attention_mixer(h_ctx, h_lat, w_qkv, lam_vecs, norm_g, w_out, layer_idx, need_ctx):
    lam_init = 0.8 - 0.6 * math.exp(-0.3 * layer_idx)
    lv = lam_vecs.astype(F32)
    lam = jnp.exp(jnp.sum(lv[0] * lv[1])) - jnp.exp(jnp.sum(lv[2] * lv[3])) + lam_init

    def prep(h):
        bn, t, _ = h.shape
        q, k, v = jnp.split(h @ w_qkv, 3, axis=-1)
        return (q.reshape(bn, t, 2 * C_HEADS, C_HD), k.reshape(bn, t, 2 * C_HEADS, C_HD),
                v.reshape(bn, t, C_HEADS, 2 * C_HD))

    qc, kc, vc = prep(h_ctx)
    ql, kl, vl = prep(h_lat)
    ql, kl = rope_2d(ql), rope_2d(kl)
    k_all = jnp.concatenate([kc, kl], axis=1)
    v_all = jnp.concatenate([vc, vl], axis=1)

    def attend(q, k, v):
        s = jnp.einsum('bqhd,bkhd->bhqk', q, k).astype(F32) * C_HD ** -0.5
        p = jax.nn.softmax(s, axis=-1)
        p = p.reshape(p.shape[0], C_HEADS, 2, p.shape[2], p.shape[3])
        p = p[:, :, 0] - lam * p[:, :, 1]
        return jnp.einsum('bhqk,bkhe->bqhe', p.astype(v.dtype), v)

    bn, t = ql.shape[:2]
    q_blocks = jnp.moveaxis(ql.reshape(bn, t // Q_BLOCK, Q_BLOCK, 2 * C_HEADS, C_HD), 1, 0)
    o_l = lax.map(lambda qb: attend(qb, k_all, v_all), q_blocks)
    o_l = jnp.moveaxis(o_l, 0, 1).reshape(bn, t, C_HEADS, 2 * C_HD)

    def out(o):
        bn2, t2 = o.shape[:2]
        o = rms_norm(o, norm_g) * (1.0 - lam_init)
        return o.reshape(bn2, t2, D_MODEL) @ w_out

    return (out(attend(qc, kc, vc)) if need_ctx else None), out(o_l)


def retention_chunk_scan(log_gamma, inputs, state):
    pos = jnp.arange(CHUNK, dtype=F32)
    rel = pos[:, None] - pos[None, :]
    decay = jnp.where(rel >= 0, jnp.exp(jnp.maximum(rel, 0.0) * log_gamma[:, None, None]), 0.0)
    q_decay = jnp.exp((pos + 1.0) * log_gamma[:, None]).T
    k_decay = jnp.exp((CHUNK - 1.0 - pos) * log_gamma[:, None])
    chunk_decay = jnp.exp(CHUNK * log_gamma)

    def step(s_prev, chunk):
        q, k, v = chunk
        s = jnp.einsum('blhd,bshd->bhls', q, k) * decay
        o = (jnp.einsum('bhls,bshv->blhv', s, v)
             + jnp.einsum('blhd,bhdv->blhv', q, s_prev) * q_decay[None, :, :, None])
        s_new = chunk_decay[:, None, None] * s_prev + jnp.einsum('bshd,bshv,hs->bhdv', k, v, k_decay)
        return s_new, o

    state, o = lax.scan(step, state, tuple(to_chunks(t) for t in inputs))
    return from_chunks(o), state


def retention_mixer(h_ctx, h_lat, w_in, decay_logit, norm_g, w_out, need_ctx):
    log_gamma = jax.nn.log_sigmoid(decay_logit.astype(F32))

    def prep(h, rotary):
        bn, t, _ = h.shape
        q, k, v, g = jnp.split(h @ w_in, [D_MODEL, 2 * D_MODEL, 4 * D_MODEL], axis=-1)
        q = q.reshape(bn, t, D_HEADS, D_QK) * D_QK ** -0.5
        k = k.reshape(bn, t, D_HEADS, D_QK)
        if rotary:
            q, k = rope_2d(q), rope_2d(k)
        v = v.reshape(bn, t, D_HEADS, D_V)
        return (q.astype(F32), k.astype(F32), v.astype(F32)), g

    in_c, g_c = prep(h_ctx, False)
    in_l, g_l = prep(h_lat, True)
    init = jnp.zeros((h_lat.shape[0], D_HEADS, D_QK, D_V), F32)
    o_c, o_l = bidirectional_prefix_scan(functools.partial(retention_chunk_scan, log_gamma[0]),
                                         functools.partial(retention_chunk_scan, log_gamma[1]),
                                         in_c, in_c, in_l, in_l, init)

    def out(o, g):
        return (head_norm(o, norm_g).astype(g.dtype) * jax.nn.silu(g)) @ w_out

    return (out(o_c, g_c) if need_ctx else None), out(o_l, g_l)


def peer_ffn(h, wq, sub_keys, u, v):
    bn, t, dm = h.shape
    blocks = h.reshape(bn * t // P_BLOCK, P_BLOCK, dm)

    def block(xb):
        q = (xb @ wq).reshape(P_BLOCK, P_HEADS, 2, P_DK // 2)
        s = jnp.einsum('thcd,hcnd->thcn', q, sub_keys).astype(F32)
        s_top, i_top = lax.top_k(s, P_TOPK)
        cand_s = (s_top[:, :, 0, :, None] + s_top[:, :, 1, None, :]).reshape(P_BLOCK, P_HEADS, P_TOPK * P_TOPK)
        cand_i = (i_top[:, :, 0, :, None] * P_NKEYS + i_top[:, :, 1, None, :]).reshape(P_BLOCK, P_HEADS, P_TOPK * P_TOPK)
        best_s, best_j = lax.top_k(cand_s, P_TOPK)
        idx = jnp.take_along_axis(cand_i, best_j, axis=-1)
        gate = jax.nn.softmax(best_s, axis=-1)
        act = jax.nn.gelu(jnp.einsum('td,thkd->thk', xb, u[idx]).astype(F32))
        return jnp.einsum('thk,thkd->td', (gate * act).astype(xb.dtype), v[idx])

    return lax.map(block, blocks).reshape(bn, t, dm)


def modulation(cond, w, b):
    return jnp.split(jax.nn.silu(cond) @ w + b, 6, axis=-1)


def setup_inputs(seed: int = 0) -> dict:
    key = jax.random.key(seed)
    keys = iter(jax.random.split(key, 64))

    def nrm(shape, scale):
        return jax.random.normal(next(keys), shape, F32) * scale

    def gain(shape):
        return 1.0 + nrm(shape, 0.05)

    dm = D_MODEL
    n_a, n_b, n_c, n_d = (n_mixer_layers(m) for m in range(N_MIXERS))
    a_i_bias = nrm((n_a, 2, 1, A_HEADS), 0.1)
    a_f_bias = jnp.linspace(3.0, 6.0, A_HEADS, dtype=F32) + nrm((n_a, 2, 1, A_HEADS), 0.1)
    a_b_gate = jnp.concatenate([a_i_bias, a_f_bias], axis=2).reshape(n_a, 4 * A_HEADS)
    dt0 = jnp.exp(jax.random.uniform(next(keys), (n_b, 2, B_HEADS), F32, math.log(1e-3), math.log(1e-1)))
    dt_bias = dt0 + jnp.log(-jnp.expm1(-dt0))
    a_log = jnp.log(jax.random.uniform(next(keys), (n_b, 2, B_HEADS), F32, 1.0, 16.0))
    gam = 1.0 - 2.0 ** (-5.0 - jnp.arange(D_HEADS, dtype=F32))
    ret_logit = jnp.log(gam) - jnp.log1p(-gam) + nrm((n_d, 2, D_HEADS), 0.1)
    return {
        'x': nrm((BATCH, SEQ, dm), 1.0),
        'c': nrm((BATCH, dm), 1.0),
        'ctx': nrm((BATCH, CTX_LEN, dm), 1.0),
        'c_ctx': nrm((dm,), 1.0),
        'ada_w': nrm((DEPTH, dm, 6 * dm), 0.5 * dm ** -0.5),
        'ada_b': nrm((DEPTH, 6 * dm), 0.02),
        'ln_g': gain((DEPTH, 2, dm)),
        'ln_b': nrm((DEPTH, 2, dm), 0.02),
        'peer_wq': nrm((DEPTH, dm, P_HEADS * P_DK), dm ** -0.5),
        'peer_keys': nrm((DEPTH, P_HEADS, 2, P_NKEYS, P_DK // 2), (P_DK // 2) ** -0.5),
        'peer_u': nrm((DEPTH, P_EXPERTS, dm), dm ** -0.5),
        'peer_v': nrm((DEPTH, P_EXPERTS, dm), BETA * P_HEADS ** -0.5),
        'mlstm_w_up': nrm((n_a, dm, 3 * A_DI), dm ** -0.5),
        'mlstm_conv_w': nrm((n_a, CONV_K, A_DI), CONV_K ** -0.5),
        'mlstm_conv_b': nrm((n_a, A_DI), 0.02),
        'mlstm_w_qk': nrm((n_a, A_DI, 2 * A_DI), A_DI ** -0.5),
        'mlstm_w_v': nrm((n_a, A_DI, A_DI), A_DI ** -0.5),
        'mlstm_w_gate': nrm((n_a, 3 * A_DI, 4 * A_HEADS), (3 * A_DI) ** -0.5),
        'mlstm_b_gate': a_b_gate,
        'mlstm_norm_g': gain((n_a, A_DI)),
        'mlstm_skip': gain((n_a, A_DI)),
        'mlstm_w_down': nrm((n_a, A_DI, dm), BETA * A_DI ** -0.5),
        'ssd_w_in': nrm((n_b, dm, B_DI + B_CONV_CH + 2 * B_HEADS), dm ** -0.5),
        'ssd_conv_w': nrm((n_b, CONV_K, B_CONV_CH), CONV_K ** -0.5),
        'ssd_conv_b': nrm((n_b, B_CONV_CH), 0.02),
        'ssd_dt_bias': dt_bias,
        'ssd_a_log': a_log,
        'ssd_d': gain((n_b, B_HEADS)),
        'ssd_norm_g': gain((n_b, B_DI)),
        'ssd_w_out': nrm((n_b, B_DI, dm), BETA * B_DI ** -0.5),
        'diff_w_qkv': nrm((n_c, dm, 3 * dm), dm ** -0.5),
        'diff_lambda': nrm((n_c, 4, C_HD), 0.1),
        'diff_norm_g': gain((n_c, 2 * C_HD)),
        'diff_w_out': nrm((n_c, dm, dm), BETA * dm ** -0.5),
        'ret_w_in': nrm((n_d, dm, 6 * dm), dm ** -0.5),
        'ret_decay_logit': ret_logit,
        'ret_norm_g': gain((n_d, 2 * dm)),
        'ret_w_out': nrm((n_d, 2 * dm, dm), BETA * (2 * dm) ** -0.5),
    }


def reference(x, c, ctx, c_ctx, ada_w, ada_b, ln_g, ln_b, peer_wq, peer_keys, peer_u, peer_v,
              mlstm_w_up, mlstm_conv_w, mlstm_conv_b, mlstm_w_qk, mlstm_w_v, mlstm_w_gate, mlstm_b_gate,
              mlstm_norm_g, mlstm_skip, mlstm_w_down,
              ssd_w_in, ssd_conv_w, ssd_conv_b, ssd_dt_bias, ssd_a_log, ssd_d, ssd_norm_g, ssd_w_out,
              diff_w_qkv, diff_lambda, diff_norm_g, diff_w_out,
              ret_w_in, ret_decay_logit, ret_norm_g, ret_w_out):
    h_lat, h_ctx = x, ctx
    for i in range(DEPTH):
        last = i == DEPTH - 1
        kind, j = i % N_MIXERS, i // N_MIXERS
        ml = [t[:, None, :] for t in modulation(c, ada_w[i], ada_b[i])]
        mc = modulation(c_ctx, ada_w[i], ada_b[i])
        in_l = h_lat * (1.0 + ml[1]) + ml[0]
        in_c = h_ctx * (1.0 + mc[1]) + mc[0]
        if kind == 0:
            y_c, y_l = mlstm_mixer(in_c, in_l, mlstm_w_up[j], mlstm_conv_w[j], mlstm_conv_b[j], mlstm_w_qk[j],
                                   mlstm_w_v[j], mlstm_w_gate[j], mlstm_b_gate[j], mlstm_norm_g[j],
                                   mlstm_skip[j], mlstm_w_down[j], not last)
        elif kind == 1:
            y_c, y_l = mamba2_mixer(in_c, in_l, ssd_w_in[j], ssd_conv_w[j], ssd_conv_b[j], ssd_dt_bias[j],
                                    ssd_a_log[j], ssd_d[j], ssd_norm_g[j], ssd_w_out[j], not last)
        elif kind == 2:
            y_c, y_l = diff_attention_mixer(in_c, in_l, diff_w_qkv[j], diff_lambda[j], diff_norm_g[j],
                                            diff_w_out[j], i, not last)
        else:
            y_c, y_l = retention_mixer(in_c, in_l, ret_w_in[j], ret_decay_logit[j], ret_norm_g[j],
                                       ret_w_out[j], not last)
        h_lat = layer_norm(ALPHA * h_lat + ml[2] * y_l, ln_g[i, 0], ln_b[i, 0])
        f_l = peer_ffn(h_lat * (1.0 + ml[4]) + ml[3], peer_wq[i], peer_keys[i], peer_u[i], peer_v[i])
        h_lat = layer_norm(ALPHA * h_lat + ml[5] * f_l, ln_g[i, 1], ln_b[i, 1])
        if not last:
            h_ctx = layer_norm(ALPHA * h_ctx + mc[2] * y_c, ln_g[i, 0], ln_b[i, 0])
            f_c = peer_ffn(h_ctx * (1.0 + mc[4]) + mc[3], peer_wq[i], peer_keys[i], peer_u[i], peer_v[i])
            h_ctx = layer_norm(ALPHA * h_ctx + mc[5] * f_c, ln_g[i, 1], ln_b[i, 1])
    return h_lat
```

```python
import math
from contextlib import ExitStack
import numpy as np
import concourse.bass as bass
import concourse.mybir as mybir
from concourse.bass_utils import run_bass_kernel_spmd

F32 = mybir.dt.float32
I32 = mybir.dt.int32
U32 = mybir.dt.uint32
AF = mybir.ActivationFunctionType
ALU = mybir.AluOpType
AX = mybir.AxisListType

D = 1024
ALPHA = (2.0 * 4) ** 0.25
LN_EPS = 1e-5
RMS_EPS = 1e-6
NEG = -1.0e30
GRID_W = 64
ROPE_BASE = 10000.0


class KB:
    def __init__(self, nc, es, n_slots=72):
        self.nc = nc
        self.es = es
        self.engs = {'pe': nc.tensor, 'act': nc.scalar, 'dve': nc.vector, 'pool': nc.gpsimd, 'sp': nc.sync}
        self.sem = {e: es.enter_context(nc.semaphore("s_" + e)) for e in self.engs}
        self.cnt = {e: 0 for e in self.engs}
        self.dsem = [es.enter_context(nc.semaphore("d%d" % i)) for i in range(n_slots)]
        self.dval = [0] * n_slots
        self.dnext = 0
        self.seen = {e: {} for e in self.engs}
        self.lastw = {}
        self.readers = {}
        self.nins = 0

    def _wait(self, eng, tok):
        sk, v = tok
        if eng == 'pe' and sk == ('e', 'pe'):
            return
        if self.seen[eng].get(sk, 0) >= v:
            return
        sem = self.sem[sk[1]] if sk[0] == 'e' else self.dsem[sk[1]]
        self.engs[eng].wait_ge(sem, v)
        self.seen[eng][sk] = v

    def _deps(self, eng, r, w):
        for k in r:
            t = self.lastw.get(k)
            if t is not None:
                self._wait(eng, t)
        for k in w:
            t = self.lastw.get(k)
            if t is not None:
                self._wait(eng, t)
            rd = self.readers.get(k)
            if rd:
                for sk, v in rd.items():
                    self._wait(eng, (sk, v))

    def _commit(self, tok, r, w):
        sk, v = tok
        for k in r:
            d = self.readers.get(k)
            if d is None:
                d = {}
                self.readers[k] = d
            if d.get(sk, 0) < v:
                d[sk] = v
        for k in w:
            self.lastw[k] = tok
            self.readers[k] = {}

    def op(self, eng, fn, r=(), w=()):
        self._deps(eng, r, w)
        ins = fn()
        self.cnt[eng] += 1
        ins.then_inc(self.sem[eng], 1)
        self._commit((('e', eng), self.cnt[eng]), r, w)
        self.nins += 1

    def dma(self, out, in_, r=(), w=(), q='sp', **kw):
        s = self.dnext
        self.dnext = (s + 1) % len(self.dsem)
        if self.dval[s] > 0:
            self._wait(q, (('d', s), self.dval[s]))
        self._deps(q, r, w)
        ins = self.engs[q].dma_start(out=out, in_=in_, **kw)
        self.dval[s] += 16
        ins.then_inc(self.dsem[s], 16)
        self._commit((('d', s), self.dval[s]), r, w)
        self.nins += 1

    def gather(self, out, table, idx_ap, r=(), w=()):
        q = 'pool'
        s = self.dnext
        self.dnext = (s + 1) % len(self.dsem)
        if self.dval[s] > 0:
            self._wait(q, (('d', s), self.dval[s]))
        self._deps(q, r, w)
        ins = self.nc.gpsimd.indirect_dma_start(
            out=out, out_offset=None, in_=table,
            in_offset=bass.IndirectOffsetOnAxis(ap=idx_ap, axis=0))
        self.dval[s] += 16
        ins.then_inc(self.dsem[s], 16)
        self._commit((('d', s), self.dval[s]), r, w)
        self.nins += 1

    def drain(self):
        for s in range(len(self.dsem)):
            if self.dval[s] > 0:
                self._wait('sp', (('d', s), self.dval[s]))
        for e in ('pe', 'act', 'dve', 'pool'):
            if self.cnt[e] > 0:
                self._wait('sp', (('e', e), self.cnt[e]))


class Ring:
    def __init__(self, tiles):
        self.tiles = tiles
        self.i = 0

    def next(self):
        t = self.tiles[self.i]
        self.i = (self.i + 1) % len(self.tiles)
        return t


def kF(name, r0, r1, t0, t1):
    return [(name, rc, tb) for rc in range(r0 // 128, (r1 + 127) // 128) for tb in range(t0 // 128, (t1 + 127) // 128)]


def kT(name, t0, t1, c0, c1):
    return [(name, tb, cb) for tb in range(t0 // 128, (t1 + 127) // 128) for cb in range(c0 // 128, (c1 + 127) // 128)]


def rope_tables(d, seq):
    h = d // 2
    q = h // 2
    t = np.arange(seq)
    row = (t // GRID_W).astype(np.float32)
    col = (t % GRID_W).astype(np.float32)
    freqs = (np.float32(ROPE_BASE) ** (-np.arange(q, dtype=np.float32) / np.float32(q))).astype(np.float32)
    cos = np.zeros((d, seq), np.float32)
    sin = np.zeros((d, seq), np.float32)
    for f in range(d):
        pos = row if f < h else col
        i = f % q
        ang = (pos * freqs[i]).astype(np.float32)
        sgn = -1.0 if (f % h) < q else 1.0
        cos[f] = np.cos(ang).astype(np.float32)
        sin[f] = (sgn * np.sin(ang)).astype(np.float32)
    return cos, sin


def make_consts(seq):
    ident = np.eye(128, dtype=np.float32)
    s = np.arange(128)[:, None]
    l = np.arange(128)[None, :]
    trif = (s <= l).astype(np.float32)
    trib = (s >= l).astype(np.float32)
    mnf = np.where(s <= l, 0.0, NEG).astype(np.float32)
    mnb = np.where(s >= l, 0.0, NEG).astype(np.float32)
    ones = np.ones((128, 128), np.float32)
    cm = np.stack([ident, trif, trib, mnf, mnb, ones], 0)
    c256, s256 = rope_tables(256, seq)
    c64, s64 = rope_tables(64, seq)
    rope_ret = np.stack([c256.reshape(2, 128, seq), s256.reshape(2, 128, seq)], 0)
    rope_dif = np.stack([np.concatenate([c64, c64], 0), np.concatenate([s64, s64], 0)], 0)
    iota16 = np.tile(np.arange(16, dtype=np.float32)[None, :], (128, 1))
    return dict(cm=cm, rope_ret=np.ascontiguousarray(rope_ret), rope_dif=np.ascontiguousarray(rope_dif), iota16=iota16)


class Prog:
    def __init__(self, NB, CTX, SEQ, kinds, layer_ids=None, final_last=True):
        self.NB, self.CTX, self.SEQ = NB, CTX, SEQ
        self.kinds = list(kinds)
        self.L = len(kinds)
        self.layer_ids = list(layer_ids) if layer_ids is not None else list(range(self.L))
        self.final_last = final_last
        self.T = CTX + SEQ
        self.NCH = self.T // 128
        self.NCC = CTX // 128
        self.nc = bass.Bass("TRN2", target_bir_lowering=False)
        self.es = ExitStack()
        self.inputs = {}
        self.build()

    def din(self, name, shape, dtype=F32):
        ap = self.nc.dram_tensor(name, list(shape), dtype, kind="ExternalInput").ap()
        self.inputs[name] = ap
        return ap

    def dscr(self, name, shape, dtype=F32):
        return self.nc.dram_tensor(name, list(shape), dtype, kind="Internal").ap()

    def sb(self, name, shape, dtype=F32):
        return self.es.enter_context(self.nc.sbuf_tensor("sb_" + name, list(shape), dtype))

    def groups(self, tgmax, include_ctx=True):
        gs = []
        if include_ctx:
            t = 0
            while t < self.CTX:
                g = min(tgmax, self.CTX - t)
                gs.append((t, g))
                t += g
        t = self.CTX
        while t < self.T:
            g = min(tgmax, self.T - t)
            gs.append((t, g))
            t += g
        return gs

    def build(self):
        nc, es = self.nc, self.es
        NB, CTX, SEQ, T, NCH = self.NB, self.CTX, self.SEQ, self.T, self.NCH
        kb = KB(nc, es)
        self.kb = kb
        self.x = self.din("x", [NB, SEQ, D])
        self.cx = self.din("ctx", [NB, CTX, D])
        self.cT = self.din("cT", [D, NB + 1])
        self.out = nc.dram_tensor("out", [NB, SEQ, D], F32, kind="ExternalOutput").ap()
        self.c_cm = self.din("cm", [6, 128, 128])
        self.c_rr = self.din("rope_ret", [2, 2, 128, SEQ])
        self.c_rd = self.din("rope_dif", [2, 128, SEQ])
        self.c_i16 = self.din("iota16", [128, 16])
        W = []
        for li, kind in enumerate(self.kinds):
            w = {}
            p = "L%d_" % li
            w['ada_w'] = self.din(p + "ada_w", [D, 6 * D])
            w['ada_b'] = self.din(p + "ada_b", [6 * D])
            w['ln_g'] = self.din(p + "ln_g", [2, D])
            w['ln_b'] = self.din(p + "ln_b", [2, D])
            w['wq'] = self.din(p + "peer_wq", [D, 2048])
            w['keys'] = self.din(p + "peer_keys", [16, 128, 128])
            w['u'] = self.din(p + "peer_u", [16384, D])
            w['v'] = self.din(p + "peer_v", [16384, D])
            if kind == 0:
                w['w_up'] = self.din(p + "w_up", [D, 6144])
                w['conv_w'] = self.din(p + "conv_w", [5, 2048])
                w['conv_b'] = self.din(p + "conv_b", [2048])
                w['w_qk'] = self.din(p + "w_qk", [2048, 4096])
                w['w_v'] = self.din(p + "w_v", [2048, 2048])
                w['w_gate'] = self.din(p + "w_gate", [6144, 16])
                w['b_gate'] = self.din(p + "b_gate", [16])
                w['norm_g'] = self.din(p + "norm_g", [2048])
                w['skip'] = self.din(p + "skip", [2048])
                w['w_down'] = self.din(p + "w_down", [2048, D])
            elif kind == 1:
                w['w_in'] = self.din(p + "w_in", [D, 6208])
                w['conv_w'] = self.din(p + "conv_w", [5, 4096])
                w['conv_b'] = self.din(p + "conv_b", [4096])
                w['dt_bias'] = self.din(p + "dt_bias", [64])
                w['a_log'] = self.din(p + "a_log", [64])
                w['d'] = self.din(p + "d", [32])
                w['norm_g'] = self.din(p + "norm_g", [2048])
                w['w_out'] = self.din(p + "w_out", [2048, D])
            elif kind == 2:
                w['w_qkv'] = self.din(p + "w_qkv", [D, 3072])
                w['lam'] = self.din(p + "lam", [4, 64])
                w['norm_g'] = self.din(p + "norm_g", [128])
                w['w_out'] = self.din(p + "w_out", [D, D])
            else:
                w['w_in'] = self.din(p + "w_in", [D, 6144])
                w['decay'] = self.din(p + "decay", [8])
                w['norm_g'] = self.din(p + "norm_g", [2048])
                w['w_out'] = self.din(p + "w_out", [2048, D])
            W.append(w)
        self.W = W
        self.HT = self.dscr("HT", [D, T])
        self.YT = self.dscr("YT", [D, T])
        self.H1T = self.dscr("H1T", [D, T])
        self.QPT = self.dscr("QPT", [2048, T])
        self.A = [self.dscr("A%d" % i, [2048, T]) for i in range(5)]
        self.A4k = self.dscr("A4k", [4096, T])
        self.Bt = [self.dscr("B%d" % i, [T, 2048]) for i in range(3)]
        self.XC4k = self.dscr("XC4k", [4096, T])
        self.cm = self.sb("cm", [128, 6, 128])
        self.ident = self.cm[:, 0, :]
        self.trif = self.cm[:, 1, :]
        self.trib = self.cm[:, 2, :]
        self.mnf = self.cm[:, 3, :]
        self.mnb = self.cm[:, 4, :]
        self.ones = self.cm[:, 5, :]
        self.onesm = self.sb("onesm", [128, 128])
        self.i16 = self.sb("i16", [128, 16])
        self.mod = self.sb("mod", [128, self.L * 48, NB + 1])
        self.lng = self.sb("lng", [128, self.L * 2, 8])
        self.lnb = self.sb("lnb", [128, self.L * 2, 8])
        self.scT = self.sb("scT", [128, 8, NB + 1])
        self.adab = self.sb("adab", [128, 48])
        self.ringL = Ring([self.sb("bigL%d" % i, [128, 4096]) for i in range(1)])
        self.bigS = [self.sb("bigS%d" % i, [128, 4096]) for i in range(4)]
        self.ringS = Ring(self.bigS)
        self.stg = Ring([self.sb("stg%d" % i, [128, 512]) for i in range(3)])
        self.smt = [self.sb("sm%d" % i, [128, 512]) for i in range(8)]
        self.sm_q = Ring(self.smt[0:2])
        self.sm_k = Ring(self.smt[2:4])
        self.sm_v = Ring(self.smt[4:6])
        self.sm_w = Ring(self.smt[6:8])
        self.t2k = Ring([self.sb("t2k%d" % i, [128, 2048]) for i in range(5)])
        self.t1k = Ring([self.sb("t1k%d" % i, [128, 1024]) for i in range(5)])
        self.tiny = Ring([self.sb("tiny%d" % i, [128, 128]) for i in range(12)])
        self.par = self.sb("par", [128, 1100])
        self.fcol = self.sb("fcol", [128, NCH, 64])
        self.ldall = self.sb("ldall", [128, NCH, 64])
        self.igall = self.sb("igall", [128, NCH, 64])
        self.ccol = self.igall
        self.keysT = self.sb("keysT", [128, 16, 128])
        self.ps = es.enter_context(nc.psum_tensor("ps", [128, 8, 512], F32))
        self.psr = Ring([4, 5, 6, 7])
        self._alt = 0

        kb.dma(self.cm[:], self.c_cm.rearrange("k p n -> p k n"), w=["cm"])
        kb.dma(self.i16[:], self.c_i16, w=["i16"])
        kb.op('dve', lambda: nc.vector.memset(self.onesm[:], 1.0 / 1024.0), w=["onesm"])
        self.preamble_mod()
        for b in range(NB):
            self.load_input(b)
            for li, kind in enumerate(self.kinds):
                last = self.final_last and (li == self.L - 1)
                if b == 0 or True:
                    self.load_layer_params(li)
                [self.mlstm, self.ssd, self.diffattn, self.retention][kind](li, b, last)
                self.post(li, b, last)
        kb.drain()
        self.es.close()

    def pbank(self):
        i = self.psr.next()
        return i, "ps%d" % i

    def evac(self, out, in_, r, w):
        nc = self.nc
        self._alt ^= 1
        if self._alt:
            self.kb.op('act', lambda: nc.scalar.copy(out=out, in_=in_), r=r, w=w)
        else:
            self.kb.op('dve', lambda: nc.vector.tensor_copy(out=out, in_=in_), r=r, w=w)

    def modcol(self, li, j, b):
        return self.mod[:, li * 48 + j * 8: li * 48 + j * 8 + 8, b:b + 1]

    def modkeys(self, li, j):
        return [("mod", li, j * 8 + c) for c in range(8)]

    def preamble_mod(self):
        nc, kb = self.nc, self.kb
        NB = self.NB
        tmp = self.smt[0]
        tv = tmp[:, :8 * (NB + 1)].rearrange("p (c b) -> p c b", c=8)
        kb.dma(tv, self.cT.rearrange("(c p) b -> p c b", p=128), w=[tmp.name])
        kb.op('act', lambda: nc.scalar.activation(out=self.scT[:], in_=tv, func=AF.Silu), r=[tmp.name], w=["scT"])
        for li in range(self.L):
            w = self.W[li]
            kb.dma(self.adab[:], w['ada_b'].rearrange("(c p) -> p c", p=128), w=["adab"], allow_slow_non_contiguous=True)
            kb.dma(self.lng[:, li * 2:li * 2 + 2, :], w['ln_g'].rearrange("a (c p) -> p a c", p=128), w=[("lng", li)],
                   allow_slow_non_contiguous=True)
            kb.dma(self.lnb[:, li * 2:li * 2 + 2, :], w['ln_b'].rearrange("a (c p) -> p a c", p=128), w=[("lnb", li)],
                   allow_slow_non_contiguous=True)
            for nb in range(12):
                wt = self.ringS.next()
                wv = wt[:, :4096].rearrange("p (k n) -> p k n", k=8)
                kb.dma(wv, w['ada_w'].rearrange("(k p) n -> p k n", p=128)[:, :, nb * 512:(nb + 1) * 512], w=[wt.name])
                for sub in range(4):
                    ch = nb * 4 + sub
                    bi, bk = self.pbank()
                    pv = self.ps[:, bi, 0:NB + 1]
                    for k in range(8):
                        kb.op('pe', lambda k=k: nc.tensor.matmul(pv, wv[:, k, sub * 128:(sub + 1) * 128], self.scT[:, k, :],
                                                                  start=(k == 0), stop=(k == 7)),
                              r=[wt.name, "scT"], w=[bk])
                    add1 = 1.0 if (ch // 8) in (1, 4) else 0.0
                    dst = self.mod[:, li * 48 + ch, :]
                    kb.op('dve', lambda: nc.vector.tensor_scalar(out=dst, in0=pv, scalar1=self.adab[:, ch:ch + 1], scalar2=add1,
                                                                 op0=ALU.add, op1=ALU.add),
                          r=[bk, "adab"], w=[("mod", li, ch)])

    def load_input(self, b):
        kb = self.kb
        for tb in range(self.NCH):
            t0 = tb * 128
            src = self.cx[b, t0:t0 + 128, :] if tb < self.NCC else self.x[b, t0 - self.CTX:t0 - self.CTX + 128, :]
            xt = self.t1k.next()
            kb.dma(xt[:, :D], src, w=[xt.name])
            self.transpose_store(xt, 8, self.HT, "HT", 0, t0)

    def transpose_to(self, src_tile, nchunk, dst_tile):
        nc, kb = self.nc, self.kb
        dv = dst_tile[:, :nchunk * 128].rearrange("p (c t) -> p c t", c=nchunk)
        for c0 in range(0, nchunk, 4):
            bi, bk = self.pbank()
            n = min(4, nchunk - c0)
            for j in range(n):
                c = c0 + j
                kb.op('pe', lambda c=c, j=j: nc.tensor.transpose(self.ps[:, bi, j * 128:(j + 1) * 128],
                                                                   src_tile[:, c * 128:(c + 1) * 128], self.ident),
                      r=[src_tile.name, "cm"], w=[bk])
            self.evac(dv[:, c0:c0 + n, :], self.ps[:, bi, :n * 128].rearrange("p (c t) -> p c t", c=n),
                      r=[bk], w=[dst_tile.name])
        return dv

    def transpose_store(self, src_tile, nchunk, dst_dram, dname, row0, t0, scale_cols=None):
        nc, kb = self.nc, self.kb
        ft = self.t1k.next() if nchunk <= 8 else self.t2k.next()
        dv = self.transpose_to(src_tile, nchunk, ft)
        if scale_cols is not None:
            kb.op('dve', lambda: nc.vector.tensor_tensor(out=dv, in0=dv, in1=scale_cols.unsqueeze(2).to_broadcast([128, nchunk, 128]),
                                                         op=ALU.mult), r=[ft.name, "par"], w=[ft.name])
        kb.dma(dst_dram[row0:row0 + nchunk * 128, t0:t0 + 128].rearrange("(c p) t -> p c t", p=128), dv,
               r=[ft.name], w=kF(dname, row0, row0 + nchunk * 128, t0, t0 + 128))

    def load_layer_params(self, li):
        nc, kb = self.nc, self.kb
        w = self.W[li]
        kind = self.kinds[li]
        par = self.par
        pk = ["par"]
        if kind == 0:
            for j in range(5):
                kb.dma(par[:, j * 16:(j + 1) * 16], w['conv_w'][j].rearrange("(c p) -> p c", p=128), w=pk, allow_slow_non_contiguous=True)
            kb.dma(par[:, 80:96], w['conv_b'].rearrange("(c p) -> p c", p=128), w=pk, allow_slow_non_contiguous=True)
            kb.dma(par[:, 96:112], w['norm_g'].rearrange("(c p) -> p c", p=128), w=pk, allow_slow_non_contiguous=True)
            kb.dma(par[:, 112:128], w['skip'].rearrange("(c p) -> p c", p=128), w=pk, allow_slow_non_contiguous=True)
            kb.dma(par[:, 128:144], w['b_gate'].partition_broadcast(128), w=pk)
            kb.dma(par[:, 256:1024].rearrange("p (c j) -> p c j", c=48), w['w_gate'].rearrange("(c p) j -> p c j", p=128), w=pk)
        elif kind == 1:
            for j in range(5):
                kb.dma(par[:, j * 32:(j + 1) * 32], w['conv_w'][j].rearrange("(c p) -> p c", p=128), w=pk, allow_slow_non_contiguous=True)
            kb.dma(par[:, 160:192], w['conv_b'].rearrange("(c p) -> p c", p=128), w=pk, allow_slow_non_contiguous=True)
            kb.dma(par[:, 192:256], w['dt_bias'].partition_broadcast(128), w=pk)
            kb.dma(par[:, 256:320], w['a_log'].partition_broadcast(128), w=pk)
            kb.op('act', lambda: nc.scalar.activation(out=par[:, 256:320], in_=par[:, 256:320], func=AF.Exp), r=pk, w=pk)
            kb.op('dve', lambda: nc.vector.tensor_scalar(out=par[:, 256:320], in0=par[:, 256:320], scalar1=-1.0, scalar2=None,
                                                         op0=ALU.mult), r=pk, w=pk)
            kb.dma(par[:, 320:352], w['d'].partition_broadcast(128), w=pk)
            kb.dma(par[:, 352:368], w['norm_g'].rearrange("(c p) -> p c", p=128), w=pk, allow_slow_non_contiguous=True)
        elif kind == 2:
            kb.dma(par[:, 0:256], w['lam'].rearrange("a b -> (a b)").partition_broadcast(128), w=pk)
            kb.dma(par[:, 256:384], w['norm_g'].partition_broadcast(128), w=pk)
            lam_init = 0.8 - 0.6 * math.exp(-0.3 * self.layer_ids[li])
            pv = par[:, 0:256].rearrange("p (a b c) -> p a b c", a=2, b=2)
            kb.op('dve', lambda: nc.vector.tensor_tensor(out=par[:, 512:640].rearrange("p (a c) -> p a c", a=2),
                                                         in0=pv[:, :, 0, :], in1=pv[:, :, 1, :], op=ALU.mult), r=pk, w=pk)
            kb.op('dve', lambda: nc.vector.reduce_sum(out=par[:, 402:404], in_=par[:, 512:640].rearrange("p (a c) -> p a c", a=2),
                                                      axis=AX.X), r=pk, w=pk)
            kb.op('act', lambda: nc.scalar.activation(out=par[:, 404:406], in_=par[:, 402:404], func=AF.Exp), r=pk, w=pk)
            kb.op('dve', lambda: nc.vector.tensor_tensor(out=par[:, 400:401], in0=par[:, 405:406], in1=par[:, 404:405],
                                                         op=ALU.subtract), r=pk, w=pk)
            kb.op('dve', lambda: nc.vector.tensor_scalar(out=par[:, 400:401], in0=par[:, 400:401], scalar1=-lam_init, scalar2=None,
                                                         op0=ALU.add), r=pk, w=pk)
        else:
            kb.dma(par[:, 0:8], w['decay'].partition_broadcast(128), w=pk)
            self.log_sigmoid(par[:, 0:8], par[:, 0:8], par[:, 8:16], pk)
            kb.dma(par[:, 16:32], w['norm_g'].rearrange("(c p) -> p c", p=128), w=pk, allow_slow_non_contiguous=True)
        for j0 in range(0, 16, 4):
            kt = self.t2k.next()
            kb.dma(kt[:, :512].rearrange("p (j d) -> p j d", j=4), w['keys'][j0:j0 + 4].rearrange("j n d -> n j d"), w=[kt.name])
            bi, bk = self.pbank()
            for j in range(4):
                kb.op('pe', lambda j=j: nc.tensor.transpose(self.ps[:, bi, j * 128:(j + 1) * 128], kt[:, j * 128:(j + 1) * 128],
                                                             self.ident), r=[kt.name, "cm"], w=[bk])
            self.evac(self.keysT[:, j0:j0 + 4, :], self.ps[:, bi, :].rearrange("p (j n) -> p j n", j=4), r=[bk], w=["keysT"])

    def log_sigmoid(self, out, in_, tmp, keys):
        nc, kb = self.nc, self.kb
        kb.op('act', lambda: nc.scalar.activation(out=tmp, in_=in_, func=AF.Exp, scale=-1.0), r=keys, w=keys)
        kb.op('act', lambda: nc.scalar.activation(out=tmp, in_=tmp, func=AF.Ln, bias=1.0, scale=1.0), r=keys, w=keys)
        kb.op('dve', lambda: nc.vector.tensor_scalar(out=out, in0=tmp, scalar1=-1.0, scalar2=None, op0=ALU.mult), r=keys, w=keys)

    def load_xin(self, src, sname, K, t0, tg, premod=None):
        nc, kb = self.nc, self.kb
        KC = K // 128
        xt = self.ringL.next()
        xv = xt[:, :KC * tg].rearrange("p (k t) -> p k t", k=KC)
        kb.dma(xv, src[0:K, t0:t0 + tg].rearrange("(k p) t -> p k t", p=128), r=kF(sname, 0, K, t0, t0 + tg), w=[xt.name])
        if premod is not None:
            li, js, jt, b = premod
            col = self.NB if t0 < self.CTX else b
            sc = self.modcol(li, js, col).to_broadcast([128, 8, tg])
            sh = self.modcol(li, jt, col).to_broadcast([128, 8, tg])
            kb.op('dve', lambda: nc.vector.tensor_tensor(out=xv, in0=xv, in1=sc, op=ALU.mult),
                  r=[xt.name] + self.modkeys(li, js), w=[xt.name])
            kb.op('dve', lambda: nc.vector.tensor_tensor(out=xv, in0=xv, in1=sh, op=ALU.add),
                  r=[xt.name] + self.modkeys(li, jt), w=[xt.name])
        return xt, xv

    def proj(self, src, sname, K, Wap, segs, premod=None, tgmax=None, include_ctx=True, groups=None):
        nc, kb = self.nc, self.kb
        KC = K // 128
        if tgmax is None:
            tgmax = 512 if KC <= 8 else 256
        wb = min(512, 4096 // KC)
        Wv = Wap.rearrange("(k p) n -> p k n", p=128)
        for (t0, tg) in (groups if groups is not None else self.groups(tgmax, include_ctx)):
            xt, xv = self.load_xin(src, sname, K, t0, tg, premod)
            blocks = []
            for (n0, n1, mode, epi) in segs:
                c = n0
                while c < n1:
                    bw = min(wb, n1 - c)
                    blocks.append((c, bw, mode, epi))
                    c += bw
            tiles = {}

            def loadw(i):
                c, bw, mode, epi = blocks[i]
                wt = self.ringS.next()
                wv = wt[:, :KC * bw].rearrange("p (k n) -> p k n", k=KC)
                kb.dma(wv, Wv[:, :, c:c + bw], w=[wt.name])
                tiles[i] = (wt, wv)
            loadw(0)
            for i in range(len(blocks)):
                if i + 1 < len(blocks):
                    loadw(i + 1)
                c, bw, mode, epi = blocks[i]
                wt, wv = tiles.pop(i)
                if mode == 'F':
                    for sub in range((bw + 127) // 128):
                        m = min(128, bw - sub * 128)
                        bi, bk = self.pbank()
                        pv = self.ps[:m, bi, :tg]
                        for k in range(KC):
                            kb.op('pe', lambda k=k: nc.tensor.matmul(pv, wv[:, k, sub * 128:sub * 128 + m], xv[:, k, :],
                                                                      start=(k == 0), stop=(k == KC - 1)),
                                  r=[wt.name, xt.name], w=[bk])
                        epi(pv, bk, (c + sub * 128) // 128, t0, tg)
                else:
                    for tt in range(tg // 128):
                        bi, bk = self.pbank()
                        pv = self.ps[:, bi, :bw]
                        for k in range(KC):
                            kb.op('pe', lambda k=k: nc.tensor.matmul(pv, xv[:, k, tt * 128:(tt + 1) * 128], wv[:, k, :],
                                                                      start=(k == 0), stop=(k == KC - 1)),
                                  r=[wt.name, xt.name], w=[bk])
                        epi(pv, bk, c, bw, t0 + tt * 128)

    def epiF(self, dst, dname, nbase):
        kb = self.kb

        def epi(pv, bk, cc, t0, tg):
            st = self.stg.next()
            sv = st[:, :tg]
            rc = cc - nbase
            self.evac(sv, pv, r=[bk], w=[st.name])
            kb.dma(dst[rc * 128:(rc + 1) * 128, t0:t0 + tg], sv, r=[st.name], w=kF(dname, rc * 128, rc * 128 + 128, t0, t0 + tg))
        return epi

    def epiT(self, dst, dname, cbase):
        kb = self.kb

        def epi(pv, bk, c, bw, tok0):
            st = self.stg.next()
            sv = st[:, :bw]
            self.evac(sv, pv, r=[bk], w=[st.name])
            kb.dma(dst[tok0:tok0 + 128, c - cbase:c - cbase + bw], sv, r=[st.name],
                   w=kT(dname, tok0, tok0 + 128, c - cbase, c - cbase + bw))
        return epi

    def proj_rope(self, src, sname, Wap, col0, ncols, dst, dname, premod, d):
        nc, kb = self.nc, self.kb
        Wv = Wap.rearrange("(k p) n -> p k n", p=128)
        q = d // 4
        nblk = 128 // (2 * q)
        for (t0, tg) in self.groups(512, True):
            lat = t0 >= self.CTX
            xt, xv = self.load_xin(src, sname, D, t0, tg, premod)
            for ch in range(ncols // 128):
                c = col0 + ch * 128
                wt = self.ringS.next()
                wv = wt[:, :2048].rearrange("p (k n) -> p k n", k=8)
                kb.dma(wv[:, :, 0:128], Wv[:, :, c:c + 128], w=[wt.name])
                if lat:
                    src4 = wv[:, :, 0:128].rearrange("p k (b h q) -> p k b h q", b=nblk, h=2)
                    dst4 = wv[:, :, 128:256].rearrange("p k (b h q) -> p k b h q", b=nblk, h=2)
                    if nblk == 1:
                        kb.op('pool', lambda: nc.gpsimd.tensor_copy(out=dst4[:, :, 0, 0, :], in_=src4[:, :, 0, 1, :]),
                              r=[wt.name], w=[(wt.name, 'p')])
                        kb.op('pool', lambda: nc.gpsimd.tensor_copy(out=dst4[:, :, 0, 1, :], in_=src4[:, :, 0, 0, :]),
                              r=[wt.name], w=[(wt.name, 'p')])
                    else:
                        kb.op('pool', lambda: nc.gpsimd.tensor_copy(out=dst4[:, :, :, 0, :], in_=src4[:, :, :, 1, :]),
                              r=[wt.name], w=[(wt.name, 'p')])
                        kb.op('pool', lambda: nc.gpsimd.tensor_copy(out=dst4[:, :, :, 1, :], in_=src4[:, :, :, 0, :]),
                              r=[wt.name], w=[(wt.name, 'p')])
                bi, bk = self.pbank()
                pv = self.ps[:, bi, :tg]
                for k in range(8):
                    kb.op('pe', lambda k=k: nc.tensor.matmul(pv, wv[:, k, 0:128], xv[:, k, :], start=(k == 0), stop=(k == 7)),
                          r=[wt.name, xt.name], w=[bk])
                st = self.stg.next()
                sv = st[:, :tg]
                if not lat:
                    self.evac(sv, pv, r=[bk], w=[st.name])
                else:
                    bi2, bk2 = self.pbank()
                    pv2 = self.ps[:, bi2, :tg]
                    for k in range(8):
                        kb.op('pe', lambda k=k: nc.tensor.matmul(pv2, wv[:, k, 128:256], xv[:, k, :], start=(k == 0), stop=(k == 7)),
                              r=[wt.name, (wt.name, 'p'), xt.name], w=[bk2])
                    ct = self.sm_k.next()
                    sn = self.sm_v.next()
                    s0 = t0 - self.CTX
                    if d == 256:
                        lc = ch % 2
                        kb.dma(ct[:, :tg], self.c_rr[0, lc, :, s0:s0 + tg], w=[ct.name])
                        kb.dma(sn[:, :tg], self.c_rr[1, lc, :, s0:s0 + tg], w=[sn.name])
                    else:
                        kb.dma(ct[:, :tg], self.c_rd[0, :, s0:s0 + tg], w=[ct.name])
                        kb.dma(sn[:, :tg], self.c_rd[1, :, s0:s0 + tg], w=[sn.name])
                    kb.op('dve', lambda: nc.vector.tensor_tensor(out=ct[:, :tg], in0=pv, in1=ct[:, :tg], op=ALU.mult),
                          r=[bk, ct.name], w=[ct.name])
                    kb.op('dve', lambda: nc.vector.tensor_tensor(out=sn[:, :tg], in0=pv2, in1=sn[:, :tg], op=ALU.mult),
                          r=[bk2, sn.name], w=[sn.name])
                    kb.op('dve', lambda: nc.vector.tensor_tensor(out=sv, in0=ct[:, :tg], in1=sn[:, :tg], op=ALU.add),
                          r=[ct.name, sn.name], w=[st.name])
                kb.dma(dst[ch * 128:(ch + 1) * 128, t0:t0 + tg], sv, r=[st.name], w=kF(dname, ch * 128, ch * 128 + 128, t0, t0 + tg))

    def conv(self, src, sname, nch, wcol, bcol, dst, dname):
        nc, kb = self.nc, self.kb
        segs = [(0, self.CTX), (self.CTX, self.T)]
        for c in range(nch):
            for (a, e) in segs:
                n = e - a
                for o in range(0, n, 2048):
                    m = min(2048, n - o)
                    xp = self.ringS.next()
                    ac = self.ringS.next()
                    lo = 2 if o == 0 else 0
                    hi = 2 if o + m == n else 0
                    if lo:
                        kb.op('pool', lambda: nc.gpsimd.memset(xp[:, 0:2], 0.0), w=[xp.name])
                    if hi:
                        kb.op('pool', lambda: nc.gpsimd.memset(xp[:, 2 + m:4 + m], 0.0), w=[xp.name])
                    s0 = a + o - (2 - lo)
                    s1 = a + o + m + (2 - hi)
                    kb.dma(xp[:, lo:lo + (s1 - s0)], src[c * 128:(c + 1) * 128, s0:s1],
                           r=kF(sname, c * 128, c * 128 + 128, s0, s1), w=[xp.name])
                    rk = [xp.name, "par"]
                    kb.op('dve', lambda: nc.vector.tensor_scalar(out=ac[:, :m], in0=xp[:, 0:m], scalar1=wcol(c, 0), scalar2=None,
                                                                 op0=ALU.mult), r=rk, w=[ac.name])
                    for j in range(1, 5):
                        kb.op('dve', lambda j=j: nc.vector.scalar_tensor_tensor(out=ac[:, :m], in0=xp[:, j:j + m], scalar=wcol(c, j),
                                                                                 in1=ac[:, :m], op0=ALU.mult, op1=ALU.add),
                              r=rk + [ac.name], w=[ac.name])
                    bc_ = bcol(c)
                    kb.op('act', lambda: nc.scalar.activation(out=ac[:, :m], in_=ac[:, :m], func=AF.Silu, bias=bc_, scale=1.0),
                          r=[ac.name, "par"], w=[ac.name])
                    kb.dma(dst[c * 128:(c + 1) * 128, a + o:a + o + m], ac[:, :m], r=[ac.name],
                           w=kF(dname, c * 128, c * 128 + 128, a + o, a + o + m))

    def decay_cols(self, NU, have_ig, scale):
        nc, kb = self.nc, self.kb
        NCH, NCC = self.NCH, self.NCC
        N2 = 2 * NU
        lns = math.log(scale)
        for lc in range(NCH):
            bi, bk = self.pbank()
            fs = [sc for sc in range(NCH) if sc <= lc]
            for i, sc in enumerate(fs):
                m = self.trif if sc == lc else self.ones
                kb.op('pe', lambda m=m, sc=sc, i=i: nc.tensor.matmul(self.ps[:, bi, 0:NU], m, self.ldall[:, sc, 0:NU],
                                                                     start=(i == 0), stop=(i == len(fs) - 1)),
                      r=["cm", "ldall"], w=[bk])
            if lc < NCC:
                bs = [sc for sc in range(lc, NCC)]
            else:
                bs = list(range(NCC)) + [sc for sc in range(lc, NCH)]
            for i, sc in enumerate(bs):
                m = self.trib if sc == lc else self.ones
                kb.op('pe', lambda m=m, sc=sc, i=i: nc.tensor.matmul(self.ps[:, bi, NU:N2], m, self.ldall[:, sc, NU:N2],
                                                                     start=(i == 0), stop=(i == len(bs) - 1)),
                      r=["cm", "ldall"], w=[bk])
            kb.op('dve', lambda: nc.vector.tensor_copy(out=self.fcol[:, lc, 0:N2], in_=self.ps[:, bi, 0:N2]), r=[bk], w=["fcol"])
            if have_ig:
                kb.op('dve', lambda: nc.vector.scalar_tensor_tensor(out=self.ccol[:, lc, 0:N2], in0=self.igall[:, lc, 0:N2], scalar=lns,
                                                                     in1=self.fcol[:, lc, 0:N2], op0=ALU.add, op1=ALU.subtract),
                      r=["fcol", "igall"], w=["igall"])
            else:
                kb.op('dve', lambda: nc.vector.tensor_scalar(out=self.ccol[:, lc, 0:N2], in0=self.fcol[:, lc, 0:N2], scalar1=-1.0,
                                                             scalar2=lns, op0=ALU.mult, op1=ALU.add), r=["fcol"], w=["igall"])

    def quad(self, G, R, KC, DV, sep, qsrc, qname, qrow0, ksrc, kname, krow0, vsrc, vname, post, lbs=None):
        nc, kb = self.nc, self.kb
        NCH, NCC = self.NCH, self.NCC
        NU = G * R
        if lbs is None:
            lbs = range(NCH)
        for lb in lbs:
            sbs_f = [sb for sb in range(NCH) if sb <= lb]
            if lb < NCC:
                sbs_b = list(range(lb, NCC))
            else:
                sbs_b = list(range(NCC)) + list(range(lb, NCH))
            union = sorted(set(sbs_f) | set(sbs_b))
            hh = self.t2k.next()
            for g in range(G):
                rowL = self.t1k.next()
                rl = rowL[:, :2 * R * 128].rearrange("p (j l) -> p j l", j=2 * R)
                for j0 in range(0, 2 * R, 4):
                    bi, bk = self.pbank()
                    n = min(4, 2 * R - j0)
                    for j in range(j0, j0 + n):
                        d_, r_ = j // R, j % R
                        ud = d_ * NU + g * R + r_
                        bc = self.tiny.next()
                        kb.op('dve', lambda bc=bc, ud=ud: nc.vector.tensor_copy(out=bc[:, :128],
                                                                                 in_=self.fcol[:, lb, ud:ud + 1].to_broadcast([128, 128])),
                              r=["fcol"], w=[bc.name])
                        kb.op('pe', lambda bc=bc, j=j: nc.tensor.matmul(self.ps[:, bi, (j - j0) * 128:(j - j0 + 1) * 128], bc[:, :128],
                                                                         self.ident, start=True, stop=True),
                              r=[bc.name, "cm"], w=[bk])
                    self.evac(rl[:, j0:j0 + n, :], self.ps[:, bi, :n * 128].rearrange("p (j l) -> p j l", j=n), r=[bk], w=[rowL.name])
                qt = self.sm_q.next()
                qv = qt[:, :KC * 128].rearrange("p (k t) -> p k t", k=KC)
                r0 = qrow0 + g * KC * 128
                kb.dma(qv, qsrc[r0:r0 + KC * 128, lb * 128:(lb + 1) * 128].rearrange("(k p) t -> p k t", p=128),
                       r=kF(qname, r0, r0 + KC * 128, lb * 128, lb * 128 + 128), w=[qt.name])
                steps = []
                for sb in union:
                    for d_ in (0, 1):
                        if sb in (sbs_f if d_ == 0 else sbs_b):
                            steps.append((sb, d_))
                first, lastu = {}, {}
                for i, (sb, d_) in enumerate(steps):
                    a = d_ if sep else 0
                    if a not in first:
                        first[a] = i
                    lastu[a] = i
                loaded = {}

                def load_sb(sb):
                    kt = self.sm_k.next()
                    kv = kt[:, :KC * 128].rearrange("p (k t) -> p k t", k=KC)
                    k0 = krow0 + g * KC * 128
                    kb.dma(kv, ksrc[k0:k0 + KC * 128, sb * 128:(sb + 1) * 128].rearrange("(k p) t -> p k t", p=128),
                           r=kF(kname, k0, k0 + KC * 128, sb * 128, sb * 128 + 128), w=[kt.name])
                    vt = self.sm_v.next()
                    c0 = g * R * DV
                    kb.dma(vt[:, :R * DV], vsrc[sb * 128:(sb + 1) * 128, c0:c0 + R * DV],
                           r=kT(vname, sb * 128, sb * 128 + 128, c0, c0 + R * DV), w=[vt.name])
                    loaded[sb] = (kt, kv, vt)
                load_sb(union[0])
                si_ = 0
                for ui, sb in enumerate(union):
                    if ui + 1 < len(union):
                        load_sb(union[ui + 1])
                    kt, kv, vt = loaded.pop(sb)
                    bi, bk = self.pbank()
                    sraw = self.ps[:, bi, 0:128]
                    for k in range(KC):
                        kb.op('pe', lambda k=k: nc.tensor.matmul(sraw, kv[:, k, :], qv[:, k, :], start=(k == 0), stop=(k == KC - 1)),
                              r=[kt.name, qt.name], w=[bk])
                    while si_ < len(steps) and steps[si_][0] == sb:
                        _, d_ = steps[si_]
                        wt = self.sm_w.next()
                        wv = wt[:, :R * 128].rearrange("p (r l) -> p r l", r=R)
                        u0 = d_ * NU + g * R
                        cc = self.ccol[:, sb, u0:u0 + R]
                        rls = rl[:, d_ * R:(d_ + 1) * R, :]
                        diag = (sb == lb)
                        if R == 1 and not diag:
                            kb.op('act', lambda: nc.scalar.activation(out=wv[:, 0, :], in_=rls[:, 0, :], func=AF.Exp, bias=cc[:, 0:1], scale=1.0),
                                  r=[rowL.name, "igall"], w=[wt.name])
                        else:
                            kb.op('dve', lambda: nc.vector.tensor_tensor(out=wv, in0=rls, in1=cc.unsqueeze(2).to_broadcast([128, R, 128]),
                                                                         op=ALU.add), r=[rowL.name, "igall"], w=[wt.name])
                            if diag:
                                mk = self.mnf if d_ == 0 else self.mnb
                                kb.op('dve', lambda: nc.vector.tensor_tensor(out=wv, in0=wv, in1=mk.unsqueeze(1).to_broadcast([128, R, 128]),
                                                                             op=ALU.add), r=[wt.name, "cm"], w=[wt.name])
                            kb.op('act', lambda: nc.scalar.activation(out=wv, in_=wv, func=AF.Exp), r=[wt.name], w=[wt.name])
                        kb.op('dve', lambda: nc.vector.tensor_tensor(out=wv, in0=wv, in1=sraw.unsqueeze(1).to_broadcast([128, R, 128]),
                                                                     op=ALU.mult), r=[wt.name, bk], w=[wt.name])
                        a = d_ if sep else 0
                        for r_ in range(R):
                            if sep:
                                acc = self.ps[:, d_, 0:DV]
                            else:
                                acc = self.ps[:, r_, 0:DV]
                            kb.op('pe', lambda r_=r_, acc=acc: nc.tensor.matmul(acc, wv[:, r_, :], vt[:, r_ * DV:(r_ + 1) * DV],
                                                                                 start=(first[a] == si_), stop=(lastu[a] == si_)),
                                  r=[wt.name, vt.name], w=["ps%d" % (d_ if sep else r_)])
                            if sep:
                                kb.op('pe', lambda: nc.tensor.matmul(self.ps[:, 2 + d_, 0:1], wv[:, r_, :], self.ones[:, 0:1],
                                                                      start=(first[a] == si_), stop=(lastu[a] == si_)),
                                      r=[wt.name, "cm"], w=["ps%d" % (2 + d_)])
                        si_ += 1
                c0 = g * R * DV
                if sep:
                    tn = self.tiny.next()
                    for d_ in (0, 1):
                        kb.op('act', lambda d_=d_: nc.scalar.activation(out=tn[:, d_:d_ + 1], in_=self.ps[:, 2 + d_, 0:1], func=AF.Abs),
                              r=["ps%d" % (2 + d_)], w=[tn.name])
                    kb.op('dve', lambda: nc.vector.tensor_scalar(out=tn[:, 0:2], in0=tn[:, 0:2], scalar1=1.0, scalar2=None, op0=ALU.max),
                          r=[tn.name], w=[tn.name])
                    kb.op('dve', lambda: nc.vector.reciprocal(out=tn[:, 2:4], in_=tn[:, 0:2]), r=[tn.name], w=[tn.name])
                    kb.op('dve', lambda: nc.vector.tensor_scalar(out=hh[:, c0:c0 + DV], in0=self.ps[:, 0, 0:DV], scalar1=tn[:, 2:3],
                                                                 scalar2=None, op0=ALU.mult), r=["ps0", tn.name], w=[hh.name])
                    kb.op('dve', lambda: nc.vector.scalar_tensor_tensor(out=hh[:, c0:c0 + DV], in0=self.ps[:, 1, 0:DV], scalar=tn[:, 3:4],
                                                                         in1=hh[:, c0:c0 + DV], op0=ALU.mult, op1=ALU.add),
                          r=["ps1", tn.name, hh.name], w=[hh.name])
                else:
                    self.evac(hh[:, c0:c0 + R * DV].rearrange("p (r d) -> p r d", r=R), self.ps[:, 0:R, 0:DV],
                              r=["ps%d" % r_ for r_ in range(R)], w=[hh.name])
            post(lb, hh)

    def head_norm_tok(self, x, nh, dh, eps, center=True, sq=None):
        nc, kb = self.nc, self.kb
        xv = x[:, :nh * dh].rearrange("p (h d) -> p h d", h=nh)
        tn = self.tiny.next()
        if sq is None:
            sq = self.t2k.next() if nh * dh > 1024 else self.t1k.next()
        sqv = sq[:, :nh * dh].rearrange("p (h d) -> p h d", h=nh)
        if center:
            kb.op('dve', lambda: nc.vector.reduce_sum(out=tn[:, 0:nh], in_=xv, axis=AX.X), r=[x.name], w=[tn.name])
            kb.op('dve', lambda: nc.vector.tensor_scalar(out=tn[:, 0:nh], in0=tn[:, 0:nh], scalar1=1.0 / dh, scalar2=None, op0=ALU.mult),
                  r=[tn.name], w=[tn.name])
            kb.op('dve', lambda: nc.vector.tensor_tensor(out=xv, in0=xv, in1=tn[:, 0:nh].unsqueeze(2).to_broadcast([128, nh, dh]),
                                                         op=ALU.subtract), r=[x.name, tn.name], w=[x.name])
        kb.op('dve', lambda: nc.vector.tensor_tensor(out=sqv, in0=xv, in1=xv, op=ALU.mult), r=[x.name], w=[sq.name])
        kb.op('dve', lambda: nc.vector.reduce_sum(out=tn[:, 16:16 + nh], in_=sqv, axis=AX.X), r=[sq.name], w=[tn.name])
        kb.op('dve', lambda: nc.vector.tensor_scalar(out=tn[:, 32:32 + nh], in0=tn[:, 16:16 + nh], scalar1=1.0 / dh, scalar2=float(eps),
                                                     op0=ALU.mult, op1=ALU.add), r=[tn.name], w=[tn.name])
        kb.op('act', lambda: nc.scalar.activation(out=tn[:, 32:32 + nh], in_=tn[:, 32:32 + nh], func=AF.Sqrt), r=[tn.name], w=[tn.name])
        kb.op('dve', lambda: nc.vector.reciprocal(out=tn[:, 48:48 + nh], in_=tn[:, 32:32 + nh]), r=[tn.name], w=[tn.name])
        kb.op('dve', lambda: nc.vector.tensor_tensor(out=xv, in0=xv, in1=tn[:, 48:48 + nh].unsqueeze(2).to_broadcast([128, nh, dh]),
                                                     op=ALU.mult), r=[x.name, tn.name], w=[x.name])

    def mlstm(self, li, b, last):
        nc, kb = self.nc, self.kb
        w = self.W[li]
        par = self.par
        NCH, NCC = self.NCH, self.NCC
        XM, ZT, XC, QT, KT = self.A[0], self.A[1], self.A[2], self.A[3], self.A[4]
        OP, V, = self.Bt[0], self.Bt[1]
        pm = (li, 1, 0, b)
        self.proj(self.HT, "HT", D, w['w_up'], [
            (0, 2048, 'F', self.epiF(XM, "A0", 0)),
            (2048, 4096, 'F', self.epiF(ZT, "A1", 16)),
            (4096, 6144, 'T', self.epiT(OP, "B0", 4096))], premod=pm)
        self.conv(XM, "A0", 16, lambda c, j: par[:, j * 16 + c:j * 16 + c + 1], lambda c: par[:, 80 + c:81 + c], XC, "A2")
        self.proj(XC, "A2", 2048, w['w_qk'], [
            (0, 2048, 'F', self.epiF(QT, "A3", 0)),
            (2048, 4096, 'F', self.epiF(KT, "A4", 16))])
        self.proj(XM, "A0", 2048, w['w_v'], [(0, 2048, 'T', self.epiT(V, "B1", 0))])
        wg = par[:, 256:1024].rearrange("p (c j) -> p c j", c=48)
        for tb in range(NCH):
            t0 = tb * 128
            qt = self.t2k.next()
            kt = self.t2k.next()
            vt = self.t2k.next()
            qv = qt[:, :2048].rearrange("p (k t) -> p k t", k=16)
            kv = kt[:, :2048].rearrange("p (k t) -> p k t", k=16)
            kb.dma(qv, QT[:, t0:t0 + 128].rearrange("(k p) t -> p k t", p=128), r=kF("A3", 0, 2048, t0, t0 + 128), w=[qt.name])
            kb.dma(kv, KT[:, t0:t0 + 128].rearrange("(k p) t -> p k t", p=128), r=kF("A4", 0, 2048, t0, t0 + 128), w=[kt.name])
            kb.dma(vt[:, :2048], V[t0:t0 + 128, :], r=kT("B1", t0, t0 + 128, 0, 2048), w=[vt.name])
            vT = self.t2k.next()
            vv = self.transpose_to(vt, 16, vT)
            gi, gk = self.pbank()
            gp = self.ps[:, gi, 0:16]
            for c in range(48):
                src, sk = (qv, qt.name) if c < 16 else ((kv, kt.name) if c < 32 else (vv, vT.name))
                kb.op('pe', lambda c=c, src=src: nc.tensor.matmul(gp, src[:, c % 16, :], wg[:, c, :], start=(c == 0), stop=(c == 47)),
                      r=[sk, "par"], w=[gk])
            gt = self.tiny.next()
            kb.op('dve', lambda: nc.vector.tensor_tensor(out=gt[:, 0:16], in0=gp, in1=par[:, 128:144], op=ALU.add),
                  r=[gk, "par"], w=[gt.name])
            g4 = gt[:, 0:16].rearrange("p (a x h) -> p a x h", a=2, x=2)
            kb.op('act', lambda: nc.scalar.activation(out=gt[:, 16:24].rearrange("p (a h) -> p a h", a=2), in_=g4[:, :, 1, :],
                                                      func=AF.Exp, scale=-1.0), r=[gt.name], w=[gt.name])
            kb.op('act', lambda: nc.scalar.activation(out=gt[:, 16:24], in_=gt[:, 16:24], func=AF.Ln, bias=1.0, scale=1.0),
                  r=[gt.name], w=[gt.name])
            kb.op('dve', lambda: nc.vector.tensor_scalar(out=self.ldall[:, tb, 0:8], in0=gt[:, 16:24], scalar1=-1.0, scalar2=None,
                                                         op0=ALU.mult), r=[gt.name], w=["ldall"])
            kb.op('dve', lambda: nc.vector.tensor_copy(out=self.igall[:, tb, 0:8].rearrange("p (a h) -> p a h", a=2), in_=g4[:, :, 0, :]),
                  r=[gt.name], w=["igall"])
        self.decay_cols(4, True, 512.0 ** -0.5)
        YIN = self.A[0]

        def post(lb, hh):
            t0 = lb * 128
            op = self.t2k.next()
            kb.dma(op[:, :2048], OP[t0:t0 + 128, :], r=kT("B0", t0, t0 + 128, 0, 2048), w=[op.name])
            kb.op('act', lambda: nc.scalar.activation(out=op[:, :2048], in_=op[:, :2048], func=AF.Sigmoid), r=[op.name], w=[op.name])
            kb.op('dve', lambda: nc.vector.tensor_tensor(out=hh[:, :2048], in0=hh[:, :2048], in1=op[:, :2048], op=ALU.mult),
                  r=[hh.name, op.name], w=[hh.name])
            self.head_norm_tok(hh, 4, 512, LN_EPS, sq=op)
            hT = self.t2k.next()
            hv = self.transpose_to(hh, 16, hT)
            xc = self.t2k.next()
            zt = self.t2k.next()
            xv = xc[:, :2048].rearrange("p (k t) -> p k t", k=16)
            zv = zt[:, :2048].rearrange("p (k t) -> p k t", k=16)
            kb.dma(xv, XC[:, t0:t0 + 128].rearrange("(k p) t -> p k t", p=128), r=kF("A2", 0, 2048, t0, t0 + 128), w=[xc.name])
            kb.dma(zv, ZT[:, t0:t0 + 128].rearrange("(k p) t -> p k t", p=128), r=kF("A1", 0, 2048, t0, t0 + 128), w=[zt.name])
            ng = par[:, 96:112].unsqueeze(2).to_broadcast([128, 16, 128])
            sk = par[:, 112:128].unsqueeze(2).to_broadcast([128, 16, 128])
            kb.op('dve', lambda: nc.vector.tensor_tensor(out=hv, in0=hv, in1=ng, op=ALU.mult), r=[hT.name, "par"], w=[hT.name])
            kb.op('dve', lambda: nc.vector.tensor_tensor(out=xv, in0=xv, in1=sk, op=ALU.mult), r=[xc.name, "par"], w=[xc.name])
            kb.op('dve', lambda: nc.vector.tensor_tensor(out=hv, in0=hv, in1=xv, op=ALU.add), r=[hT.name, xc.name], w=[hT.name])
            kb.op('act', lambda: nc.scalar.activation(out=zv, in_=zv, func=AF.Silu), r=[zt.name], w=[zt.name])
            kb.op('dve', lambda: nc.vector.tensor_tensor(out=hv, in0=hv, in1=zv, op=ALU.mult), r=[hT.name, zt.name], w=[hT.name])
            kb.dma(YIN[:, t0:t0 + 128].rearrange("(k p) t -> p k t", p=128), hv, r=[hT.name], w=kF("A0", 0, 2048, t0, t0 + 128))

        lbs = range(NCC, NCH) if last else None
        self.quad(4, 1, 4, 512, True, QT, "A3", 0, KT, "A4", 0, V, "B1", post, lbs=lbs)
        self.proj(YIN, "A0", 2048, w['w_down'], [(0, D, 'F', self.epiF(self.YT, "YT", 0))], include_ctx=not last)

    def retention(self, li, b, last):
        nc, kb = self.nc, self.kb
        w = self.W[li]
        par = self.par
        NCH, NCC = self.NCH, self.NCC
        QT, KT, GT, YIN = self.A[0], self.A[1], self.A[2], self.A[3]
        V = self.Bt[0]
        pm = (li, 1, 0, b)
        self.proj_rope(self.HT, "HT", w['w_in'], 0, 1024, QT, "A0", pm, 256)
        self.proj_rope(self.HT, "HT", w['w_in'], 1024, 1024, KT, "A1", pm, 256)
        self.proj(self.HT, "HT", D, w['w_in'], [
            (2048, 4096, 'T', self.epiT(V, "B0", 2048)),
            (4096, 6144, 'F', self.epiF(GT, "A2", 32))], premod=pm)
        kb.op('dve', lambda: nc.vector.tensor_copy(out=self.ldall[:, :, 0:8],
                                                   in_=par[:, 0:8].unsqueeze(1).to_broadcast([128, NCH, 8])),
              r=["par"], w=["ldall"])
        self.decay_cols(4, False, 256.0 ** -0.5)

        def post(lb, hh):
            t0 = lb * 128
            self.head_norm_tok(hh, 4, 512, LN_EPS)
            hT = self.t2k.next()
            hv = self.transpose_to(hh, 16, hT)
            gt = self.t2k.next()
            gv = gt[:, :2048].rearrange("p (k t) -> p k t", k=16)
            kb.dma(gv, GT[:, t0:t0 + 128].rearrange("(k p) t -> p k t", p=128), r=kF("A2", 0, 2048, t0, t0 + 128), w=[gt.name])
            ng = par[:, 16:32].unsqueeze(2).to_broadcast([128, 16, 128])
            kb.op('dve', lambda: nc.vector.tensor_tensor(out=hv, in0=hv, in1=ng, op=ALU.mult), r=[hT.name, "par"], w=[hT.name])
            kb.op('act', lambda: nc.scalar.activation(out=gv, in_=gv, func=AF.Silu), r=[gt.name], w=[gt.name])
            kb.op('dve', lambda: nc.vector.tensor_tensor(out=hv, in0=hv, in1=gv, op=ALU.mult), r=[hT.name, gt.name], w=[hT.name])
            kb.dma(YIN[:, t0:t0 + 128].rearrange("(k p) t -> p k t", p=128), hv, r=[hT.name], w=kF("A3", 0, 2048, t0, t0 + 128))

        lbs = range(NCC, NCH) if last else None
        self.quad(4, 1, 2, 512, False, QT, "A0", 0, KT, "A1", 0, V, "B0", post, lbs=lbs)
        self.proj(YIN, "A3", 2048, w['w_out'], [(0, D, 'F', self.epiF(self.YT, "YT", 0))], include_ctx=not last)

    def ssd(self, li, b, last):
        nc, kb = self.nc, self.kb
        w = self.W[li]
        par = self.par
        NCH, NCC = self.NCH, self.NCC
        XBC, XC = self.A4k, self.XC4k
        ZTOK, XS = self.Bt[0], self.Bt[1]
        pm = (li, 1, 0, b)

        def epi_dt(pv, bk, c, bw, tok0):
            tb = tok0 // 128
            tn = self.tiny.next()
            kb.op('dve', lambda: nc.vector.tensor_tensor(out=tn[:, 0:64], in0=pv, in1=par[:, 192:256], op=ALU.add), r=[bk, "par"], w=[tn.name])
            kb.op('act', lambda: nc.scalar.activation(out=tn[:, 0:64], in_=tn[:, 0:64], func=AF.Exp), r=[tn.name], w=[tn.name])
            kb.op('act', lambda: nc.scalar.activation(out=tn[:, 0:64], in_=tn[:, 0:64], func=AF.Ln, bias=1.0, scale=1.0), r=[tn.name], w=[tn.name])
            kb.op('act', lambda: nc.scalar.activation(out=self.igall[:, tb, 0:64], in_=tn[:, 0:64], func=AF.Ln), r=[tn.name], w=["igall"])
            kb.op('dve', lambda: nc.vector.tensor_tensor(out=self.ldall[:, tb, 0:64], in0=tn[:, 0:64], in1=par[:, 256:320], op=ALU.mult),
                  r=[tn.name, "par"], w=["ldall"])

        self.proj(self.HT, "HT", D, w['w_in'], [
            (0, 2048, 'T', self.epiT(ZTOK, "B0", 0)),
            (2048, 6144, 'F', self.epiF(XBC, "A4k", 16)),
            (6144, 6208, 'T', epi_dt)], premod=pm)
        self.conv(XBC, "A4k", 32, lambda c, j: par[:, j * 32 + c:j * 32 + c + 1], lambda c: par[:, 160 + c:161 + c], XC, "XC4k")
        for tb in range(NCH):
            t0 = tb * 128
            xt = self.t2k.next()
            xv = xt[:, :2048].rearrange("p (k t) -> p k t", k=16)
            kb.dma(xv, XC[0:2048, t0:t0 + 128].rearrange("(k p) t -> p k t", p=128), r=kF("XC4k", 0, 2048, t0, t0 + 128), w=[xt.name])
            xo = self.t2k.next()
            for c0 in range(0, 16, 4):
                bi, bk = self.pbank()
                for j in range(4):
                    c = c0 + j
                    kb.op('pe', lambda c=c, j=j: nc.tensor.transpose(self.ps[:, bi, j * 128:(j + 1) * 128], xv[:, c, :], self.ident),
                          r=[xt.name, "cm"], w=[bk])
                self.evac(xo[:, c0 * 128:(c0 + 4) * 128], self.ps[:, bi, :], r=[bk], w=[xo.name])
            kb.dma(XS[t0:t0 + 128, :], xo[:, :2048], r=[xo.name], w=kT("B1", t0, t0 + 128, 0, 2048))
        self.decay_cols(32, True, 1.0)
        YIN = self.A[0]

        def post(lb, hh):
            t0 = lb * 128
            xs = self.t2k.next()
            zt = self.t2k.next()
            kb.dma(xs[:, :2048], XS[t0:t0 + 128, :], r=kT("B1", t0, t0 + 128, 0, 2048), w=[xs.name])
            kb.dma(zt[:, :2048], ZTOK[t0:t0 + 128, :], r=kT("B0", t0, t0 + 128, 0, 2048), w=[zt.name])
            x3 = xs[:, :2048].rearrange("p (h d) -> p h d", h=32)
            kb.op('dve', lambda: nc.vector.tensor_tensor(out=x3, in0=x3, in1=par[:, 320:352].unsqueeze(2).to_broadcast([128, 32, 64]),
                                                         op=ALU.mult), r=[xs.name, "par"], w=[xs.name])
            kb.op('dve', lambda: nc.vector.tensor_tensor(out=hh[:, :2048], in0=hh[:, :2048], in1=xs[:, :2048], op=ALU.add),
                  r=[hh.name, xs.name], w=[hh.name])
            kb.op('act', lambda: nc.scalar.activation(out=zt[:, :2048], in_=zt[:, :2048], func=AF.Silu), r=[zt.name], w=[zt.name])
            kb.op('dve', lambda: nc.vector.tensor_tensor(out=hh[:, :2048], in0=hh[:, :2048], in1=zt[:, :2048], op=ALU.mult),
                  r=[hh.name, zt.name], w=[hh.name])
            self.head_norm_tok(hh, 8, 256, RMS_EPS, center=False, sq=xs)
            self.transpose_store(hh, 16, YIN, "A0", 0, t0, scale_cols=par[:, 352:368])

        lbs = range(NCC, NCH) if last else None
        self.quad(8, 4, 1, 64, False, XC, "XC4k", 3072, XC, "XC4k", 2048, XS, "B1", post, lbs=lbs)
        self.proj(YIN, "A0", 2048, w['w_out'], [(0, D, 'F', self.epiF(self.YT, "YT", 0))], include_ctx=not last)

    def diffattn(self, li, b, last):
        nc, kb = self.nc, self.kb
        w = self.W[li]
        par = self.par
        T, NCH, NCC, CTX = self.T, self.NCH, self.NCC, self.CTX
        QT, KT, YIN = self.A[0], self.A[1], self.A[2]
        V, O = self.Bt[0], self.Bt[1]
        pm = (li, 1, 0, b)
        lam_init = 0.8 - 0.6 * math.exp(-0.3 * self.layer_ids[li])
        self.proj_rope(self.HT, "HT", w['w_qkv'], 0, 1024, QT, "A0", pm, 64)
        self.proj_rope(self.HT, "HT", w['w_qkv'], 1024, 1024, KT, "A1", pm, 64)
        self.proj(self.HT, "HT", D, w['w_qkv'], [(2048, 3072, 'T', self.epiT(V, "B0", 2048))], premod=pm)
        sc = 64.0 ** -0.5
        kt, vt = self.bigS[0], self.bigS[1]
        ptring = Ring(self.bigS[2:4])
        lbl = range(NCC, NCH) if last else range(NCH)
        for h in range(8):
            kb.dma(kt[:, :T], KT[h * 128:(h + 1) * 128, :], r=kF("A1", h * 128, h * 128 + 128, 0, T), w=[kt.name])
            vv = vt[:, :NCH * 128].rearrange("p (c e) -> p c e", c=NCH)
            kb.dma(vv, V[:, h * 128:(h + 1) * 128].rearrange("(c p) e -> p c e", p=128), r=kT("B0", 0, T, h * 128, h * 128 + 128), w=[vt.name])
            for lb in lbl:
                nk = CTX if lb < NCC else T
                nkb = nk // 128
                nb5 = (nk + 511) // 512
                qt = self.sm_q.next()
                kb.op('dve', lambda: nc.vector.memset(qt[:, 0:256], 0.0), w=[qt.name])
                kb.dma(qt[0:64, 0:128], QT[h * 128:h * 128 + 64, lb * 128:(lb + 1) * 128],
                       r=kF("A0", h * 128, h * 128 + 128, lb * 128, lb * 128 + 128) + [qt.name], w=[(qt.name, 0)])
                kb.dma(qt[64:128, 128:256], QT[h * 128 + 64:h * 128 + 128, lb * 128:(lb + 1) * 128],
                       r=kF("A0", h * 128, h * 128 + 128, lb * 128, lb * 128 + 128) + [qt.name], w=[(qt.name, 1)])
                ot = self.sm_k.next()
                for m in (0, 1):
                    for kbk in range(nb5):
                        n = min(512, nk - kbk * 512)
                        kb.op('pe', lambda kbk=kbk, n=n: nc.tensor.matmul(self.ps[:, kbk, 0:n], qt[:, m * 128:(m + 1) * 128],
                                                                            kt[:, kbk * 512:kbk * 512 + n], start=True, stop=True),
                              r=[qt.name, (qt.name, 0), (qt.name, 1), kt.name], w=["ps%d" % kbk])
                    tn = self.tiny.next()
                    for kbk in range(nb5):
                        n = min(512, nk - kbk * 512)
                        kb.op('dve', lambda kbk=kbk, n=n: nc.vector.reduce_max(out=tn[:, kbk:kbk + 1], in_=self.ps[:, kbk, 0:n], axis=AX.X),
                              r=["ps%d" % kbk], w=[tn.name])
                    kb.op('dve', lambda: nc.vector.reduce_max(out=tn[:, 8:9], in_=tn[:, 0:nb5], axis=AX.X), r=[tn.name], w=[tn.name])
                    kb.op('dve', lambda: nc.vector.tensor_scalar(out=tn[:, 9:10], in0=tn[:, 8:9], scalar1=-sc, scalar2=None, op0=ALU.mult),
                          r=[tn.name], w=[tn.name])
                    pt = ptring.next()
                    for kbk in range(nb5):
                        n = min(512, nk - kbk * 512)
                        kb.op('act', lambda kbk=kbk, n=n: nc.scalar.activation(out=pt[:, kbk * 512:kbk * 512 + n], in_=self.ps[:, kbk, 0:n],
                                                                                func=AF.Exp, bias=tn[:, 9:10], scale=sc),
                              r=["ps%d" % kbk, tn.name], w=[pt.name])
                    kb.op('dve', lambda: nc.vector.reduce_sum(out=tn[:, 10:11], in_=pt[:, 0:nk], axis=AX.X), r=[pt.name], w=[tn.name])
                    kb.op('dve', lambda: nc.vector.reciprocal(out=tn[:, 11:12], in_=tn[:, 10:11]), r=[tn.name], w=[tn.name])
                    if m == 1:
                        kb.op('dve', lambda: nc.vector.tensor_tensor(out=tn[:, 11:12], in0=tn[:, 11:12], in1=par[:, 400:401], op=ALU.mult),
                              r=[tn.name, "par"], w=[tn.name])
                    for s0 in range(0, nkb, 4):
                        n4 = min(4, nkb - s0)
                        tb_ = 5 + ((s0 // 4) % 2)
                        for j in range(n4):
                            kb.op('pe', lambda j=j: nc.tensor.transpose(self.ps[:, tb_, j * 128:(j + 1) * 128],
                                                                         pt[:, (s0 + j) * 128:(s0 + j + 1) * 128], self.ident),
                                  r=[pt.name, "cm"], w=["ps%d" % tb_])
                        ptt = self.sm_w.next()
                        self.evac(ptt[:, :n4 * 128], self.ps[:, tb_, :n4 * 128], r=["ps%d" % tb_], w=[ptt.name])
                        for j in range(n4):
                            sbk = s0 + j
                            kb.op('pe', lambda j=j, sbk=sbk: nc.tensor.matmul(self.ps[:, 7, 0:128], ptt[:, j * 128:(j + 1) * 128], vv[:, sbk, :],
                                                                               start=(sbk == 0), stop=(sbk == nkb - 1)),
                                  r=[ptt.name, vt.name], w=["ps7"])
                    if m == 0:
                        kb.op('dve', lambda: nc.vector.tensor_scalar(out=ot[:, 0:128], in0=self.ps[:, 7, 0:128], scalar1=tn[:, 11:12], scalar2=None,
                                                                     op0=ALU.mult), r=["ps7", tn.name], w=[ot.name])
                    else:
                        kb.op('dve', lambda: nc.vector.scalar_tensor_tensor(out=ot[:, 0:128], in0=self.ps[:, 7, 0:128], scalar=tn[:, 11:12],
                                                                             in1=ot[:, 0:128], op0=ALU.mult, op1=ALU.add),
                              r=["ps7", tn.name, ot.name], w=[ot.name])
                kb.dma(O[lb * 128:(lb + 1) * 128, h * 128:(h + 1) * 128], ot[:, 0:128], r=[ot.name],
                       w=kT("B1", lb * 128, lb * 128 + 128, h * 128, h * 128 + 128))
        for tb in lbl:
            t0 = tb * 128
            o = self.t1k.next()
            kb.dma(o[:, :D], O[t0:t0 + 128, 0:D], r=kT("B1", t0, t0 + 128, 0, D), w=[o.name])
            self.head_norm_tok(o, 8, 128, RMS_EPS, center=False)
            ov = o[:, :D].rearrange("p (h d) -> p h d", h=8)
            kb.op('dve', lambda: nc.vector.tensor_tensor(out=ov, in0=ov, in1=par[:, 256:384].unsqueeze(1).to_broadcast([128, 8, 128]),
                                                         op=ALU.mult), r=[o.name, "par"], w=[o.name])
            kb.op('dve', lambda: nc.vector.tensor_scalar(out=o[:, :D], in0=o[:, :D], scalar1=1.0 - lam_init, scalar2=None, op0=ALU.mult),
                  r=[o.name], w=[o.name])
            self.transpose_store(o, 8, YIN, "A2", 0, t0)
        self.proj(YIN, "A2", D, w['w_out'], [(0, D, 'F', self.epiF(self.YT, "YT", 0))], include_ctx=not last)

    def ln_feat(self, zt, zv, n, li, which):
        nc, kb = self.nc, self.kb
        bi, bk = self.pbank()
        mv = self.ps[:, bi, 0:n]
        for c in range(8):
            kb.op('pe', lambda c=c: nc.tensor.matmul(mv, self.onesm[:], zv[:, c, :], start=(c == 0), stop=(c == 7)),
                  r=["onesm", zt.name], w=[bk])
        kb.op('dve', lambda: nc.vector.tensor_tensor(out=zv, in0=zv, in1=mv.unsqueeze(1).to_broadcast([128, 8, n]), op=ALU.subtract),
              r=[zt.name, bk], w=[zt.name])
        sq = self.t1k.next()
        sv = sq[:, :8 * n].rearrange("p (c t) -> p c t", c=8)
        kb.op('dve', lambda: nc.vector.tensor_tensor(out=sv, in0=zv, in1=zv, op=ALU.mult), r=[zt.name], w=[sq.name])
        bi2, bk2 = self.pbank()
        vv = self.ps[:, bi2, 0:n]
        for c in range(8):
            kb.op('pe', lambda c=c: nc.tensor.matmul(vv, self.onesm[:], sv[:, c, :], start=(c == 0), stop=(c == 7)),
                  r=["onesm", sq.name], w=[bk2])
        rs = self.tiny.next()
        kb.op('dve', lambda: nc.vector.tensor_scalar(out=rs[:, :n], in0=vv, scalar1=LN_EPS, scalar2=None, op0=ALU.add), r=[bk2], w=[rs.name])
        kb.op('act', lambda: nc.scalar.activation(out=rs[:, :n], in_=rs[:, :n], func=AF.Sqrt), r=[rs.name], w=[rs.name])
        kb.op('dve', lambda: nc.vector.reciprocal(out=rs[:, :n], in_=rs[:, :n]), r=[rs.name], w=[rs.name])
        kb.op('dve', lambda: nc.vector.tensor_tensor(out=zv, in0=zv, in1=rs[:, :n].unsqueeze(1).to_broadcast([128, 8, n]), op=ALU.mult),
              r=[zt.name, rs.name], w=[zt.name])
        g = self.lng[:, li * 2 + which, :].unsqueeze(2).to_broadcast([128, 8, n])
        bb = self.lnb[:, li * 2 + which, :].unsqueeze(2).to_broadcast([128, 8, n])
        kb.op('dve', lambda: nc.vector.tensor_tensor(out=zv, in0=zv, in1=g, op=ALU.mult), r=[zt.name, ("lng", li)], w=[zt.name])
        kb.op('dve', lambda: nc.vector.tensor_tensor(out=zv, in0=zv, in1=bb, op=ALU.add), r=[zt.name, ("lnb", li)], w=[zt.name])

    def post(self, li, b, last):
        nc, kb = self.nc, self.kb
        w = self.W[li]
        NB, NCH, NCC = self.NB, self.NCH, self.NCC
        tbs = list(range(NCC, NCH)) if last else list(range(NCH))
        for tb in tbs:
            t0 = tb * 128
            col = NB if tb < NCC else b
            ht = self.t1k.next()
            yt = self.t1k.next()
            hv = ht[:, :1024].rearrange("p (c t) -> p c t", c=8)
            yv = yt[:, :1024].rearrange("p (c t) -> p c t", c=8)
            kb.dma(hv, self.HT[:, t0:t0 + 128].rearrange("(c p) t -> p c t", p=128), r=kF("HT", 0, D, t0, t0 + 128), w=[ht.name])
            kb.dma(yv, self.YT[:, t0:t0 + 128].rearrange("(c p) t -> p c t", p=128), r=kF("YT", 0, D, t0, t0 + 128), w=[yt.name])
            gm = self.modcol(li, 2, col).to_broadcast([128, 8, 128])
            kb.op('dve', lambda: nc.vector.tensor_tensor(out=yv, in0=yv, in1=gm, op=ALU.mult), r=[yt.name] + self.modkeys(li, 2), w=[yt.name])
            kb.op('dve', lambda: nc.vector.scalar_tensor_tensor(out=ht[:, :1024], in0=ht[:, :1024], scalar=ALPHA, in1=yt[:, :1024],
                                                                 op0=ALU.mult, op1=ALU.add), r=[ht.name, yt.name], w=[ht.name])
            self.ln_feat(ht, hv, 128, li, 0)
            kb.dma(self.H1T[:, t0:t0 + 128].rearrange("(c p) t -> p c t", p=128), hv, r=[ht.name], w=kF("H1T", 0, D, t0, t0 + 128))
        grp = self.groups(512, include_ctx=not last)
        self.proj(self.H1T, "H1T", D, w['wq'], [(0, 2048, 'F', self.epiF(self.QPT, "QPT", 0))], premod=(li, 4, 3, b), groups=grp)
        for tb in tbs:
            self.peer_tile(li, b, tb, last)

    def peer_tile(self, li, b, tb, last):
        nc, kb = self.nc, self.kb
        w = self.W[li]
        NB, CTX, NCC = self.NB, self.CTX, self.NCC
        t0 = tb * 128
        col = NB if tb < NCC else b
        h1 = self.t1k.next()
        h1v = h1[:, :1024].rearrange("p (c t) -> p c t", c=8)
        kb.dma(h1v, self.H1T[:, t0:t0 + 128].rearrange("(c p) t -> p c t", p=128), r=kF("H1T", 0, D, t0, t0 + 128), w=[h1.name])
        pT = self.t1k.next()
        pTv = pT[:, :1024].rearrange("p (c t) -> p c t", c=8)
        kb.op('dve', lambda: nc.vector.tensor_tensor(out=pTv, in0=h1v, in1=self.modcol(li, 4, col).to_broadcast([128, 8, 128]), op=ALU.mult),
              r=[h1.name] + self.modkeys(li, 4), w=[pT.name])
        kb.op('dve', lambda: nc.vector.tensor_tensor(out=pTv, in0=pTv, in1=self.modcol(li, 3, col).to_broadcast([128, 8, 128]), op=ALU.add),
              r=[pT.name] + self.modkeys(li, 3), w=[pT.name])
        ptok = self.t1k.next()
        for c0 in (0, 4):
            bi, bk = self.pbank()
            for j in range(4):
                kb.op('pe', lambda j=j: nc.tensor.transpose(self.ps[:, bi, j * 128:(j + 1) * 128], pTv[:, c0 + j, :], self.ident),
                      r=[pT.name, "cm"], w=[bk])
            self.evac(ptok[:, c0 * 128:(c0 + 4) * 128], self.ps[:, bi, :], r=[bk], w=[ptok.name])
        qt = self.t2k.next()
        qv = qt[:, :2048].rearrange("p (j t) -> p j t", j=16)
        kb.dma(qv, self.QPT[:, t0:t0 + 128].rearrange("(j p) t -> p j t", p=128), r=kF("QPT", 0, 2048, t0, t0 + 128), w=[qt.name])
        sc = self.t2k.next()
        scv = sc[:, :2048].rearrange("p (j n) -> p j n", j=16)
        for j0 in range(0, 16, 4):
            bi, bk = self.pbank()
            for j in range(4):
                kb.op('pe', lambda j=j: nc.tensor.matmul(self.ps[:, bi, j * 128:(j + 1) * 128], qv[:, j0 + j, :], self.keysT[:, j0 + j, :],
                                                          start=True, stop=True), r=[qt.name, "keysT"], w=[bk])
            self.evac(scv[:, j0:j0 + 4, :], self.ps[:, bi, :].rearrange("p (j n) -> p j n", j=4), r=[bk], w=[sc.name])
        sc2 = self.t2k.next()
        sc2v = sc2[:, :2048].rearrange("p (j n) -> p j n", j=16)
        tp, ti, bs, bj, ef, ei, gt, act = self.smt
        tiu = ti[:, 0:256].bitcast(U32)
        stop_ = tp[:, 0:256].rearrange("p (j k) -> p j k", j=16)
        for j in range(16):
            kb.op('dve', lambda j=j: nc.vector.max(out=stop_[:, j, 0:8], in_=scv[:, j, :]), r=[sc.name], w=[tp.name])
            kb.op('dve', lambda j=j: nc.vector.match_replace(out=sc2v[:, j, :], in_to_replace=stop_[:, j, 0:8], in_values=scv[:, j, :],
                                                              imm_value=NEG), r=[sc.name, tp.name], w=[sc2.name])
            kb.op('dve', lambda j=j: nc.vector.max(out=stop_[:, j, 8:16], in_=sc2v[:, j, :]), r=[sc2.name], w=[tp.name])
            kb.op('dve', lambda j=j: nc.vector.max_index(out=tiu[:, j * 16:j * 16 + 8], in_max=stop_[:, j, 0:8], in_values=scv[:, j, :]),
                  r=[sc.name, tp.name], w=[ti.name])
            kb.op('dve', lambda j=j: nc.vector.max_index(out=tiu[:, j * 16 + 8:j * 16 + 16], in_max=stop_[:, j, 8:16], in_values=sc2v[:, j, :]),
                  r=[sc2.name, tp.name], w=[ti.name])
        kb.op('dve', lambda: nc.vector.tensor_copy(out=tp[:, 256:512], in_=tiu), r=[ti.name], w=[tp.name])
        st4 = tp[:, 0:256].rearrange("p (h c k) -> p h c k", h=8, c=2)
        it4 = tp[:, 256:512].rearrange("p (h c k) -> p h c k", h=8, c=2)
        cand = self.t2k.next()
        cv = cand[:, :2048].rearrange("p (h a b) -> p h a b", h=8, a=16)
        kb.op('dve', lambda: nc.vector.tensor_tensor(out=cv, in0=st4[:, :, 0, :].unsqueeze(3).to_broadcast([128, 8, 16, 16]),
                                                     in1=st4[:, :, 1, :].unsqueeze(2).to_broadcast([128, 8, 16, 16]), op=ALU.add),
              r=[tp.name], w=[cand.name])
        cand2 = self.t2k.next()
        bju = bj[:, 0:128].bitcast(U32)
        bsv = bs[:, 0:128].rearrange("p (h k) -> p h k", h=8)
        c1 = cand[:, :2048].rearrange("p (h n) -> p h n", h=8)
        c2 = cand2[:, :2048].rearrange("p (h n) -> p h n", h=8)
        for h in range(8):
            kb.op('dve', lambda h=h: nc.vector.max(out=bsv[:, h, 0:8], in_=c1[:, h, :]), r=[cand.name], w=[bs.name])
            kb.op('dve', lambda h=h: nc.vector.match_replace(out=c2[:, h, :], in_to_replace=bsv[:, h, 0:8], in_values=c1[:, h, :],
                                                              imm_value=NEG), r=[cand.name, bs.name], w=[cand2.name])
            kb.op('dve', lambda h=h: nc.vector.max(out=bsv[:, h, 8:16], in_=c2[:, h, :]), r=[cand2.name], w=[bs.name])
            kb.op('dve', lambda h=h: nc.vector.max_index(out=bju[:, h * 16:h * 16 + 8], in_max=bsv[:, h, 0:8], in_values=c1[:, h, :]),
                  r=[cand.name, bs.name], w=[bj.name])
            kb.op('dve', lambda h=h: nc.vector.max_index(out=bju[:, h * 16 + 8:h * 16 + 16], in_max=bsv[:, h, 8:16], in_values=c2[:, h, :]),
                  r=[cand2.name, bs.name], w=[bj.name])
        bau = bj[:, 128:256].bitcast(U32)
        bbu = bj[:, 256:384].bitcast(U32)
        kb.op('dve', lambda: nc.vector.tensor_single_scalar(out=bau, in_=bju, scalar=4, op=ALU.logical_shift_right), r=[bj.name], w=[bj.name])
        kb.op('dve', lambda: nc.vector.tensor_single_scalar(out=bbu, in_=bju, scalar=15, op=ALU.bitwise_and), r=[bj.name], w=[bj.name])
        kb.op('dve', lambda: nc.vector.tensor_copy(out=bs[:, 256:384], in_=bau), r=[bj.name], w=[bs.name])
        kb.op('dve', lambda: nc.vector.tensor_copy(out=bs[:, 384:512], in_=bbu), r=[bj.name], w=[bs.name])
        oh = self.t2k.next()
        ohv = oh[:, :2048].rearrange("p (h k a) -> p h k a", h=8, k=16)
        i16b = self.i16[:, :].unsqueeze(1).unsqueeze(1).to_broadcast([128, 8, 16, 16])
        for which in (0, 1):
            ab = bs[:, 256 + which * 128:384 + which * 128].rearrange("p (h k) -> p h k", h=8)
            kb.op('dve', lambda ab=ab: nc.vector.tensor_tensor(out=ohv, in0=ab.unsqueeze(3).to_broadcast([128, 8, 16, 16]), in1=i16b,
                                                                op=ALU.is_equal), r=[bs.name, "i16"], w=[oh.name])
            kb.op('dve', lambda which=which: nc.vector.tensor_tensor(out=ohv, in0=ohv,
                                                                      in1=it4[:, :, which, :].unsqueeze(2).to_broadcast([128, 8, 16, 16]),
                                                                      op=ALU.mult), r=[oh.name, tp.name], w=[oh.name])
            kb.op('dve', lambda which=which: nc.vector.reduce_sum(out=ef[:, which * 128:(which + 1) * 128].rearrange("p (h k) -> p h k", h=8),
                                                                   in_=ohv, axis=AX.X), r=[oh.name], w=[ef.name])
        kb.op('dve', lambda: nc.vector.scalar_tensor_tensor(out=ef[:, 256:384], in0=ef[:, 0:128], scalar=128.0, in1=ef[:, 128:256],
                                                             op0=ALU.mult, op1=ALU.add), r=[ef.name], w=[ef.name])
        eiv = ei[:, 0:128].bitcast(I32)
        kb.op('dve', lambda: nc.vector.tensor_copy(out=eiv, in_=ef[:, 256:384]), r=[ef.name], w=[ei.name])
        gv = gt[:, 0:128].rearrange("p (h k) -> p h k", h=8)
        kb.op('dve', lambda: nc.vector.tensor_tensor(out=gv, in0=bsv, in1=bsv[:, :, 0:1].to_broadcast([128, 8, 16]), op=ALU.subtract),
              r=[bs.name], w=[gt.name])
        kb.op('act', lambda: nc.scalar.activation(out=gt[:, 0:128], in_=gt[:, 0:128], func=AF.Exp), r=[gt.name], w=[gt.name])
        kb.op('dve', lambda: nc.vector.reduce_sum(out=gt[:, 128:136], in_=gv, axis=AX.X), r=[gt.name], w=[gt.name])
        kb.op('dve', lambda: nc.vector.reciprocal(out=gt[:, 136:144], in_=gt[:, 128:136]), r=[gt.name], w=[gt.name])
        kb.op('dve', lambda: nc.vector.tensor_tensor(out=gv, in0=gv, in1=gt[:, 136:144].unsqueeze(2).to_broadcast([128, 8, 16]), op=ALU.mult),
              r=[gt.name], w=[gt.name])
        for g4 in range(32):
            ut = self.ringS.next()
            uv = ut[:, :4096].rearrange("p (k d) -> p k d", k=4)
            for i in range(4):
                k = g4 * 4 + i
                kb.gather(uv[:, i, :], w['u'], eiv[:, k:k + 1], r=[ei.name], w=([ut.name] if i == 0 else []) + [(ut.name, i)])
            kb.op('dve', lambda: nc.vector.tensor_tensor(out=uv, in0=uv, in1=ptok[:, :1024].unsqueeze(1).to_broadcast([128, 4, 1024]),
                                                         op=ALU.mult), r=[(ut.name, i) for i in range(4)] + [ptok.name],
                  w=[ut.name])
            kb.op('dve', lambda: nc.vector.reduce_sum(out=act[:, g4 * 4:g4 * 4 + 4], in_=uv, axis=AX.X), r=[ut.name], w=[act.name])
        a0 = act[:, 0:128]
        a1 = act[:, 128:256]
        kb.op('dve', lambda: nc.vector.tensor_tensor(out=a1, in0=a0, in1=a0, op=ALU.mult), r=[act.name], w=[act.name])
        kb.op('dve', lambda: nc.vector.tensor_scalar(out=a1, in0=a1, scalar1=0.044715, scalar2=1.0, op0=ALU.mult, op1=ALU.add),
              r=[act.name], w=[act.name])
        kb.op('dve', lambda: nc.vector.tensor_tensor(out=a1, in0=a1, in1=a0, op=ALU.mult), r=[act.name], w=[act.name])
        kb.op('act', lambda: nc.scalar.activation(out=a1, in_=a1, func=AF.Sigmoid, scale=2.0 * math.sqrt(2.0 / math.pi)), r=[act.name], w=[act.name])
        kb.op('dve', lambda: nc.vector.tensor_tensor(out=a1, in0=a1, in1=a0, op=ALU.mult), r=[act.name], w=[act.name])
        kb.op('dve', lambda: nc.vector.tensor_tensor(out=act[:, 256:384], in0=a1, in1=gt[:, 0:128], op=ALU.mult), r=[act.name, gt.name], w=[act.name])
        wts = act[:, 256:384]
        ft = self.t1k.next()
        for g4 in range(32):
            vt = self.ringS.next()
            vv = vt[:, :4096].rearrange("p (k d) -> p k d", k=4)
            for i in range(4):
                k = g4 * 4 + i
                kb.gather(vv[:, i, :], w['v'], eiv[:, k:k + 1], r=[ei.name], w=([vt.name] if i == 0 else []) + [(vt.name, i)])
            for i in range(4):
                k = g4 * 4 + i
                if k == 0:
                    kb.op('dve', lambda: nc.vector.tensor_scalar(out=ft[:, :1024], in0=vv[:, 0, :], scalar1=wts[:, 0:1], scalar2=None,
                                                                 op0=ALU.mult), r=[(vt.name, 0), vt.name, act.name], w=[ft.name])
                else:
                    kb.op('dve', lambda i=i, k=k: nc.vector.scalar_tensor_tensor(out=ft[:, :1024], in0=vv[:, i, :], scalar=wts[:, k:k + 1],
                                                                                  in1=ft[:, :1024], op0=ALU.mult, op1=ALU.add),
                          r=[(vt.name, i), vt.name, act.name, ft.name], w=[ft.name])
        fT = self.t1k.next()
        fv = self.transpose_to(ft, 8, fT)
        kb.op('dve', lambda: nc.vector.tensor_tensor(out=fv, in0=fv, in1=self.modcol(li, 5, col).to_broadcast([128, 8, 128]), op=ALU.mult),
              r=[fT.name] + self.modkeys(li, 5), w=[fT.name])
        kb.op('dve', lambda: nc.vector.scalar_tensor_tensor(out=fT[:, :1024], in0=h1[:, :1024], scalar=ALPHA, in1=fT[:, :1024],
                                                             op0=ALU.mult, op1=ALU.add), r=[h1.name, fT.name], w=[fT.name])
        self.ln_feat(fT, fv, 128, li, 1)
        if last:
            if tb >= NCC:
                ot = self.t1k.next()
                for c0 in (0, 4):
                    bi, bk = self.pbank()
                    for j in range(4):
                        kb.op('pe', lambda j=j: nc.tensor.transpose(self.ps[:, bi, j * 128:(j + 1) * 128], fv[:, c0 + j, :], self.ident),
                              r=[fT.name, "cm"], w=[bk])
                    self.evac(ot[:, c0 * 128:(c0 + 4) * 128], self.ps[:, bi, :], r=[bk], w=[ot.name])
                s0 = t0 - CTX
                kb.dma(self.out[b, s0:s0 + 128, :], ot[:, :1024], r=[ot.name], w=[("out", b, tb)])
        else:
            kb.dma(self.HT[:, t0:t0 + 128].rearrange("(c p) t -> p c t", p=128), fv, r=[fT.name], w=kF("HT", 0, D, t0, t0 + 128))


_CACHE = {}


def layer_weight_maps(inputs, kinds, layer_ids):
    m = {}
    cnt = {0: 0, 1: 0, 2: 0, 3: 0}
    for li, (kind, lid) in enumerate(zip(kinds, layer_ids)):
        p = "L%d_" % li
        j = lid // 4
        f = lambda a: np.ascontiguousarray(np.asarray(a, dtype=np.float32))
        m[p + "ada_w"] = f(inputs['ada_w'][lid])
        m[p + "ada_b"] = f(inputs['ada_b'][lid])
        m[p + "ln_g"] = f(inputs['ln_g'][lid])
        m[p + "ln_b"] = f(inputs['ln_b'][lid])
        m[p + "peer_wq"] = f(inputs['peer_wq'][lid])
        m[p + "peer_keys"] = f(inputs['peer_keys'][lid]).reshape(16, 128, 128)
        m[p + "peer_u"] = f(inputs['peer_u'][lid])
        m[p + "peer_v"] = f(inputs['peer_v'][lid])
        if kind == 0:
            m[p + "w_up"] = f(inputs['mlstm_w_up'][j])
            m[p + "conv_w"] = f(inputs['mlstm_conv_w'][j])
            m[p + "conv_b"] = f(inputs['mlstm_conv_b'][j])
            m[p + "w_qk"] = f(inputs['mlstm_w_qk'][j])
            m[p + "w_v"] = f(inputs['mlstm_w_v'][j])
            m[p + "w_gate"] = f(inputs['mlstm_w_gate'][j])
            m[p + "b_gate"] = f(inputs['mlstm_b_gate'][j])
            m[p + "norm_g"] = f(inputs['mlstm_norm_g'][j])
            m[p + "skip"] = f(inputs['mlstm_skip'][j])
            m[p + "w_down"] = f(inputs['mlstm_w_down'][j])
        elif kind == 1:
            m[p + "w_in"] = f(inputs['ssd_w_in'][j])
            m[p + "conv_w"] = f(inputs['ssd_conv_w'][j])
            m[p + "conv_b"] = f(inputs['ssd_conv_b'][j])
            m[p + "dt_bias"] = f(inputs['ssd_dt_bias'][j]).reshape(64)
            m[p + "a_log"] = f(inputs['ssd_a_log'][j]).reshape(64)
            m[p + "d"] = f(inputs['ssd_d'][j]).reshape(32)
            m[p + "norm_g"] = f(inputs['ssd_norm_g'][j])
            m[p + "w_out"] = f(inputs['ssd_w_out'][j])
        elif kind == 2:
            m[p + "w_qkv"] = f(inputs['diff_w_qkv'][j])
            m[p + "lam"] = f(inputs['diff_lambda'][j])
            m[p + "norm_g"] = f(inputs['diff_norm_g'][j])
            m[p + "w_out"] = f(inputs['diff_w_out'][j])
        else:
            m[p + "w_in"] = f(inputs['ret_w_in'][j])
            m[p + "decay"] = f(inputs['ret_decay_logit'][j]).reshape(8)
            m[p + "norm_g"] = f(inputs['ret_norm_g'][j])
            m[p + "w_out"] = f(inputs['ret_w_out'][j])
    return m


def run(inputs, NB, n_cores, kinds, layer_ids, final_last=True, trace=False):
    x = np.asarray(inputs['x'], np.float32)
    cx = np.asarray(inputs['ctx'], np.float32)
    c = np.asarray(inputs['c'], np.float32)
    c_ctx = np.asarray(inputs['c_ctx'], np.float32)
    B, SEQ, _ = x.shape
    CTX = cx.shape[1]
    assert B == NB * n_cores
    key = (NB, CTX, SEQ, tuple(kinds), tuple(layer_ids), final_last)
    if key not in _CACHE:
        _CACHE[key] = Prog(NB, CTX, SEQ, kinds, layer_ids, final_last)
    prog = _CACHE[key]
    consts = make_consts(SEQ)
    wm = layer_weight_maps(inputs, kinds, layer_ids)
    in_maps = []
    for ci in range(n_cores):
        sl = slice(ci * NB, (ci + 1) * NB)
        m = dict(wm)
        m.update(consts)
        m['x'] = np.ascontiguousarray(x[sl])
        m['ctx'] = np.ascontiguousarray(cx[sl])
        m['cT'] = np.ascontiguousarray(np.concatenate([c[sl].T, c_ctx[:, None]], axis=1))
        in_maps.append(m)
    res = run_bass_kernel_spmd(prog.nc, in_maps, core_ids=list(range(n_cores)), trace=trace)
    out = np.concatenate([np.asarray(r["out"]) for r in res.results], axis=0)
    return out.astype(np.float32), res


N_LAUNCH = 4


def kernel(**inputs):
    if N_LAUNCH == 1:
        out, _ = run(inputs, 4, 8, [0, 1, 2, 3], [0, 1, 2, 3], True)
        return out
    x = np.asarray(inputs['x'])
    outs = []
    for i in range(4):
        sub = dict(inputs)
        sl = slice(i * 8, (i + 1) * 8)
        sub['x'] = x[sl]
        sub['ctx'] = np.asarray(inputs['ctx'])[sl]
        sub['c'] = np.asarray(inputs['c'])[sl]
        o, _ = run(sub, 1, 8, [0, 1, 2, 3], [0, 1, 2, 3], True)
        outs.append(o)
    return np.concatenate(outs, axis=0)
```

```python
import math
from contextlib import ExitStack
import numpy as np
import concourse.bass as bass
import concourse.mybir as mybir
from concourse.bass_utils import run_bass_kernel_spmd

F32 = mybir.dt.float32
I32 = mybir.dt.int32
U32 = mybir.dt.uint32
AF = mybir.ActivationFunctionType
ALU = mybir.AluOpType
AX = mybir.AxisListType

D = 1024
ALPHA = (2.0 * 4) ** 0.25
LN_EPS = 1e-5
RMS_EPS = 1e-6
NEG = -1.0e30
GRID_W = 64
ROPE_BASE = 10000.0


class KB:
    def __init__(self, nc, es, n_slots=72):
        self.nc = nc
        self.es = es
        self.engs = {'pe': nc.tensor, 'act': nc.scalar, 'dve': nc.vector, 'pool': nc.gpsimd, 'sp': nc.sync}
        self.sem = {e: es.enter_context(nc.semaphore("s_" + e)) for e in self.engs}
        self.cnt = {e: 0 for e in self.engs}
        self.dsem = [es.enter_context(nc.semaphore("d%d" % i)) for i in range(n_slots)]
        self.dval = [0] * n_slots
        self.dnext = 0
        self.seen = {e: {} for e in self.engs}
        self.lastw = {}
        self.readers = {}
        self.nins = 0

    def _wait(self, eng, tok):
        sk, v = tok
        if eng == 'pe' and sk == ('e', 'pe'):
            return
        if self.seen[eng].get(sk, 0) >= v:
            return
        sem = self.sem[sk[1]] if sk[0] == 'e' else self.dsem[sk[1]]
        self.engs[eng].wait_ge(sem, v)
        self.seen[eng][sk] = v

    def _deps(self, eng, r, w):
        for k in r:
            t = self.lastw.get(k)
            if t is not None:
                self._wait(eng, t)
        for k in w:
            t = self.lastw.get(k)
            if t is not None:
                self._wait(eng, t)
            rd = self.readers.get(k)
            if rd:
                for sk, v in rd.items():
                    self._wait(eng, (sk, v))

    def _commit(self, tok, r, w):
        sk, v = tok
        for k in r:
            d = self.readers.get(k)
            if d is None:
                d = {}
                self.readers[k] = d
            if d.get(sk, 0) < v:
                d[sk] = v
        for k in w:
            self.lastw[k] = tok
            self.readers[k] = {}

    def op(self, eng, fn, r=(), w=()):
        self._deps(eng, r, w)
        ins = fn()
        self.cnt[eng] += 1
        ins.then_inc(self.sem[eng], 1)
        self._commit((('e', eng), self.cnt[eng]), r, w)
        self.nins += 1

    def dma(self, out, in_, r=(), w=(), q='sp', **kw):
        s = self.dnext
        self.dnext = (s + 1) % len(self.dsem)
        if self.dval[s] > 0:
            self._wait(q, (('d', s), self.dval[s]))
        self._deps(q, r, w)
        ins = self.engs[q].dma_start(out=out, in_=in_, **kw)
        self.dval[s] += 16
        ins.then_inc(self.dsem[s], 16)
        self._commit((('d', s), self.dval[s]), r, w)
        self.nins += 1

    def gather(self, out, table, idx_ap, r=(), w=()):
        q = 'pool'
        s = self.dnext
        self.dnext = (s + 1) % len(self.dsem)
        if self.dval[s] > 0:
            self._wait(q, (('d', s), self.dval[s]))
        self._deps(q, r, w)
        ins = self.nc.gpsimd.indirect_dma_start(
            out=out, out_offset=None, in_=table,
            in_offset=bass.IndirectOffsetOnAxis(ap=idx_ap, axis=0))
        self.dval[s] += 16
        ins.then_inc(self.dsem[s], 16)
        self._commit((('d', s), self.dval[s]), r, w)
        self.nins += 1

    def drain(self):
        for s in range(len(self.dsem)):
            if self.dval[s] > 0:
                self._wait('sp', (('d', s), self.dval[s]))
        for e in ('pe', 'act', 'dve', 'pool'):
            if self.cnt[e] > 0:
                self._wait('sp', (('e', e), self.cnt[e]))


class Ring:
    def __init__(self, tiles):
        self.tiles = tiles
        self.i = 0

    def next(self):
        t = self.tiles[self.i]
        self.i = (self.i + 1) % len(self.tiles)
        return t


def kF(name, r0, r1, t0, t1):
    return [(name, rc, tb) for rc in range(r0 // 128, (r1 + 127) // 128) for tb in range(t0 // 128, (t1 + 127) // 128)]


def kT(name, t0, t1, c0, c1):
    return [(name, tb, cb) for tb in range(t0 // 128, (t1 + 127) // 128) for cb in range(c0 // 128, (c1 + 127) // 128)]


def rope_tables(d, seq):
    h = d // 2
    q = h // 2
    t = np.arange(seq)
    row = (t // GRID_W).astype(np.float32)
    col = (t % GRID_W).astype(np.float32)
    freqs = (np.float32(ROPE_BASE) ** (-np.arange(q, dtype=np.float32) / np.float32(q))).astype(np.float32)
    cos = np.zeros((d, seq), np.float32)
    sin = np.zeros((d, seq), np.float32)
    for f in range(d):
        pos = row if f < h else col
        i = f % q
        ang = (pos * freqs[i]).astype(np.float32)
        sgn = -1.0 if (f % h) < q else 1.0
        cos[f] = np.cos(ang).astype(np.float32)
        sin[f] = (sgn * np.sin(ang)).astype(np.float32)
    return cos, sin


def make_consts(seq):
    ident = np.eye(128, dtype=np.float32)
    s = np.arange(128)[:, None]
    l = np.arange(128)[None, :]
    trif = (s <= l).astype(np.float32)
    trib = (s >= l).astype(np.float32)
    mnf = np.where(s <= l, 0.0, NEG).astype(np.float32)
    mnb = np.where(s >= l, 0.0, NEG).astype(np.float32)
    ones = np.ones((128, 128), np.float32)
    cm = np.stack([ident, trif, trib, mnf, mnb, ones], 0)
    c256, s256 = rope_tables(256, seq)
    c64, s64 = rope_tables(64, seq)
    rope_ret = np.stack([c256.reshape(2, 128, seq), s256.reshape(2, 128, seq)], 0)
    rope_dif = np.stack([np.concatenate([c64, c64], 0), np.concatenate([s64, s64], 0)], 0)
    iota16 = np.tile(np.arange(16, dtype=np.float32)[None, :], (128, 1))
    return dict(cm=cm, rope_ret=np.ascontiguousarray(rope_ret), rope_dif=np.ascontiguousarray(rope_dif), iota16=iota16)


class Prog:
    def __init__(self, NB, CTX, SEQ, kinds, layer_ids=None, final_last=True):
        self.NB, self.CTX, self.SEQ = NB, CTX, SEQ
        self.kinds = list(kinds)
        self.L = len(kinds)
        self.layer_ids = list(layer_ids) if layer_ids is not None else list(range(self.L))
        self.final_last = final_last
        self.T = CTX + SEQ
        self.NCH = self.T // 128
        self.NCC = CTX // 128
        self.nc = bass.Bass("TRN2", target_bir_lowering=False)
        self.es = ExitStack()
        self.inputs = {}
        self.build()

    def din(self, name, shape, dtype=F32):
        ap = self.nc.dram_tensor(name, list(shape), dtype, kind="ExternalInput").ap()
        self.inputs[name] = ap
        return ap

    def dscr(self, name, shape, dtype=F32):
        return self.nc.dram_tensor(name, list(shape), dtype, kind="Internal").ap()

    def sb(self, name, shape, dtype=F32):
        return self.es.enter_context(self.nc.sbuf_tensor("sb_" + name, list(shape), dtype))

    def groups(self, tgmax, include_ctx=True):
        gs = []
        if include_ctx:
            t = 0
            while t < self.CTX:
                g = min(tgmax, self.CTX - t)
                gs.append((t, g))
                t += g
        t = self.CTX
        while t < self.T:
            g = min(tgmax, self.T - t)
            gs.append((t, g))
            t += g
        return gs

    def build(self):
        nc, es = self.nc, self.es
        NB, CTX, SEQ, T, NCH = self.NB, self.CTX, self.SEQ, self.T, self.NCH
        kb = KB(nc, es)
        self.kb = kb
        self.x = self.din("x", [NB, SEQ, D])
        self.cx = self.din("ctx", [NB, CTX, D])
        self.cT = self.din("cT", [D, NB + 1])
        self.out = nc.dram_tensor("out", [NB, SEQ, D], F32, kind="ExternalOutput").ap()
        self.c_cm = self.din("cm", [6, 128, 128])
        self.c_rr = self.din("rope_ret", [2, 2, 128, SEQ])
        self.c_rd = self.din("rope_dif", [2, 128, SEQ])
        self.c_i16 = self.din("iota16", [128, 16])
        W = []
        for li, kind in enumerate(self.kinds):
            w = {}
            p = "L%d_" % li
            w['ada_w'] = self.din(p + "ada_w", [D, 6 * D])
            w['ada_b'] = self.din(p + "ada_b", [6 * D])
            w['ln_g'] = self.din(p + "ln_g", [2, D])
            w['ln_b'] = self.din(p + "ln_b", [2, D])
            w['wq'] = self.din(p + "peer_wq", [D, 2048])
            w['keys'] = self.din(p + "peer_keys", [16, 128, 128])
            w['u'] = self.din(p + "peer_u", [16384, D])
            w['v'] = self.din(p + "peer_v", [16384, D])
            if kind == 0:
                w['w_up'] = self.din(p + "w_up", [D, 6144])
                w['conv_w'] = self.din(p + "conv_w", [5, 2048])
                w['conv_b'] = self.din(p + "conv_b", [2048])
                w['w_qk'] = self.din(p + "w_qk", [2048, 4096])
                w['w_v'] = self.din(p + "w_v", [2048, 2048])
                w['w_gate'] = self.din(p + "w_gate", [6144, 16])
                w['b_gate'] = self.din(p + "b_gate", [16])
                w['norm_g'] = self.din(p + "norm_g", [2048])
                w['skip'] = self.din(p + "skip", [2048])
                w['w_down'] = self.din(p + "w_down", [2048, D])
            elif kind == 1:
                w['w_in'] = self.din(p + "w_in", [D, 6208])
                w['conv_w'] = self.din(p + "conv_w", [5, 4096])
                w['conv_b'] = self.din(p + "conv_b", [4096])
                w['dt_bias'] = self.din(p + "dt_bias", [64])
                w['a_log'] = self.din(p + "a_log", [64])
                w['d'] = self.din(p + "d", [32])
                w['norm_g'] = self.din(p + "norm_g", [2048])
                w['w_out'] = self.din(p + "w_out", [2048, D])
            elif kind == 2:
                w['w_qkv'] = self.din(p + "w_qkv", [D, 3072])
                w['lam'] = self.din(p + "lam", [4, 64])
                w['norm_g'] = self.din(p + "norm_g", [128])
                w['w_out'] = self.din(p + "w_out", [D, D])
            else:
                w['w_in'] = self.din(p + "w_in", [D, 6144])
                w['decay'] = self.din(p + "decay", [8])
                w['norm_g'] = self.din(p + "norm_g", [2048])
                w['w_out'] = self.din(p + "w_out", [2048, D])
            W.append(w)
        self.W = W
        self.HT = self.dscr("HT", [D, T])
        self.YT = self.dscr("YT", [D, T])
        self.H1T = self.dscr("H1T", [D, T])
        self.QPT = self.dscr("QPT", [2048, T])
        self.A = [self.dscr("A%d" % i, [2048, T]) for i in range(5)]
        self.A4k = self.dscr("A4k", [4096, T])
        self.Bt = [self.dscr("B%d" % i, [T, 2048]) for i in range(3)]
        self.XC4k = self.dscr("XC4k", [4096, T])
        self.cm = self.sb("cm", [128, 6, 128])
        self.ident = self.cm[:, 0, :]
        self.trif = self.cm[:, 1, :]
        self.trib = self.cm[:, 2, :]
        self.mnf = self.cm[:, 3, :]
        self.mnb = self.cm[:, 4, :]
        self.ones = self.cm[:, 5, :]
        self.onesm = self.sb("onesm", [128, 128])
        self.i16 = self.sb("i16", [128, 16])
        self.mod = self.sb("mod", [128, self.L * 48, NB + 1])
        self.lng = self.sb("lng", [128, self.L * 2, 8])
        self.lnb = self.sb("lnb", [128, self.L * 2, 8])
        self.scT = self.sb("scT", [128, 8, NB + 1])
        self.adab = self.sb("adab", [128, 48])
        self.ringL = Ring([self.sb("bigL%d" % i, [128, 4096]) for i in range(1)])
        self.bigS = [self.sb("bigS%d" % i, [128, 4096]) for i in range(4)]
        self.ringS = Ring(self.bigS)
        self.stg = Ring([self.sb("stg%d" % i, [128, 512]) for i in range(3)])
        self.smt = [self.sb("sm%d" % i, [128, 512]) for i in range(8)]
        self.sm_q = Ring(self.smt[0:2])
        self.sm_k = Ring(self.smt[2:4])
        self.sm_v = Ring(self.smt[4:6])
        self.sm_w = Ring(self.smt[6:8])
        self.t2k = Ring([self.sb("t2k%d" % i, [128, 2048]) for i in range(5)])
        self.t1k = Ring([self.sb("t1k%d" % i, [128, 1024]) for i in range(5)])
        self.tiny = Ring([self.sb("tiny%d" % i, [128, 128]) for i in range(12)])
        self.par = self.sb("par", [128, 1100])
        self.fcol = self.sb("fcol", [128, NCH, 64])
        self.ldall = self.sb("ldall", [128, NCH, 64])
        self.igall = self.sb("igall", [128, NCH, 64])
        self.ccol = self.igall
        self.keysT = self.sb("keysT", [128, 16, 128])
        self.ps = es.enter_context(nc.psum_tensor("ps", [128, 8, 512], F32))
        self.psr = Ring([4, 5, 6, 7])
        self._alt = 0

        kb.dma(self.cm[:], self.c_cm.rearrange("k p n -> p k n"), w=["cm"])
        kb.dma(self.i16[:], self.c_i16, w=["i16"])
        kb.op('dve', lambda: nc.vector.memset(self.onesm[:], 1.0 / 1024.0), w=["onesm"])
        self.preamble_mod()
        for b in range(NB):
            self.load_input(b)
            for li, kind in enumerate(self.kinds):
                last = self.final_last and (li == self.L - 1)
                if b == 0 or True:
                    self.load_layer_params(li)
                [self.mlstm, self.ssd, self.diffattn, self.retention][kind](li, b, last)
                self.post(li, b, last)
        kb.drain()
        self.es.close()

    def pbank(self):
        i = self.psr.next()
        return i, "ps%d" % i

    def evac(self, out, in_, r, w):
        nc = self.nc
        self._alt ^= 1
        if self._alt:
            self.kb.op('act', lambda: nc.scalar.copy(out=out, in_=in_), r=r, w=w)
        else:
            self.kb.op('dve', lambda: nc.vector.tensor_copy(out=out, in_=in_), r=r, w=w)

    def modcol(self, li, j, b):
        return self.mod[:, li * 48 + j * 8: li * 48 + j * 8 + 8, b:b + 1]

    def modkeys(self, li, j):
        return [("mod", li, j * 8 + c) for c in range(8)]

    def preamble_mod(self):
        nc, kb = self.nc, self.kb
        NB = self.NB
        tmp = self.smt[0]
        tv = tmp[:, :8 * (NB + 1)].rearrange("p (c b) -> p c b", c=8)
        kb.dma(tv, self.cT.rearrange("(c p) b -> p c b", p=128), w=[tmp.name])
        kb.op('act', lambda: nc.scalar.activation(out=self.scT[:], in_=tv, func=AF.Silu), r=[tmp.name], w=["scT"])
        for li in range(self.L):
            w = self.W[li]
            kb.dma(self.adab[:], w['ada_b'].rearrange("(c p) -> p c", p=128), w=["adab"], allow_slow_non_contiguous=True)
            kb.dma(self.lng[:, li * 2:li * 2 + 2, :], w['ln_g'].rearrange("a (c p) -> p a c", p=128), w=[("lng", li)],
                   allow_slow_non_contiguous=True)
            kb.dma(self.lnb[:, li * 2:li * 2 + 2, :], w['ln_b'].rearrange("a (c p) -> p a c", p=128), w=[("lnb", li)],
                   allow_slow_non_contiguous=True)
            for nb in range(12):
                wt = self.ringS.next()
                wv = wt[:, :4096].rearrange("p (k n) -> p k n", k=8)
                kb.dma(wv, w['ada_w'].rearrange("(k p) n -> p k n", p=128)[:, :, nb * 512:(nb + 1) * 512], w=[wt.name])
                for sub in range(4):
                    ch = nb * 4 + sub
                    bi, bk = self.pbank()
                    pv = self.ps[:, bi, 0:NB + 1]
                    for k in range(8):
                        kb.op('pe', lambda k=k: nc.tensor.matmul(pv, wv[:, k, sub * 128:(sub + 1) * 128], self.scT[:, k, :],
                                                                  start=(k == 0), stop=(k == 7)),
                              r=[wt.name, "scT"], w=[bk])
                    add1 = 1.0 if (ch // 8) in (1, 4) else 0.0
                    dst = self.mod[:, li * 48 + ch, :]
                    kb.op('dve', lambda: nc.vector.tensor_scalar(out=dst, in0=pv, scalar1=self.adab[:, ch:ch + 1], scalar2=add1,
                                                                 op0=ALU.add, op1=ALU.add),
                          r=[bk, "adab"], w=[("mod", li, ch)])

    def load_input(self, b):
        kb = self.kb
        for tb in range(self.NCH):
            t0 = tb * 128
            src = self.cx[b, t0:t0 + 128, :] if tb < self.NCC else self.x[b, t0 - self.CTX:t0 - self.CTX + 128, :]
            xt = self.t1k.next()
            kb.dma(xt[:, :D], src, w=[xt.name])
            self.transpose_store(xt, 8, self.HT, "HT", 0, t0)

    def transpose_to(self, src_tile, nchunk, dst_tile):
        nc, kb = self.nc, self.kb
        dv = dst_tile[:, :nchunk * 128].rearrange("p (c t) -> p c t", c=nchunk)
        for c0 in range(0, nchunk, 4):
            bi, bk = self.pbank()
            n = min(4, nchunk - c0)
            for j in range(n):
                c = c0 + j
                kb.op('pe', lambda c=c, j=j: nc.tensor.transpose(self.ps[:, bi, j * 128:(j + 1) * 128],
                                                                   src_tile[:, c * 128:(c + 1) * 128], self.ident),
                      r=[src_tile.name, "cm"], w=[bk])
            self.evac(dv[:, c0:c0 + n, :], self.ps[:, bi, :n * 128].rearrange("p (c t) -> p c t", c=n),
                      r=[bk], w=[dst_tile.name])
        return dv

    def transpose_store(self, src_tile, nchunk, dst_dram, dname, row0, t0, scale_cols=None):
        nc, kb = self.nc, self.kb
        ft = self.t1k.next() if nchunk <= 8 else self.t2k.next()
        dv = self.transpose_to(src_tile, nchunk, ft)
        if scale_cols is not None:
            kb.op('dve', lambda: nc.vector.tensor_tensor(out=dv, in0=dv, in1=scale_cols.unsqueeze(2).to_broadcast([128, nchunk, 128]),
                                                         op=ALU.mult), r=[ft.name, "par"], w=[ft.name])
        kb.dma(dst_dram[row0:row0 + nchunk * 128, t0:t0 + 128].rearrange("(c p) t -> p c t", p=128), dv,
               r=[ft.name], w=kF(dname, row0, row0 + nchunk * 128, t0, t0 + 128))

    def load_layer_params(self, li):
        nc, kb = self.nc, self.kb
        w = self.W[li]
        kind = self.kinds[li]
        par = self.par
        pk = ["par"]
        if kind == 0:
            for j in range(5):
                kb.dma(par[:, j * 16:(j + 1) * 16], w['conv_w'][j].rearrange("(c p) -> p c", p=128), w=pk, allow_slow_non_contiguous=True)
            kb.dma(par[:, 80:96], w['conv_b'].rearrange("(c p) -> p c", p=128), w=pk, allow_slow_non_contiguous=True)
            kb.dma(par[:, 96:112], w['norm_g'].rearrange("(c p) -> p c", p=128), w=pk, allow_slow_non_contiguous=True)
            kb.dma(par[:, 112:128], w['skip'].rearrange("(c p) -> p c", p=128), w=pk, allow_slow_non_contiguous=True)
            kb.dma(par[:, 128:144], w['b_gate'].partition_broadcast(128), w=pk)
            kb.dma(par[:, 256:1024].rearrange("p (c j) -> p c j", c=48), w['w_gate'].rearrange("(c p) j -> p c j", p=128), w=pk)
        elif kind == 1:
            for j in range(5):
                kb.dma(par[:, j * 32:(j + 1) * 32], w['conv_w'][j].rearrange("(c p) -> p c", p=128), w=pk, allow_slow_non_contiguous=True)
            kb.dma(par[:, 160:192], w['conv_b'].rearrange("(c p) -> p c", p=128), w=pk, allow_slow_non_contiguous=True)
            kb.dma(par[:, 192:256], w['dt_bias'].partition_broadcast(128), w=pk)
            kb.dma(par[:, 256:320], w['a_log'].partition_broadcast(128), w=pk)
            kb.op('act', lambda: nc.scalar.activation(out=par[:, 256:320], in_=par[:, 256:320], func=AF.Exp), r=pk, w=pk)
            kb.op('dve', lambda: nc.vector.tensor_scalar(out=par[:, 256:320], in0=par[:, 256:320], scalar1=-1.0, scalar2=None,
                                                         op0=ALU.mult), r=pk, w=pk)
            kb.dma(par[:, 320:352], w['d'].partition_broadcast(128), w=pk)
            kb.dma(par[:, 352:368], w['norm_g'].rearrange("(c p) -> p c", p=128), w=pk, allow_slow_non_contiguous=True)
        elif kind == 2:
            kb.dma(par[:, 0:256], w['lam'].rearrange("a b -> (a b)").partition_broadcast(128), w=pk)
            kb.dma(par[:, 256:384], w['norm_g'].partition_broadcast(128), w=pk)
            lam_init = 0.8 - 0.6 * math.exp(-0.3 * self.layer_ids[li])
            pv = par[:, 0:256].rearrange("p (a b c) -> p a b c", a=2, b=2)
            kb.op('dve', lambda: nc.vector.tensor_tensor(out=par[:, 512:640].rearrange("p (a c) -> p a c", a=2),
                                                         in0=pv[:, :, 0, :], in1=pv[:, :, 1, :], op=ALU.mult), r=pk, w=pk)
            kb.op('dve', lambda: nc.vector.reduce_sum(out=par[:, 402:404], in_=par[:, 512:640].rearrange("p (a c) -> p a c", a=2),
                                                      axis=AX.X), r=pk, w=pk)
            kb.op('act', lambda: nc.scalar.activation(out=par[:, 404:406], in_=par[:, 402:404], func=AF.Exp), r=pk, w=pk)
            kb.op('dve', lambda: nc.vector.tensor_tensor(out=par[:, 400:401], in0=par[:, 405:406], in1=par[:, 404:405],
                                                         op=ALU.subtract), r=pk, w=pk)
            kb.op('dve', lambda: nc.vector.tensor_scalar(out=par[:, 400:401], in0=par[:, 400:401], scalar1=-lam_init, scalar2=None,
                                                         op0=ALU.add), r=pk, w=pk)
        else:
            kb.dma(par[:, 0:8], w['decay'].partition_broadcast(128), w=pk)
            self.log_sigmoid(par[:, 0:8], par[:, 0:8], par[:, 8:16], pk)
            kb.dma(par[:, 16:32], w['norm_g'].rearrange("(c p) -> p c", p=128), w=pk, allow_slow_non_contiguous=True)
        for j0 in range(0, 16, 4):
            kt = self.t2k.next()
            kb.dma(kt[:, :512].rearrange("p (j d) -> p j d", j=4), w['keys'][j0:j0 + 4].rearrange("j n d -> n j d"), w=[kt.name])
            bi, bk = self.pbank()
            for j in range(4):
                kb.op('pe', lambda j=j: nc.tensor.transpose(self.ps[:, bi, j * 128:(j + 1) * 128], kt[:, j * 128:(j + 1) * 128],
                                                             self.ident), r=[kt.name, "cm"], w=[bk])
            self.evac(self.keysT[:, j0:j0 + 4, :], self.ps[:, bi, :].rearrange("p (j n) -> p j n", j=4), r=[bk], w=["keysT"])

    def log_sigmoid(self, out, in_, tmp, keys):
        nc, kb = self.nc, self.kb
        kb.op('act', lambda: nc.scalar.activation(out=tmp, in_=in_, func=AF.Exp, scale=-1.0), r=keys, w=keys)
        kb.op('act', lambda: nc.scalar.activation(out=tmp, in_=tmp, func=AF.Ln, bias=1.0, scale=1.0), r=keys, w=keys)
        kb.op('dve', lambda: nc.vector.tensor_scalar(out=out, in0=tmp, scalar1=-1.0, scalar2=None, op0=ALU.mult), r=keys, w=keys)

    def load_xin(self, src, sname, K, t0, tg, premod=None):
        nc, kb = self.nc, self.kb
        KC = K // 128
        xt = self.ringL.next()
        xv = xt[:, :KC * tg].rearrange("p (k t) -> p k t", k=KC)
        kb.dma(xv, src[0:K, t0:t0 + tg].rearrange("(k p) t -> p k t", p=128), r=kF(sname, 0, K, t0, t0 + tg), w=[xt.name])
        if premod is not None:
            li, js, jt, b = premod
            col = self.NB if t0 < self.CTX else b
            sc = self.modcol(li, js, col).to_broadcast([128, 8, tg])
            sh = self.modcol(li, jt, col).to_broadcast([128, 8, tg])
            kb.op('dve', lambda: nc.vector.tensor_tensor(out=xv, in0=xv, in1=sc, op=ALU.mult),
                  r=[xt.name] + self.modkeys(li, js), w=[xt.name])
            kb.op('dve', lambda: nc.vector.tensor_tensor(out=xv, in0=xv, in1=sh, op=ALU.add),
                  r=[xt.name] + self.modkeys(li, jt), w=[xt.name])
        return xt, xv

    def proj(self, src, sname, K, Wap, segs, premod=None, tgmax=None, include_ctx=True, groups=None):
        nc, kb = self.nc, self.kb
        KC = K // 128
        if tgmax is None:
            tgmax = 512 if KC <= 8 else 256
        wb = min(512, 4096 // KC)
        Wv = Wap.rearrange("(k p) n -> p k n", p=128)
        for (t0, tg) in (groups if groups is not None else self.groups(tgmax, include_ctx)):
            xt, xv = self.load_xin(src, sname, K, t0, tg, premod)
            blocks = []
            for (n0, n1, mode, epi) in segs:
                c = n0
                while c < n1:
                    bw = min(wb, n1 - c)
                    blocks.append((c, bw, mode, epi))
                    c += bw
            tiles = {}

            def loadw(i):
                c, bw, mode, epi = blocks[i]
                wt = self.ringS.next()
                wv = wt[:, :KC * bw].rearrange("p (k n) -> p k n", k=KC)
                kb.dma(wv, Wv[:, :, c:c + bw], w=[wt.name])
                tiles[i] = (wt, wv)
            loadw(0)
            for i in range(len(blocks)):
                if i + 1 < len(blocks):
                    loadw(i + 1)
                c, bw, mode, epi = blocks[i]
                wt, wv = tiles.pop(i)
                if mode == 'F':
                    for sub in range((bw + 127) // 128):
                        m = min(128, bw - sub * 128)
                        bi, bk = self.pbank()
                        pv = self.ps[:m, bi, :tg]
                        for k in range(KC):
                            kb.op('pe', lambda k=k: nc.tensor.matmul(pv, wv[:, k, sub * 128:sub * 128 + m], xv[:, k, :],
                                                                      start=(k == 0), stop=(k == KC - 1)),
                                  r=[wt.name, xt.name], w=[bk])
                        epi(pv, bk, (c + sub * 128) // 128, t0, tg)
                else:
                    for tt in range(tg // 128):
                        bi, bk = self.pbank()
                        pv = self.ps[:, bi, :bw]
                        for k in range(KC):
                            kb.op('pe', lambda k=k: nc.tensor.matmul(pv, xv[:, k, tt * 128:(tt + 1) * 128], wv[:, k, :],
                                                                      start=(k == 0), stop=(k == KC - 1)),
                                  r=[wt.name, xt.name], w=[bk])
                        epi(pv, bk, c, bw, t0 + tt * 128)

    def epiF(self, dst, dname, nbase):
        kb = self.kb

        def epi(pv, bk, cc, t0, tg):
            st = self.stg.next()
            sv = st[:, :tg]
            rc = cc - nbase
            self.evac(sv, pv, r=[bk], w=[st.name])
            kb.dma(dst[rc * 128:(rc + 1) * 128, t0:t0 + tg], sv, r=[st.name], w=kF(dname, rc * 128, rc * 128 + 128, t0, t0 + tg))
        return epi

    def epiT(self, dst, dname, cbase):
        kb = self.kb

        def epi(pv, bk, c, bw, tok0):
            st = self.stg.next()
            sv = st[:, :bw]
            self.evac(sv, pv, r=[bk], w=[st.name])
            kb.dma(dst[tok0:tok0 + 128, c - cbase:c - cbase + bw], sv, r=[st.name],
                   w=kT(dname, tok0, tok0 + 128, c - cbase, c - cbase + bw))
        return epi

    def proj_rope(self, src, sname, Wap, col0, ncols, dst, dname, premod, d):
        nc, kb = self.nc, self.kb
        Wv = Wap.rearrange("(k p) n -> p k n", p=128)
        q = d // 4
        nblk = 128 // (2 * q)
        for (t0, tg) in self.groups(512, True):
            lat = t0 >= self.CTX
            xt, xv = self.load_xin(src, sname, D, t0, tg, premod)
            for ch in range(ncols // 128):
                c = col0 + ch * 128
                wt = self.ringS.next()
                wv = wt[:, :2048].rearrange("p (k n) -> p k n", k=8)
                kb.dma(wv[:, :, 0:128], Wv[:, :, c:c + 128], w=[wt.name])
                if lat:
                    src4 = wv[:, :, 0:128].rearrange("p k (b h q) -> p k b h q", b=nblk, h=2)
                    dst4 = wv[:, :, 128:256].rearrange("p k (b h q) -> p k b h q", b=nblk, h=2)
                    if nblk == 1:
                        kb.op('pool', lambda: nc.gpsimd.tensor_copy(out=dst4[:, :, 0, 0, :], in_=src4[:, :, 0, 1, :]),
                              r=[wt.name], w=[(wt.name, 'p')])
                        kb.op('pool', lambda: nc.gpsimd.tensor_copy(out=dst4[:, :, 0, 1, :], in_=src4[:, :, 0, 0, :]),
                              r=[wt.name], w=[(wt.name, 'p')])
                    else:
                        kb.op('pool', lambda: nc.gpsimd.tensor_copy(out=dst4[:, :, :, 0, :], in_=src4[:, :, :, 1, :]),
                              r=[wt.name], w=[(wt.name, 'p')])
                        kb.op('pool', lambda: nc.gpsimd.tensor_copy(out=dst4[:, :, :, 1, :], in_=src4[:, :, :, 0, :]),
                              r=[wt.name], w=[(wt.name, 'p')])
                bi, bk = self.pbank()
                pv = self.ps[:, bi, :tg]
                for k in range(8):
                    kb.op('pe', lambda k=k: nc.tensor.matmul(pv, wv[:, k, 0:128], xv[:, k, :], start=(k == 0), stop=(k == 7)),
                          r=[wt.name, xt.name], w=[bk])
                st = self.stg.next()
                sv = st[:, :tg]
                if not lat:
                    self.evac(sv, pv, r=[bk], w=[st.name])
                else:
                    bi2, bk2 = self.pbank()
                    pv2 = self.ps[:, bi2, :tg]
                    for k in range(8):
                        kb.op('pe', lambda k=k: nc.tensor.matmul(pv2, wv[:, k, 128:256], xv[:, k, :], start=(k == 0), stop=(k == 7)),
                              r=[wt.name, (wt.name, 'p'), xt.name], w=[bk2])
                    ct = self.sm_k.next()
                    sn = self.sm_v.next()
                    s0 = t0 - self.CTX
                    if d == 256:
                        lc = ch % 2
                        kb.dma(ct[:, :tg], self.c_rr[0, lc, :, s0:s0 + tg], w=[ct.name])
                        kb.dma(sn[:, :tg], self.c_rr[1, lc, :, s0:s0 + tg], w=[sn.name])
                    else:
                        kb.dma(ct[:, :tg], self.c_rd[0, :, s0:s0 + tg], w=[ct.name])
                        kb.dma(sn[:, :tg], self.c_rd[1, :, s0:s0 + tg], w=[sn.name])
                    kb.op('dve', lambda: nc.vector.tensor_tensor(out=ct[:, :tg], in0=pv, in1=ct[:, :tg], op=ALU.mult),
                          r=[bk, ct.name], w=[ct.name])
                    kb.op('dve', lambda: nc.vector.tensor_tensor(out=sn[:, :tg], in0=pv2, in1=sn[:, :tg], op=ALU.mult),
                          r=[bk2, sn.name], w=[sn.name])
                    kb.op('dve', lambda: nc.vector.tensor_tensor(out=sv, in0=ct[:, :tg], in1=sn[:, :tg], op=ALU.add),
                          r=[ct.name, sn.name], w=[st.name])
                kb.dma(dst[ch * 128:(ch + 1) * 128, t0:t0 + tg], sv, r=[st.name], w=kF(dname, ch * 128, ch * 128 + 128, t0, t0 + tg))

    def conv(self, src, sname, nch, wcol, bcol, dst, dname):
        nc, kb = self.nc, self.kb
        segs = [(0, self.CTX), (self.CTX, self.T)]
        for c in range(nch):
            for (a, e) in segs:
                n = e - a
                for o in range(0, n, 2048):
                    m = min(2048, n - o)
                    xp = self.ringS.next()
                    ac = self.ringS.next()
                    lo = 2 if o == 0 else 0
                    hi = 2 if o + m == n else 0
                    if lo:
                        kb.op('pool', lambda: nc.gpsimd.memset(xp[:, 0:2], 0.0), w=[xp.name])
                    if hi:
                        kb.op('pool', lambda: nc.gpsimd.memset(xp[:, 2 + m:4 + m], 0.0), w=[xp.name])
                    s0 = a + o - (2 - lo)
                    s1 = a + o + m + (2 - hi)
                    kb.dma(xp[:, lo:lo + (s1 - s0)], src[c * 128:(c + 1) * 128, s0:s1],
                           r=kF(sname, c * 128, c * 128 + 128, s0, s1), w=[xp.name])
                    rk = [xp.name, "par"]
                    kb.op('dve', lambda: nc.vector.tensor_scalar(out=ac[:, :m], in0=xp[:, 0:m], scalar1=wcol(c, 0), scalar2=None,
                                                                 op0=ALU.mult), r=rk, w=[ac.name])
                    for j in range(1, 5):
                        kb.op('dve', lambda j=j: nc.vector.scalar_tensor_tensor(out=ac[:, :m], in0=xp[:, j:j + m], scalar=wcol(c, j),
                                                                                 in1=ac[:, :m], op0=ALU.mult, op1=ALU.add),
                              r=rk + [ac.name], w=[ac.name])
                    bc_ = bcol(c)
                    kb.op('act', lambda: nc.scalar.activation(out=ac[:, :m], in_=ac[:, :m], func=AF.Silu, bias=bc_, scale=1.0),
                          r=[ac.name, "par"], w=[ac.name])
                    kb.dma(dst[c * 128:(c + 1) * 128, a + o:a + o + m], ac[:, :m], r=[ac.name],
                           w=kF(dname, c * 128, c * 128 + 128, a + o, a + o + m))

    def decay_cols(self, NU, have_ig, scale):
        nc, kb = self.nc, self.kb
        NCH, NCC = self.NCH, self.NCC
        N2 = 2 * NU
        lns = math.log(scale)
        for lc in range(NCH):
            bi, bk = self.pbank()
            fs = [sc for sc in range(NCH) if sc <= lc]
            for i, sc in enumerate(fs):
                m = self.trif if sc == lc else self.ones
                kb.op('pe', lambda m=m, sc=sc, i=i: nc.tensor.matmul(self.ps[:, bi, 0:NU], m, self.ldall[:, sc, 0:NU],
                                                                     start=(i == 0), stop=(i == len(fs) - 1)),
                      r=["cm", "ldall"], w=[bk])
            if lc < NCC:
                bs = [sc for sc in range(lc, NCC)]
            else:
                bs = list(range(NCC)) + [sc for sc in range(lc, NCH)]
            for i, sc in enumerate(bs):
                m = self.trib if sc == lc else self.ones
                kb.op('pe', lambda m=m, sc=sc, i=i: nc.tensor.matmul(self.ps[:, bi, NU:N2], m, self.ldall[:, sc, NU:N2],
                                                                     start=(i == 0), stop=(i == len(bs) - 1)),
                      r=["cm", "ldall"], w=[bk])
            kb.op('dve', lambda: nc.vector.tensor_copy(out=self.fcol[:, lc, 0:N2], in_=self.ps[:, bi, 0:N2]), r=[bk], w=["fcol"])
            if have_ig:
                kb.op('dve', lambda: nc.vector.scalar_tensor_tensor(out=self.ccol[:, lc, 0:N2], in0=self.igall[:, lc, 0:N2], scalar=lns,
                                                                     in1=self.fcol[:, lc, 0:N2], op0=ALU.add, op1=ALU.subtract),
                      r=["fcol", "igall"], w=["igall"])
            else:
                kb.op('dve', lambda: nc.vector.tensor_scalar(out=self.ccol[:, lc, 0:N2], in0=self.fcol[:, lc, 0:N2], scalar1=-1.0,
                                                             scalar2=lns, op0=ALU.mult, op1=ALU.add), r=["fcol"], w=["igall"])

    def quad(self, G, R, KC, DV, sep, qsrc, qname, qrow0, ksrc, kname, krow0, vsrc, vname, post, lbs=None):
        nc, kb = self.nc, self.kb
        NCH, NCC = self.NCH, self.NCC
        NU = G * R
        if lbs is None:
            lbs = range(NCH)
        for lb in lbs:
            sbs_f = [sb for sb in range(NCH) if sb <= lb]
            if lb < NCC:
                sbs_b = list(range(lb, NCC))
            else:
                sbs_b = list(range(NCC)) + list(range(lb, NCH))
            union = sorted(set(sbs_f) | set(sbs_b))
            hh = self.t2k.next()
            for g in range(G):
                rowL = self.t1k.next()
                rl = rowL[:, :2 * R * 128].rearrange("p (j l) -> p j l", j=2 * R)
                for j0 in range(0, 2 * R, 4):
                    bi, bk = self.pbank()
                    n = min(4, 2 * R - j0)
                    for j in range(j0, j0 + n):
                        d_, r_ = j // R, j % R
                        ud = d_ * NU + g * R + r_
                        bc = self.tiny.next()
                        kb.op('dve', lambda bc=bc, ud=ud: nc.vector.tensor_copy(out=bc[:, :128],
                                                                                 in_=self.fcol[:, lb, ud:ud + 1].to_broadcast([128, 128])),
                              r=["fcol"], w=[bc.name])
                        kb.op('pe', lambda bc=bc, j=j: nc.tensor.matmul(self.ps[:, bi, (j - j0) * 128:(j - j0 + 1) * 128], bc[:, :128],
                                                                         self.ident, start=True, stop=True),
                              r=[bc.name, "cm"], w=[bk])
                    self.evac(rl[:, j0:j0 + n, :], self.ps[:, bi, :n * 128].rearrange("p (j l) -> p j l", j=n), r=[bk], w=[rowL.name])
                qt = self.sm_q.next()
                qv = qt[:, :KC * 128].rearrange("p (k t) -> p k t", k=KC)
                r0 = qrow0 + g * KC * 128
                kb.dma(qv, qsrc[r0:r0 + KC * 128, lb * 128:(lb + 1) * 128].rearrange("(k p) t -> p k t", p=128),
                       r=kF(qname, r0, r0 + KC * 128, lb * 128, lb * 128 + 128), w=[qt.name])
                steps = []
                for sb in union:
                    for d_ in (0, 1):
                        if sb in (sbs_f if d_ == 0 else sbs_b):
                            steps.append((sb, d_))
                first, lastu = {}, {}
                for i, (sb, d_) in enumerate(steps):
                    a = d_ if sep else 0
                    if a not in first:
                        first[a] = i
                    lastu[a] = i
                loaded = {}

                def load_sb(sb):
                    kt = self.sm_k.next()
                    kv = kt[:, :KC * 128].rearrange("p (k t) -> p k t", k=KC)
                    k0 = krow0 + g * KC * 128
                    kb.dma(kv, ksrc[k0:k0 + KC * 128, sb * 128:(sb + 1) * 128].rearrange("(k p) t -> p k t", p=128),
                           r=kF(kname, k0, k0 + KC * 128, sb * 128, sb * 128 + 128), w=[kt.name])
                    vt = self.sm_v.next()
                    c0 = g * R * DV
                    kb.dma(vt[:, :R * DV], vsrc[sb * 128:(sb + 1) * 128, c0:c0 + R * DV],
                           r=kT(vname, sb * 128, sb * 128 + 128, c0, c0 + R * DV), w=[vt.name])
                    loaded[sb] = (kt, kv, vt)
                load_sb(union[0])
                si_ = 0
                for ui, sb in enumerate(union):
                    if ui + 1 < len(union):
                        load_sb(union[ui + 1])
                    kt, kv, vt = loaded.pop(sb)
                    bi, bk = self.pbank()
                    sraw = self.ps[:, bi, 0:128]
                    for k in range(KC):
                        kb.op('pe', lambda k=k: nc.tensor.matmul(sraw, kv[:, k, :], qv[:, k, :], start=(k == 0), stop=(k == KC - 1)),
                              r=[kt.name, qt.name], w=[bk])
                    while si_ < len(steps) and steps[si_][0] == sb:
                        _, d_ = steps[si_]
                        wt = self.sm_w.next()
                        wv = wt[:, :R * 128].rearrange("p (r l) -> p r l", r=R)
                        u0 = d_ * NU + g * R
                        cc = self.ccol[:, sb, u0:u0 + R]
                        rls = rl[:, d_ * R:(d_ + 1) * R, :]
                        diag = (sb == lb)
                        if R == 1 and not diag:
                            kb.op('act', lambda: nc.scalar.activation(out=wv[:, 0, :], in_=rls[:, 0, :], func=AF.Exp, bias=cc[:, 0:1], scale=1.0),
                                  r=[rowL.name, "igall"], w=[wt.name])
                        else:
                            kb.op('dve', lambda: nc.vector.tensor_tensor(out=wv, in0=rls, in1=cc.unsqueeze(2).to_broadcast([128, R, 128]),
                                                                         op=ALU.add), r=[rowL.name, "igall"], w=[wt.name])
                            if diag:
                                mk = self.mnf if d_ == 0 else self.mnb
                                kb.op('dve', lambda: nc.vector.tensor_tensor(out=wv, in0=wv, in1=mk.unsqueeze(1).to_broadcast([128, R, 128]),
                                                                             op=ALU.add), r=[wt.name, "cm"], w=[wt.name])
                            kb.op('act', lambda: nc.scalar.activation(out=wv, in_=wv, func=AF.Exp), r=[wt.name], w=[wt.name])
                        kb.op('dve', lambda: nc.vector.tensor_tensor(out=wv, in0=wv, in1=sraw.unsqueeze(1).to_broadcast([128, R, 128]),
                                                                     op=ALU.mult), r=[wt.name, bk], w=[wt.name])
                        a = d_ if sep else 0
                        for r_ in range(R):
                            if sep:
                                acc = self.ps[:, d_, 0:DV]
                            else:
                                acc = self.ps[:, r_, 0:DV]
                            kb.op('pe', lambda r_=r_, acc=acc: nc.tensor.matmul(acc, wv[:, r_, :], vt[:, r_ * DV:(r_ + 1) * DV],
                                                                                 start=(first[a] == si_), stop=(lastu[a] == si_)),
                                  r=[wt.name, vt.name], w=["ps%d" % (d_ if sep else r_)])
                            if sep:
                                kb.op('pe', lambda: nc.tensor.matmul(self.ps[:, 2 + d_, 0:1], wv[:, r_, :], self.ones[:, 0:1],
                                                                      start=(first[a] == si_), stop=(lastu[a] == si_)),
                                      r=[wt.name, "cm"], w=["ps%d" % (2 + d_)])
                        si_ += 1
                c0 = g * R * DV
                if sep:
                    tn = self.tiny.next()
                    for d_ in (0, 1):
                        kb.op('act', lambda d_=d_: nc.scalar.activation(out=tn[:, d_:d_ + 1], in_=self.ps[:, 2 + d_, 0:1], func=AF.Abs),
                              r=["ps%d" % (2 + d_)], w=[tn.name])
                    kb.op('dve', lambda: nc.vector.tensor_scalar(out=tn[:, 0:2], in0=tn[:, 0:2], scalar1=1.0, scalar2=None, op0=ALU.max),
                          r=[tn.name], w=[tn.name])
                    kb.op('dve', lambda: nc.vector.reciprocal(out=tn[:, 2:4], in_=tn[:, 0:2]), r=[tn.name], w=[tn.name])
                    kb.op('dve', lambda: nc.vector.tensor_scalar(out=hh[:, c0:c0 + DV], in0=self.ps[:, 0, 0:DV], scalar1=tn[:, 2:3],
                                                                 scalar2=None, op0=ALU.mult), r=["ps0", tn.name], w=[hh.name])
                    kb.op('dve', lambda: nc.vector.scalar_tensor_tensor(out=hh[:, c0:c0 + DV], in0=self.ps[:, 1, 0:DV], scalar=tn[:, 3:4],
                                                                         in1=hh[:, c0:c0 + DV], op0=ALU.mult, op1=ALU.add),
                          r=["ps1", tn.name, hh.name], w=[hh.name])
                else:
                    self.evac(hh[:, c0:c0 + R * DV].rearrange("p (r d) -> p r d", r=R), self.ps[:, 0:R, 0:DV],
                              r=["ps%d" % r_ for r_ in range(R)], w=[hh.name])
            post(lb, hh)

    def head_norm_tok(self, x, nh, dh, eps, center=True, sq=None):
        nc, kb = self.nc, self.kb
        xv = x[:, :nh * dh].rearrange("p (h d) -> p h d", h=nh)
        tn = self.tiny.next()
        if sq is None:
            sq = self.t2k.next() if nh * dh > 1024 else self.t1k.next()
        sqv = sq[:, :nh * dh].rearrange("p (h d) -> p h d", h=nh)
        if center:
            kb.op('dve', lambda: nc.vector.reduce_sum(out=tn[:, 0:nh], in_=xv, axis=AX.X), r=[x.name], w=[tn.name])
            kb.op('dve', lambda: nc.vector.tensor_scalar(out=tn[:, 0:nh], in0=tn[:, 0:nh], scalar1=1.0 / dh, scalar2=None, op0=ALU.mult),
                  r=[tn.name], w=[tn.name])
            kb.op('dve', lambda: nc.vector.tensor_tensor(out=xv, in0=xv, in1=tn[:, 0:nh].unsqueeze(2).to_broadcast([128, nh, dh]),
                                                         op=ALU.subtract), r=[x.name, tn.name], w=[x.name])
        kb.op('dve', lambda: nc.vector.tensor_tensor(out=sqv, in0=xv, in1=xv, op=ALU.mult), r=[x.name], w=[sq.name])
        kb.op('dve', lambda: nc.vector.reduce_sum(out=tn[:, 16:16 + nh], in_=sqv, axis=AX.X), r=[sq.name], w=[tn.name])
        kb.op('dve', lambda: nc.vector.tensor_scalar(out=tn[:, 32:32 + nh], in0=tn[:, 16:16 + nh], scalar1=1.0 / dh, scalar2=float(eps),
                                                     op0=ALU.mult, op1=ALU.add), r=[tn.name], w=[tn.name])
        kb.op('act', lambda: nc.scalar.activation(out=tn[:, 32:32 + nh], in_=tn[:, 32:32 + nh], func=AF.Sqrt), r=[tn.name], w=[tn.name])
        kb.op('dve', lambda: nc.vector.reciprocal(out=tn[:, 48:48 + nh], in_=tn[:, 32:32 + nh]), r=[tn.name], w=[tn.name])
        kb.op('dve', lambda: nc.vector.tensor_tensor(out=xv, in0=xv, in1=tn[:, 48:48 + nh].unsqueeze(2).to_broadcast([128, nh, dh]),
                                                     op=ALU.mult), r=[x.name, tn.name], w=[x.name])

    def mlstm(self, li, b, last):
        nc, kb = self.nc, self.kb
        w = self.W[li]
        par = self.par
        NCH, NCC = self.NCH, self.NCC
        XM, ZT, XC, QT, KT = self.A[0], self.A[1], self.A[2], self.A[3], self.A[4]
        OP, V, = self.Bt[0], self.Bt[1]
        pm = (li, 1, 0, b)
        self.proj(self.HT, "HT", D, w['w_up'], [
            (0, 2048, 'F', self.epiF(XM, "A0", 0)),
            (2048, 4096, 'F', self.epiF(ZT, "A1", 16)),
            (4096, 6144, 'T', self.epiT(OP, "B0", 4096))], premod=pm)
        self.conv(XM, "A0", 16, lambda c, j: par[:, j * 16 + c:j * 16 + c + 1], lambda c: par[:, 80 + c:81 + c], XC, "A2")
        self.proj(XC, "A2", 2048, w['w_qk'], [
            (0, 2048, 'F', self.epiF(QT, "A3", 0)),
            (2048, 4096, 'F', self.epiF(KT, "A4", 16))])
        self.proj(XM, "A0", 2048, w['w_v'], [(0, 2048, 'T', self.epiT(V, "B1", 0))])
        wg = par[:, 256:1024].rearrange("p (c j) -> p c j", c=48)
        for tb in range(NCH):
            t0 = tb * 128
            qt = self.t2k.next()
            kt = self.t2k.next()
            vt = self.t2k.next()
            qv = qt[:, :2048].rearrange("p (k t) -> p k t", k=16)
            kv = kt[:, :2048].rearrange("p (k t) -> p k t", k=16)
            kb.dma(qv, QT[:, t0:t0 + 128].rearrange("(k p) t -> p k t", p=128), r=kF("A3", 0, 2048, t0, t0 + 128), w=[qt.name])
            kb.dma(kv, KT[:, t0:t0 + 128].rearrange("(k p) t -> p k t", p=128), r=kF("A4", 0, 2048, t0, t0 + 128), w=[kt.name])
            kb.dma(vt[:, :2048], V[t0:t0 + 128, :], r=kT("B1", t0, t0 + 128, 0, 2048), w=[vt.name])
            vT = self.t2k.next()
            vv = self.transpose_to(vt, 16, vT)
            gi, gk = self.pbank()
            gp = self.ps[:, gi, 0:16]
            for c in range(48):
                src, sk = (qv, qt.name) if c < 16 else ((kv, kt.name) if c < 32 else (vv, vT.name))
                kb.op('pe', lambda c=c, src=src: nc.tensor.matmul(gp, src[:, c % 16, :], wg[:, c, :], start=(c == 0), stop=(c == 47)),
                      r=[sk, "par"], w=[gk])
            gt = self.tiny.next()
            kb.op('dve', lambda: nc.vector.tensor_tensor(out=gt[:, 0:16], in0=gp, in1=par[:, 128:144], op=ALU.add),
                  r=[gk, "par"], w=[gt.name])
            g4 = gt[:, 0:16].rearrange("p (a x h) -> p a x h", a=2, x=2)
            kb.op('act', lambda: nc.scalar.activation(out=gt[:, 16:24].rearrange("p (a h) -> p a h", a=2), in_=g4[:, :, 1, :],
                                                      func=AF.Exp, scale=-1.0), r=[gt.name], w=[gt.name])
            kb.op('act', lambda: nc.scalar.activation(out=gt[:, 16:24], in_=gt[:, 16:24], func=AF.Ln, bias=1.0, scale=1.0),
                  r=[gt.name], w=[gt.name])
            kb.op('dve', lambda: nc.vector.tensor_scalar(out=self.ldall[:, tb, 0:8], in0=gt[:, 16:24], scalar1=-1.0, scalar2=None,
                                                         op0=ALU.mult), r=[gt.name], w=["ldall"])
            kb.op('dve', lambda: nc.vector.tensor_copy(out=self.igall[:, tb, 0:8].rearrange("p (a h) -> p a h", a=2), in_=g4[:, :, 0, :]),
                  r=[gt.name], w=["igall"])
        self.decay_cols(4, True, 512.0 ** -0.5)
        YIN = self.A[0]

        def post(lb, hh):
            t0 = lb * 128
            op = self.t2k.next()
            kb.dma(op[:, :2048], OP[t0:t0 + 128, :], r=kT("B0", t0, t0 + 128, 0, 2048), w=[op.name])
            kb.op('act', lambda: nc.scalar.activation(out=op[:, :2048], in_=op[:, :2048], func=AF.Sigmoid), r=[op.name], w=[op.name])
            kb.op('dve', lambda: nc.vector.tensor_tensor(out=hh[:, :2048], in0=hh[:, :2048], in1=op[:, :2048], op=ALU.mult),
                  r=[hh.name, op.name], w=[hh.name])
            self.head_norm_tok(hh, 4, 512, LN_EPS, sq=op)
            hT = self.t2k.next()
            hv = self.transpose_to(hh, 16, hT)
            xc = self.t2k.next()
            zt = self.t2k.next()
            xv = xc[:, :2048].rearrange("p (k t) -> p k t", k=16)
            zv = zt[:, :2048].rearrange("p (k t) -> p k t", k=16)
            kb.dma(xv, XC[:, t0:t0 + 128].rearrange("(k p) t -> p k t", p=128), r=kF("A2", 0, 2048, t0, t0 + 128), w=[xc.name])
            kb.dma(zv, ZT[:, t0:t0 + 128].rearrange("(k p) t -> p k t", p=128), r=kF("A1", 0, 2048, t0, t0 + 128), w=[zt.name])
            ng = par[:, 96:112].unsqueeze(2).to_broadcast([128, 16, 128])
            sk = par[:, 112:128].unsqueeze(2).to_broadcast([128, 16, 128])
            kb.op('dve', lambda: nc.vector.tensor_tensor(out=hv, in0=hv, in1=ng, op=ALU.mult), r=[hT.name, "par"], w=[hT.name])
            kb.op('dve', lambda: nc.vector.tensor_tensor(out=xv, in0=xv, in1=sk, op=ALU.mult), r=[xc.name, "par"], w=[xc.name])
            kb.op('dve', lambda: nc.vector.tensor_tensor(out=hv, in0=hv, in1=xv, op=ALU.add), r=[hT.name, xc.name], w=[hT.name])
            kb.op('act', lambda: nc.scalar.activation(out=zv, in_=zv, func=AF.Silu), r=[zt.name], w=[zt.name])
            kb.op('dve', lambda: nc.vector.tensor_tensor(out=hv, in0=hv, in1=zv, op=ALU.mult), r=[hT.name, zt.name], w=[hT.name])
            kb.dma(YIN[:, t0:t0 + 128].rearrange("(k p) t -> p k t", p=128), hv, r=[hT.name], w=kF("A0", 0, 2048, t0, t0 + 128))

        lbs = range(NCC, NCH) if last else None
        self.quad(4, 1, 4, 512, True, QT, "A3", 0, KT, "A4", 0, V, "B1", post, lbs=lbs)
        self.proj(YIN, "A0", 2048, w['w_down'], [(0, D, 'F', self.epiF(self.YT, "YT", 0))], include_ctx=not last)

    def retention(self, li, b, last):
        nc, kb = self.nc, self.kb
        w = self.W[li]
        par = self.par
        NCH, NCC = self.NCH, self.NCC
        QT, KT, GT, YIN = self.A[0], self.A[1], self.A[2], self.A[3]
        V = self.Bt[0]
        pm = (li, 1, 0, b)
        self.proj_rope(self.HT, "HT", w['w_in'], 0, 1024, QT, "A0", pm, 256)
        self.proj_rope(self.HT, "HT", w['w_in'], 1024, 1024, KT, "A1", pm, 256)
        self.proj(self.HT, "HT", D, w['w_in'], [
            (2048, 4096, 'T', self.epiT(V, "B0", 2048)),
            (4096, 6144, 'F', self.epiF(GT, "A2", 32))], premod=pm)
        kb.op('dve', lambda: nc.vector.tensor_copy(out=self.ldall[:, :, 0:8],
                                                   in_=par[:, 0:8].unsqueeze(1).to_broadcast([128, NCH, 8])),
              r=["par"], w=["ldall"])
        self.decay_cols(4, False, 256.0 ** -0.5)

        def post(lb, hh):
            t0 = lb * 128
            self.head_norm_tok(hh, 4, 512, LN_EPS)
            hT = self.t2k.next()
            hv = self.transpose_to(hh, 16, hT)
            gt = self.t2k.next()
            gv = gt[:, :2048].rearrange("p (k t) -> p k t", k=16)
            kb.dma(gv, GT[:, t0:t0 + 128].rearrange("(k p) t -> p k t", p=128), r=kF("A2", 0, 2048, t0, t0 + 128), w=[gt.name])
            ng = par[:, 16:32].unsqueeze(2).to_broadcast([128, 16, 128])
            kb.op('dve', lambda: nc.vector.tensor_tensor(out=hv, in0=hv, in1=ng, op=ALU.mult), r=[hT.name, "par"], w=[hT.name])
            kb.op('act', lambda: nc.scalar.activation(out=gv, in_=gv, func=AF.Silu), r=[gt.name], w=[gt.name])
            kb.op('dve', lambda: nc.vector.tensor_tensor(out=hv, in0=hv, in1=gv, op=ALU.mult), r=[hT.name, gt.name], w=[hT.name])
            kb.dma(YIN[:, t0:t0 + 128].rearrange("(k p) t -> p k t", p=128), hv, r=[hT.name], w=kF("A3", 0, 2048, t0, t0 + 128))

        lbs = range(NCC, NCH) if last else None
        self.quad(4, 1, 2, 512, False, QT, "A0", 0, KT, "A1", 0, V, "B0", post, lbs=lbs)
        self.proj(YIN, "A3", 2048, w['w_out'], [(0, D, 'F', self.epiF(self.YT, "YT", 0))], include_ctx=not last)

    def ssd(self, li, b, last):
        nc, kb = self.nc, self.kb
        w = self.W[li]
        par = self.par
        NCH, NCC = self.NCH, self.NCC
        XBC, XC = self.A4k, self.XC4k
        ZTOK, XS = self.Bt[0], self.Bt[1]
        pm = (li, 1, 0, b)

        def epi_dt(pv, bk, c, bw, tok0):
            tb = tok0 // 128
            tn = self.tiny.next()
            kb.op('dve', lambda: nc.vector.tensor_tensor(out=tn[:, 0:64], in0=pv, in1=par[:, 192:256], op=ALU.add), r=[bk, "par"], w=[tn.name])
            kb.op('act', lambda: nc.scalar.activation(out=tn[:, 0:64], in_=tn[:, 0:64], func=AF.Exp), r=[tn.name], w=[tn.name])
            kb.op('act', lambda: nc.scalar.activation(out=tn[:, 0:64], in_=tn[:, 0:64], func=AF.Ln, bias=1.0, scale=1.0), r=[tn.name], w=[tn.name])
            kb.op('act', lambda: nc.scalar.activation(out=self.igall[:, tb, 0:64], in_=tn[:, 0:64], func=AF.Ln), r=[tn.name], w=["igall"])
            kb.op('dve', lambda: nc.vector.tensor_tensor(out=self.ldall[:, tb, 0:64], in0=tn[:, 0:64], in1=par[:, 256:320], op=ALU.mult),
                  r=[tn.name, "par"], w=["ldall"])

        self.proj(self.HT, "HT", D, w['w_in'], [
            (0, 2048, 'T', self.epiT(ZTOK, "B0", 0)),
            (2048, 6144, 'F', self.epiF(XBC, "A4k", 16)),
            (6144, 6208, 'T', epi_dt)], premod=pm)
        self.conv(XBC, "A4k", 32, lambda c, j: par[:, j * 32 + c:j * 32 + c + 1], lambda c: par[:, 160 + c:161 + c], XC, "XC4k")
        for tb in range(NCH):
            t0 = tb * 128
            xt = self.t2k.next()
            xv = xt[:, :2048].rearrange("p (k t) -> p k t", k=16)
            kb.dma(xv, XC[0:2048, t0:t0 + 128].rearrange("(k p) t -> p k t", p=128), r=kF("XC4k", 0, 2048, t0, t0 + 128), w=[xt.name])
            xo = self.t2k.next()
            for c0 in range(0, 16, 4):
                bi, bk = self.pbank()
                for j in range(4):
                    c = c0 + j
                    kb.op('pe', lambda c=c, j=j: nc.tensor.transpose(self.ps[:, bi, j * 128:(j + 1) * 128], xv[:, c, :], self.ident),
                          r=[xt.name, "cm"], w=[bk])
                self.evac(xo[:, c0 * 128:(c0 + 4) * 128], self.ps[:, bi, :], r=[bk], w=[xo.name])
            kb.dma(XS[t0:t0 + 128, :], xo[:, :2048], r=[xo.name], w=kT("B1", t0, t0 + 128, 0, 2048))
        self.decay_cols(32, True, 1.0)
        YIN = self.A[0]

        def post(lb, hh):
            t0 = lb * 128
            xs = self.t2k.next()
            zt = self.t2k.next()
            kb.dma(xs[:, :2048], XS[t0:t0 + 128, :], r=kT("B1", t0, t0 + 128, 0, 2048), w=[xs.name])
            kb.dma(zt[:, :2048], ZTOK[t0:t0 + 128, :], r=kT("B0", t0, t0 + 128, 0, 2048), w=[zt.name])
            x3 = xs[:, :2048].rearrange("p (h d) -> p h d", h=32)
            kb.op('dve', lambda: nc.vector.tensor_tensor(out=x3, in0=x3, in1=par[:, 320:352].unsqueeze(2).to_broadcast([128, 32, 64]),
                                                         op=ALU.mult), r=[xs.name, "par"], w=[xs.name])
            kb.op('dve', lambda: nc.vector.tensor_tensor(out=hh[:, :2048], in0=hh[:, :2048], in1=xs[:, :2048], op=ALU.add),
                  r=[hh.name, xs.name], w=[hh.name])
            kb.op('act', lambda: nc.scalar.activation(out=zt[:, :2048], in_=zt[:, :2048], func=AF.Silu), r=[zt.name], w=[zt.name])
            kb.op('dve', lambda: nc.vector.tensor_tensor(out=hh[:, :2048], in0=hh[:, :2048], in1=zt[:, :2048], op=ALU.mult),
                  r=[hh.name, zt.name], w=[hh.name])
            self.head_norm_tok(hh, 8, 256, RMS_EPS, center=False, sq=xs)
            self.transpose_store(hh, 16, YIN, "A0", 0, t0, scale_cols=par[:, 352:368])

        lbs = range(NCC, NCH) if last else None
        self.quad(8, 4, 1, 64, False, XC, "XC4k", 3072, XC, "XC4k", 2048, XS, "B1", post, lbs=lbs)
        self.proj(YIN, "A0", 2048, w['w_out'], [(0, D, 'F', self.epiF(self.YT, "YT", 0))], include_ctx=not last)

    def diffattn(self, li, b, last):
        nc, kb = self.nc, self.kb
        w = self.W[li]
        par = self.par
        T, NCH, NCC, CTX = self.T, self.NCH, self.NCC, self.CTX
        QT, KT, YIN = self.A[0], self.A[1], self.A[2]
        V, O = self.Bt[0], self.Bt[1]
        pm = (li, 1, 0, b)
        lam_init = 0.8 - 0.6 * math.exp(-0.3 * self.layer_ids[li])
        self.proj_rope(self.HT, "HT", w['w_qkv'], 0, 1024, QT, "A0", pm, 64)
        self.proj_rope(self.HT, "HT", w['w_qkv'], 1024, 1024, KT, "A1", pm, 64)
        self.proj(self.HT, "HT", D, w['w_qkv'], [(2048, 3072, 'T', self.epiT(V, "B0", 2048))], premod=pm)
        sc = 64.0 ** -0.5
        kt, vt = self.bigS[0], self.bigS[1]
        ptring = Ring(self.bigS[2:4])
        lbl = range(NCC, NCH) if last else range(NCH)
        for h in range(8):
            kb.dma(kt[:, :T], KT[h * 128:(h + 1) * 128, :], r=kF("A1", h * 128, h * 128 + 128, 0, T), w=[kt.name])
            vv = vt[:, :NCH * 128].rearrange("p (c e) -> p c e", c=NCH)
            kb.dma(vv, V[:, h * 128:(h + 1) * 128].rearrange("(c p) e -> p c e", p=128), r=kT("B0", 0, T, h * 128, h * 128 + 128), w=[vt.name])
            for lb in lbl:
                nk = CTX if lb < NCC else T
                nkb = nk // 128
                nb5 = (nk + 511) // 512
                qt = self.sm_q.next()
                kb.op('dve', lambda: nc.vector.memset(qt[:, 0:256], 0.0), w=[qt.name])
                kb.dma(qt[0:64, 0:128], QT[h * 128:h * 128 + 64, lb * 128:(lb + 1) * 128],
                       r=kF("A0", h * 128, h * 128 + 128, lb * 128, lb * 128 + 128) + [qt.name], w=[(qt.name, 0)])
                kb.dma(qt[64:128, 128:256], QT[h * 128 + 64:h * 128 + 128, lb * 128:(lb + 1) * 128],
                       r=kF("A0", h * 128, h * 128 + 128, lb * 128, lb * 128 + 128) + [qt.name], w=[(qt.name, 1)])
                ot = self.sm_k.next()
                for m in (0, 1):
                    for kbk in range(nb5):
                        n = min(512, nk - kbk * 512)
                        kb.op('pe', lambda kbk=kbk, n=n: nc.tensor.matmul(self.ps[:, kbk, 0:n], qt[:, m * 128:(m + 1) * 128],
                                                                            kt[:, kbk * 512:kbk * 512 + n], start=True, stop=True),
                              r=[qt.name, (qt.name, 0), (qt.name, 1), kt.name], w=["ps%d" % kbk])
                    tn = self.tiny.next()
                    for kbk in range(nb5):
                        n = min(512, nk - kbk * 512)
                        kb.op('dve', lambda kbk=kbk, n=n: nc.vector.reduce_max(out=tn[:, kbk:kbk + 1], in_=self.ps[:, kbk, 0:n], axis=AX.X),
                              r=["ps%d" % kbk], w=[tn.name])
                    kb.op('dve', lambda: nc.vector.reduce_max(out=tn[:, 8:9], in_=tn[:, 0:nb5], axis=AX.X), r=[tn.name], w=[tn.name])
                    kb.op('dve', lambda: nc.vector.tensor_scalar(out=tn[:, 9:10], in0=tn[:, 8:9], scalar1=-sc, scalar2=None, op0=ALU.mult),
                          r=[tn.name], w=[tn.name])
                    pt = ptring.next()
                    for kbk in range(nb5):
                        n = min(512, nk - kbk * 512)
                        kb.op('act', lambda kbk=kbk, n=n: nc.scalar.activation(out=pt[:, kbk * 512:kbk * 512 + n], in_=self.ps[:, kbk, 0:n],
                                                                                func=AF.Exp, bias=tn[:, 9:10], scale=sc),
                              r=["ps%d" % kbk, tn.name], w=[pt.name])
                    kb.op('dve', lambda: nc.vector.reduce_sum(out=tn[:, 10:11], in_=pt[:, 0:nk], axis=AX.X), r=[pt.name], w=[tn.name])
                    kb.op('dve', lambda: nc.vector.reciprocal(out=tn[:, 11:12], in_=tn[:, 10:11]), r=[tn.name], w=[tn.name])
                    if m == 1:
                        kb.op('dve', lambda: nc.vector.tensor_tensor(out=tn[:, 11:12], in0=tn[:, 11:12], in1=par[:, 400:401], op=ALU.mult),
                              r=[tn.name, "par"], w=[tn.name])
                    for s0 in range(0, nkb, 4):
                        n4 = min(4, nkb - s0)
                        tb_ = 5 + ((s0 // 4) % 2)
                        for j in range(n4):
                            kb.op('pe', lambda j=j: nc.tensor.transpose(self.ps[:, tb_, j * 128:(j + 1) * 128],
                                                                         pt[:, (s0 + j) * 128:(s0 + j + 1) * 128], self.ident),
                                  r=[pt.name, "cm"], w=["ps%d" % tb_])
                        ptt = self.sm_w.next()
                        self.evac(ptt[:, :n4 * 128], self.ps[:, tb_, :n4 * 128], r=["ps%d" % tb_], w=[ptt.name])
                        for j in range(n4):
                            sbk = s0 + j
                            kb.op('pe', lambda j=j, sbk=sbk: nc.tensor.matmul(self.ps[:, 7, 0:128], ptt[:, j * 128:(j + 1) * 128], vv[:, sbk, :],
                                                                               start=(sbk == 0), stop=(sbk == nkb - 1)),
                                  r=[ptt.name, vt.name], w=["ps7"])
                    if m == 0:
                        kb.op('dve', lambda: nc.vector.tensor_scalar(out=ot[:, 0:128], in0=self.ps[:, 7, 0:128], scalar1=tn[:, 11:12], scalar2=None,
                                                                     op0=ALU.mult), r=["ps7", tn.name], w=[ot.name])
                    else:
                        kb.op('dve', lambda: nc.vector.scalar_tensor_tensor(out=ot[:, 0:128], in0=self.ps[:, 7, 0:128], scalar=tn[:, 11:12],
                                                                             in1=ot[:, 0:128], op0=ALU.mult, op1=ALU.add),
                              r=["ps7", tn.name, ot.name], w=[ot.name])
                kb.dma(O[lb * 128:(lb + 1) * 128, h * 128:(h + 1) * 128], ot[:, 0:128], r=[ot.name],
                       w=kT("B1", lb * 128, lb * 128 + 128, h * 128, h * 128 + 128))
        for tb in lbl:
            t0 = tb * 128
            o = self.t1k.next()
            kb.dma(o[:, :D], O[t0:t0 + 128, 0:D], r=kT("B1", t0, t0 + 128, 0, D), w=[o.name])
            self.head_norm_tok(o, 8, 128, RMS_EPS, center=False)
            ov = o[:, :D].rearrange("p (h d) -> p h d", h=8)
            kb.op('dve', lambda: nc.vector.tensor_tensor(out=ov, in0=ov, in1=par[:, 256:384].unsqueeze(1).to_broadcast([128, 8, 128]),
                                                         op=ALU.mult), r=[o.name, "par"], w=[o.name])
            kb.op('dve', lambda: nc.vector.tensor_scalar(out=o[:, :D], in0=o[:, :D], scalar1=1.0 - lam_init, scalar2=None, op0=ALU.mult),
                  r=[o.name], w=[o.name])
            self.transpose_store(o, 8, YIN, "A2", 0, t0)
        self.proj(YIN, "A2", D, w['w_out'], [(0, D, 'F', self.epiF(self.YT, "YT", 0))], include_ctx=not last)

    def ln_feat(self, zt, zv, n, li, which):
        nc, kb = self.nc, self.kb
        bi, bk = self.pbank()
        mv = self.ps[:, bi, 0:n]
        for c in range(8):
            kb.op('pe', lambda c=c: nc.tensor.matmul(mv, self.onesm[:], zv[:, c, :], start=(c == 0), stop=(c == 7)),
                  r=["onesm", zt.name], w=[bk])
        kb.op('dve', lambda: nc.vector.tensor_tensor(out=zv, in0=zv, in1=mv.unsqueeze(1).to_broadcast([128, 8, n]), op=ALU.subtract),
              r=[zt.name, bk], w=[zt.name])
        sq = self.t1k.next()
        sv = sq[:, :8 * n].rearrange("p (c t) -> p c t", c=8)
        kb.op('dve', lambda: nc.vector.tensor_tensor(out=sv, in0=zv, in1=zv, op=ALU.mult), r=[zt.name], w=[sq.name])
        bi2, bk2 = self.pbank()
        vv = self.ps[:, bi2, 0:n]
        for c in range(8):
            kb.op('pe', lambda c=c: nc.tensor.matmul(vv, self.onesm[:], sv[:, c, :], start=(c == 0), stop=(c == 7)),
                  r=["onesm", sq.name], w=[bk2])
        rs = self.tiny.next()
        kb.op('dve', lambda: nc.vector.tensor_scalar(out=rs[:, :n], in0=vv, scalar1=LN_EPS, scalar2=None, op0=ALU.add), r=[bk2], w=[rs.name])
        kb.op('act', lambda: nc.scalar.activation(out=rs[:, :n], in_=rs[:, :n], func=AF.Sqrt), r=[rs.name], w=[rs.name])
        kb.op('dve', lambda: nc.vector.reciprocal(out=rs[:, :n], in_=rs[:, :n]), r=[rs.name], w=[rs.name])
        kb.op('dve', lambda: nc.vector.tensor_tensor(out=zv, in0=zv, in1=rs[:, :n].unsqueeze(1).to_broadcast([128, 8, n]), op=ALU.mult),
              r=[zt.name, rs.name], w=[zt.name])
        g = self.lng[:, li * 2 + which, :].unsqueeze(2).to_broadcast([128, 8, n])
        bb = self.lnb[:, li * 2 + which, :].unsqueeze(2).to_broadcast([128, 8, n])
        kb.op('dve', lambda: nc.vector.tensor_tensor(out=zv, in0=zv, in1=g, op=ALU.mult), r=[zt.name, ("lng", li)], w=[zt.name])
        kb.op('dve', lambda: nc.vector.tensor_tensor(out=zv, in0=zv, in1=bb, op=ALU.add), r=[zt.name, ("lnb", li)], w=[zt.name])

    def post(self, li, b, last):
        nc, kb = self.nc, self.kb
        w = self.W[li]
        NB, NCH, NCC = self.NB, self.NCH, self.NCC
        tbs = list(range(NCC, NCH)) if last else list(range(NCH))
        for tb in tbs:
            t0 = tb * 128
            col = NB if tb < NCC else b
            ht = self.t1k.next()
            yt = self.t1k.next()
            hv = ht[:, :1024].rearrange("p (c t) -> p c t", c=8)
            yv = yt[:, :1024].rearrange("p (c t) -> p c t", c=8)
            kb.dma(hv, self.HT[:, t0:t0 + 128].rearrange("(c p) t -> p c t", p=128), r=kF("HT", 0, D, t0, t0 + 128), w=[ht.name])
            kb.dma(yv, self.YT[:, t0:t0 + 128].rearrange("(c p) t -> p c t", p=128), r=kF("YT", 0, D, t0, t0 + 128), w=[yt.name])
            gm = self.modcol(li, 2, col).to_broadcast([128, 8, 128])
            kb.op('dve', lambda: nc.vector.tensor_tensor(out=yv, in0=yv, in1=gm, op=ALU.mult), r=[yt.name] + self.modkeys(li, 2), w=[yt.name])
            kb.op('dve', lambda: nc.vector.scalar_tensor_tensor(out=ht[:, :1024], in0=ht[:, :1024], scalar=ALPHA, in1=yt[:, :1024],
                                                                 op0=ALU.mult, op1=ALU.add), r=[ht.name, yt.name], w=[ht.name])
            self.ln_feat(ht, hv, 128, li, 0)
            kb.dma(self.H1T[:, t0:t0 + 128].rearrange("(c p) t -> p c t", p=128), hv, r=[ht.name], w=kF("H1T", 0, D, t0, t0 + 128))
        grp = self.groups(512, include_ctx=not last)
        self.proj(self.H1T, "H1T", D, w['wq'], [(0, 2048, 'F', self.epiF(self.QPT, "QPT", 0))], premod=(li, 4, 3, b), groups=grp)
        for tb in tbs:
            self.peer_tile(li, b, tb, last)

    def peer_tile(self, li, b, tb, last):
        nc, kb = self.nc, self.kb
        w = self.W[li]
        NB, CTX, NCC = self.NB, self.CTX, self.NCC
        t0 = tb * 128
        col = NB if tb < NCC else b
        h1 = self.t1k.next()
        h1v = h1[:, :1024].rearrange("p (c t) -> p c t", c=8)
        kb.dma(h1v, self.H1T[:, t0:t0 + 128].rearrange("(c p) t -> p c t", p=128), r=kF("H1T", 0, D, t0, t0 + 128), w=[h1.name])
        pT = self.t1k.next()
        pTv = pT[:, :1024].rearrange("p (c t) -> p c t", c=8)
        kb.op('dve', lambda: nc.vector.tensor_tensor(out=pTv, in0=h1v, in1=self.modcol(li, 4, col).to_broadcast([128, 8, 128]), op=ALU.mult),
              r=[h1.name] + self.modkeys(li, 4), w=[pT.name])
        kb.op('dve', lambda: nc.vector.tensor_tensor(out=pTv, in0=pTv, in1=self.modcol(li, 3, col).to_broadcast([128, 8, 128]), op=ALU.add),
              r=[pT.name] + self.modkeys(li, 3), w=[pT.name])
        ptok = self.t1k.next()
        for c0 in (0, 4):
            bi, bk = self.pbank()
            for j in range(4):
                kb.op('pe', lambda j=j: nc.tensor.transpose(self.ps[:, bi, j * 128:(j + 1) * 128], pTv[:, c0 + j, :], self.ident),
                      r=[pT.name, "cm"], w=[bk])
            self.evac(ptok[:, c0 * 128:(c0 + 4) * 128], self.ps[:, bi, :], r=[bk], w=[ptok.name])
        qt = self.t2k.next()
        qv = qt[:, :2048].rearrange("p (j t) -> p j t", j=16)
        kb.dma(qv, self.QPT[:, t0:t0 + 128].rearrange("(j p) t -> p j t", p=128), r=kF("QPT", 0, 2048, t0, t0 + 128), w=[qt.name])
        sc = self.t2k.next()
        scv = sc[:, :2048].rearrange("p (j n) -> p j n", j=16)
        for j0 in range(0, 16, 4):
            bi, bk = self.pbank()
            for j in range(4):
                kb.op('pe', lambda j=j: nc.tensor.matmul(self.ps[:, bi, j * 128:(j + 1) * 128], qv[:, j0 + j, :], self.keysT[:, j0 + j, :],
                                                          start=True, stop=True), r=[qt.name, "keysT"], w=[bk])
            self.evac(scv[:, j0:j0 + 4, :], self.ps[:, bi, :].rearrange("p (j n) -> p j n", j=4), r=[bk], w=[sc.name])
        sc2 = self.t2k.next()
        sc2v = sc2[:, :2048].rearrange("p (j n) -> p j n", j=16)
        tp, ti, bs, bj, ef, ei, gt, act = self.smt
        tiu = ti[:, 0:256].bitcast(U32)
        stop_ = tp[:, 0:256].rearrange("p (j k) -> p j k", j=16)
        for j in range(16):
            kb.op('dve', lambda j=j: nc.vector.max(out=stop_[:, j, 0:8], in_=scv[:, j, :]), r=[sc.name], w=[tp.name])
            kb.op('dve', lambda j=j: nc.vector.match_replace(out=sc2v[:, j, :], in_to_replace=stop_[:, j, 0:8], in_values=scv[:, j, :],
                                                              imm_value=NEG), r=[sc.name, tp.name], w=[sc2.name])
            kb.op('dve', lambda j=j: nc.vector.max(out=stop_[:, j, 8:16], in_=sc2v[:, j, :]), r=[sc2.name], w=[tp.name])
            kb.op('dve', lambda j=j: nc.vector.max_index(out=tiu[:, j * 16:j * 16 + 8], in_max=stop_[:, j, 0:8], in_values=scv[:, j, :]),
                  r=[sc.name, tp.name], w=[ti.name])
            kb.op('dve', lambda j=j: nc.vector.max_index(out=tiu[:, j * 16 + 8:j * 16 + 16], in_max=stop_[:, j, 8:16], in_values=sc2v[:, j, :]),
                  r=[sc2.name, tp.name], w=[ti.name])
        kb.op('dve', lambda: nc.vector.tensor_copy(out=tp[:, 256:512], in_=tiu), r=[ti.name], w=[tp.name])
        st4 = tp[:, 0:256].rearrange("p (h c k) -> p h c k", h=8, c=2)
        it4 = tp[:, 256:512].rearrange("p (h c k) -> p h c k", h=8, c=2)
        cand = self.t2k.next()
        cv = cand[:, :2048].rearrange("p (h a b) -> p h a b", h=8, a=16)
        kb.op('dve', lambda: nc.vector.tensor_tensor(out=cv, in0=st4[:, :, 0, :].unsqueeze(3).to_broadcast([128, 8, 16, 16]),
                                                     in1=st4[:, :, 1, :].unsqueeze(2).to_broadcast([128, 8, 16, 16]), op=ALU.add),
              r=[tp.name], w=[cand.name])
        cand2 = self.t2k.next()
        bju = bj[:, 0:128].bitcast(U32)
        bsv = bs[:, 0:128].rearrange("p (h k) -> p h k", h=8)
        c1 = cand[:, :2048].rearrange("p (h n) -> p h n", h=8)
        c2 = cand2[:, :2048].rearrange("p (h n) -> p h n", h=8)
        for h in range(8):
            kb.op('dve', lambda h=h: nc.vector.max(out=bsv[:, h, 0:8], in_=c1[:, h, :]), r=[cand.name], w=[bs.name])
            kb.op('dve', lambda h=h: nc.vector.match_replace(out=c2[:, h, :], in_to_replace=bsv[:, h, 0:8], in_values=c1[:, h, :],
                                                              imm_value=NEG), r=[cand.name, bs.name], w=[cand2.name])
            kb.op('dve', lambda h=h: nc.vector.max(out=bsv[:, h, 8:16], in_=c2[:, h, :]), r=[cand2.name], w=[bs.name])
            kb.op('dve', lambda h=h: nc.vector.max_index(out=bju[:, h * 16:h * 16 + 8], in_max=bsv[:, h, 0:8], in_values=c1[:, h, :]),
                  r=[cand.name, bs.name], w=[bj.name])
            kb.op('dve', lambda h=h: nc.vector.max_index(out=bju[:, h * 16 + 8:h * 16 + 16], in_max=bsv[:, h, 8:16], in_values=c2[:, h, :]),
                  r=[cand2.name, bs.name], w=[bj.name])
        bau = bj[:, 128:256].bitcast(U32)
        bbu = bj[:, 256:384].bitcast(U32)
        kb.op('dve', lambda: nc.vector.tensor_single_scalar(out=bau, in_=bju, scalar=4, op=ALU.logical_shift_right), r=[bj.name], w=[bj.name])
        kb.op('dve', lambda: nc.vector.tensor_single_scalar(out=bbu, in_=bju, scalar=15, op=ALU.bitwise_and), r=[bj.name], w=[bj.name])
        kb.op('dve', lambda: nc.vector.tensor_copy(out=bs[:, 256:384], in_=bau), r=[bj.name], w=[bs.name])
        kb.op('dve', lambda: nc.vector.tensor_copy(out=bs[:, 384:512], in_=bbu), r=[bj.name], w=[bs.name])
        oh = self.t2k.next()
        ohv = oh[:, :2048].rearrange("p (h k a) -> p h k a", h=8, k=16)
        i16b = self.i16[:, :].unsqueeze(1).unsqueeze(1).to_broadcast([128, 8, 16, 16])
        for which in (0, 1):
            ab = bs[:, 256 + which * 128:384 + which * 128].rearrange("p (h k) -> p h k", h=8)
            kb.op('dve', lambda ab=ab: nc.vector.tensor_tensor(out=ohv, in0=ab.unsqueeze(3).to_broadcast([128, 8, 16, 16]), in1=i16b,
                                                                op=ALU.is_equal), r=[bs.name, "i16"], w=[oh.name])
            kb.op('dve', lambda which=which: nc.vector.tensor_tensor(out=ohv, in0=ohv,
                                                                      in1=it4[:, :, which, :].unsqueeze(2).to_broadcast([128, 8, 16, 16]),
                                                                      op=ALU.mult), r=[oh.name, tp.name], w=[oh.name])
            kb.op('dve', lambda which=which: nc.vector.reduce_sum(out=ef[:, which * 128:(which + 1) * 128].rearrange("p (h k) -> p h k", h=8),
                                                                   in_=ohv, axis=AX.X), r=[oh.name], w=[ef.name])
        kb.op('dve', lambda: nc.vector.scalar_tensor_tensor(out=ef[:, 256:384], in0=ef[:, 0:128], scalar=128.0, in1=ef[:, 128:256],
                                                             op0=ALU.mult, op1=ALU.add), r=[ef.name], w=[ef.name])
        eiv = ei[:, 0:128].bitcast(I32)
        kb.op('dve', lambda: nc.vector.tensor_copy(out=eiv, in_=ef[:, 256:384]), r=[ef.name], w=[ei.name])
        gv = gt[:, 0:128].rearrange("p (h k) -> p h k", h=8)
        kb.op('dve', lambda: nc.vector.tensor_tensor(out=gv, in0=bsv, in1=bsv[:, :, 0:1].to_broadcast([128, 8, 16]), op=ALU.subtract),
              r=[bs.name], w=[gt.name])
        kb.op('act', lambda: nc.scalar.activation(out=gt[:, 0:128], in_=gt[:, 0:128], func=AF.Exp), r=[gt.name], w=[gt.name])
        kb.op('dve', lambda: nc.vector.reduce_sum(out=gt[:, 128:136], in_=gv, axis=AX.X), r=[gt.name], w=[gt.name])
        kb.op('dve', lambda: nc.vector.reciprocal(out=gt[:, 136:144], in_=gt[:, 128:136]), r=[gt.name], w=[gt.name])
        kb.op('dve', lambda: nc.vector.tensor_tensor(out=gv, in0=gv, in1=gt[:, 136:144].unsqueeze(2).to_broadcast([128, 8, 16]), op=ALU.mult),
              r=[gt.name], w=[gt.name])
        for g4 in range(32):
            ut = self.ringS.next()
            uv = ut[:, :4096].rearrange("p (k d) -> p k d", k=4)
            for i in range(4):
                k = g4 * 4 + i
                kb.gather(uv[:, i, :], w['u'], eiv[:, k:k + 1], r=[ei.name], w=([ut.name] if i == 0 else []) + [(ut.name, i)])
            for i in range(4):
                k = g4 * 4 + i
                kb.op('dve', lambda i=i, k=k: nc.vector.scalar_tensor_tensor(out=uv[:, i, :], in0=uv[:, i, :], scalar=1.0, in1=ptok[:, :1024],
                                                                              op0=ALU.mult, op1=ALU.mult, accum_out=act[:, k:k + 1]),
                      r=[(ut.name, i), ut.name, ptok.name], w=[act.name, (ut.name, i)])
        a0 = act[:, 0:128]
        a1 = act[:, 128:256]
        kb.op('dve', lambda: nc.vector.tensor_tensor(out=a1, in0=a0, in1=a0, op=ALU.mult), r=[act.name], w=[act.name])
        kb.op('dve', lambda: nc.vector.tensor_scalar(out=a1, in0=a1, scalar1=0.044715, scalar2=1.0, op0=ALU.mult, op1=ALU.add),
              r=[act.name], w=[act.name])
        kb.op('dve', lambda: nc.vector.tensor_tensor(out=a1, in0=a1, in1=a0, op=ALU.mult), r=[act.name], w=[act.name])
        kb.op('act', lambda: nc.scalar.activation(out=a1, in_=a1, func=AF.Sigmoid, scale=2.0 * math.sqrt(2.0 / math.pi)), r=[act.name], w=[act.name])
        kb.op('dve', lambda: nc.vector.tensor_tensor(out=a1, in0=a1, in1=a0, op=ALU.mult), r=[act.name], w=[act.name])
        kb.op('dve', lambda: nc.vector.tensor_tensor(out=act[:, 256:384], in0=a1, in1=gt[:, 0:128], op=ALU.mult), r=[act.name, gt.name], w=[act.name])
        wts = act[:, 256:384]
        ft = self.t1k.next()
        for g4 in range(32):
            vt = self.ringS.next()
            vv = vt[:, :4096].rearrange("p (k d) -> p k d", k=4)
            for i in range(4):
                k = g4 * 4 + i
                kb.gather(vv[:, i, :], w['v'], eiv[:, k:k + 1], r=[ei.name], w=([vt.name] if i == 0 else []) + [(vt.name, i)])
            for i in range(4):
                k = g4 * 4 + i
                if k == 0:
                    kb.op('dve', lambda: nc.vector.tensor_scalar(out=ft[:, :1024], in0=vv[:, 0, :], scalar1=wts[:, 0:1], scalar2=None,
                                                                 op0=ALU.mult), r=[(vt.name, 0), vt.name, act.name], w=[ft.name])
                else:
                    kb.op('dve', lambda i=i, k=k: nc.vector.scalar_tensor_tensor(out=ft[:, :1024], in0=vv[:, i, :], scalar=wts[:, k:k + 1],
                                                                                  in1=ft[:, :1024], op0=ALU.mult, op1=ALU.add),
                          r=[(vt.name, i), vt.name, act.name, ft.name], w=[ft.name])
        fT = self.t1k.next()
        fv = self.transpose_to(ft, 8, fT)
        kb.op('dve', lambda: nc.vector.tensor_tensor(out=fv, in0=fv, in1=self.modcol(li, 5, col).to_broadcast([128, 8, 128]), op=ALU.mult),
              r=[fT.name] + self.modkeys(li, 5), w=[fT.name])
        kb.op('dve', lambda: nc.vector.scalar_tensor_tensor(out=fT[:, :1024], in0=h1[:, :1024], scalar=ALPHA, in1=fT[:, :1024],
                                                             op0=ALU.mult, op1=ALU.add), r=[h1.name, fT.name], w=[fT.name])
        self.ln_feat(fT, fv, 128, li, 1)
        if last:
            if tb >= NCC:
                ot = self.t1k.next()
                for c0 in (0, 4):
                    bi, bk = self.pbank()
                    for j in range(4):
                        kb.op('pe', lambda j=j: nc.tensor.transpose(self.ps[:, bi, j * 128:(j + 1) * 128], fv[:, c0 + j, :], self.ident),
                              r=[fT.name, "cm"], w=[bk])
                    self.evac(ot[:, c0 * 128:(c0 + 4) * 128], self.ps[:, bi, :], r=[bk], w=[ot.name])
                s0 = t0 - CTX
                kb.dma(self.out[b, s0:s0 + 128, :], ot[:, :1024], r=[ot.name], w=[("out", b, tb)])
        else:
            kb.dma(self.HT[:, t0:t0 + 128].rearrange("(c p) t -> p c t", p=128), fv, r=[fT.name], w=kF("HT", 0, D, t0, t0 + 128))


_CACHE = {}


def layer_weight_maps(inputs, kinds, layer_ids):
    m = {}
    cnt = {0: 0, 1: 0, 2: 0, 3: 0}
    for li, (kind, lid) in enumerate(zip(kinds, layer_ids)):
        p = "L%d_" % li
        j = lid // 4
        f = lambda a: np.ascontiguousarray(np.asarray(a, dtype=np.float32))
        m[p + "ada_w"] = f(inputs['ada_w'][lid])
        m[p + "ada_b"] = f(inputs['ada_b'][lid])
        m[p + "ln_g"] = f(inputs['ln_g'][lid])
        m[p + "ln_b"] = f(inputs['ln_b'][lid])
        m[p + "peer_wq"] = f(inputs['peer_wq'][lid])
        m[p + "peer_keys"] = f(inputs['peer_keys'][lid]).reshape(16, 128, 128)
        m[p + "peer_u"] = f(inputs['peer_u'][lid])
        m[p + "peer_v"] = f(inputs['peer_v'][lid])
        if kind == 0:
            m[p + "w_up"] = f(inputs['mlstm_w_up'][j])
            m[p + "conv_w"] = f(inputs['mlstm_conv_w'][j])
            m[p + "conv_b"] = f(inputs['mlstm_conv_b'][j])
            m[p + "w_qk"] = f(inputs['mlstm_w_qk'][j])
            m[p + "w_v"] = f(inputs['mlstm_w_v'][j])
            m[p + "w_gate"] = f(inputs['mlstm_w_gate'][j])
            m[p + "b_gate"] = f(inputs['mlstm_b_gate'][j])
            m[p + "norm_g"] = f(inputs['mlstm_norm_g'][j])
            m[p + "skip"] = f(inputs['mlstm_skip'][j])
            m[p + "w_down"] = f(inputs['mlstm_w_down'][j])
        elif kind == 1:
            m[p + "w_in"] = f(inputs['ssd_w_in'][j])
            m[p + "conv_w"] = f(inputs['ssd_conv_w'][j])
            m[p + "conv_b"] = f(inputs['ssd_conv_b'][j])
            m[p + "dt_bias"] = f(inputs['ssd_dt_bias'][j]).reshape(64)
            m[p + "a_log"] = f(inputs['ssd_a_log'][j]).reshape(64)
            m[p + "d"] = f(inputs['ssd_d'][j]).reshape(32)
            m[p + "norm_g"] = f(inputs['ssd_norm_g'][j])
            m[p + "w_out"] = f(inputs['ssd_w_out'][j])
        elif kind == 2:
            m[p + "w_qkv"] = f(inputs['diff_w_qkv'][j])
            m[p + "lam"] = f(inputs['diff_lambda'][j])
            m[p + "norm_g"] = f(inputs['diff_norm_g'][j])
            m[p + "w_out"] = f(inputs['diff_w_out'][j])
        else:
            m[p + "w_in"] = f(inputs['ret_w_in'][j])
            m[p + "decay"] = f(inputs['ret_decay_logit'][j]).reshape(8)
            m[p + "norm_g"] = f(inputs['ret_norm_g'][j])
            m[p + "w_out"] = f(inputs['ret_w_out'][j])
    return m


def run(inputs, NB, n_cores, kinds, layer_ids, final_last=True, trace=False):
    x = np.asarray(inputs['x'], np.float32)
    cx = np.asarray(inputs['ctx'], np.float32)
    c = np.asarray(inputs['c'], np.float32)
    c_ctx = np.asarray(inputs['c_ctx'], np.float32)
    B, SEQ, _ = x.shape
    CTX = cx.shape[1]
    assert B == NB * n_cores
    key = (NB, CTX, SEQ, tuple(kinds), tuple(layer_ids), final_last)
    if key not in _CACHE:
        _CACHE[key] = Prog(NB, CTX, SEQ, kinds, layer_ids, final_last)
    prog = _CACHE[key]
    consts = make_consts(SEQ)
    wm = layer_weight_maps(inputs, kinds, layer_ids)
    in_maps = []
    for ci in range(n_cores):
        sl = slice(ci * NB, (ci + 1) * NB)
        m = dict(wm)
        m.update(consts)
        m['x'] = np.ascontiguousarray(x[sl])
        m['ctx'] = np.ascontiguousarray(cx[sl])
        m['cT'] = np.ascontiguousarray(np.concatenate([c[sl].T, c_ctx[:, None]], axis=1))
        in_maps.append(m)
    res = run_bass_kernel_spmd(prog.nc, in_maps, core_ids=list(range(n_cores)), trace=trace)
    out = np.concatenate([np.asarray(r["out"]) for r in res.results], axis=0)
    return out.astype(np.float32), res


N_LAUNCH = 4


def kernel(**inputs):
    if N_LAUNCH == 1:
        out, _ = run(inputs, 4, 8, [0, 1, 2, 3], [0, 1, 2, 3], True)
        return out
    x = np.asarray(inputs['x'])
    outs = []
    for i in range(4):
        sub = dict(inputs)
        sl = slice(i * 8, (i + 1) * 8)
        sub['x'] = x[sl]
        sub['ctx'] = np.asarray(inputs['ctx'])[sl]
        sub['c'] = np.asarray(inputs['c'])[sl]
        o, _ = run(sub, 1, 8, [0, 1, 2, 3], [0, 1, 2, 3], True)
        outs.append(o)
    return np.concatenate(outs, axis=0)
```

```python
import math
from contextlib import ExitStack
import numpy as np
import concourse.bass as bass
import concourse.mybir as mybir
from concourse.bass_utils import run_bass_kernel_spmd

F32 = mybir.dt.float32
I32 = mybir.dt.int32
U32 = mybir.dt.uint32
AF = mybir.ActivationFunctionType
ALU = mybir.AluOpType
AX = mybir.AxisListType

D = 1024
ALPHA = (2.0 * 4) ** 0.25
LN_EPS = 1e-5
RMS_EPS = 1e-6
NEG = -1.0e30
GRID_W = 64
ROPE_BASE = 10000.0


class KB:
    def __init__(self, nc, es, n_slots=72):
        self.nc = nc
        self.es = es
        self.engs = {'pe': nc.tensor, 'act': nc.scalar, 'dve': nc.vector, 'pool': nc.gpsimd, 'sp': nc.sync}
        self.sem = {e: es.enter_context(nc.semaphore("s_" + e)) for e in self.engs}
        self.cnt = {e: 0 for e in self.engs}
        self.dsem = [es.enter_context(nc.semaphore("d%d" % i)) for i in range(n_slots)]
        self.dval = [0] * n_slots
        self.dnext = 0
        self.seen = {e: {} for e in self.engs}
        self.lastw = {}
        self.readers = {}
        self.nins = 0

    def _wait(self, eng, tok):
        sk, v = tok
        if eng == 'pe' and sk == ('e', 'pe'):
            return
        if self.seen[eng].get(sk, 0) >= v:
            return
        sem = self.sem[sk[1]] if sk[0] == 'e' else self.dsem[sk[1]]
        self.engs[eng].wait_ge(sem, v)
        self.seen[eng][sk] = v

    def _deps(self, eng, r, w):
        for k in r:
            t = self.lastw.get(k)
            if t is not None:
                self._wait(eng, t)
        for k in w:
            t = self.lastw.get(k)
            if t is not None:
                self._wait(eng, t)
            rd = self.readers.get(k)
            if rd:
                for sk, v in rd.items():
                    self._wait(eng, (sk, v))

    def _commit(self, tok, r, w):
        sk, v = tok
        for k in r:
            d = self.readers.get(k)
            if d is None:
                d = {}
                self.readers[k] = d
            if d.get(sk, 0) < v:
                d[sk] = v
        for k in w:
            self.lastw[k] = tok
            self.readers[k] = {}

    def op(self, eng, fn, r=(), w=()):
        self._deps(eng, r, w)
        ins = fn()
        self.cnt[eng] += 1
        ins.then_inc(self.sem[eng], 1)
        self._commit((('e', eng), self.cnt[eng]), r, w)
        self.nins += 1

    def dma(self, out, in_, r=(), w=(), q='sp', **kw):
        s = self.dnext
        self.dnext = (s + 1) % len(self.dsem)
        if self.dval[s] > 0:
            self._wait(q, (('d', s), self.dval[s]))
        self._deps(q, r, w)
        ins = self.engs[q].dma_start(out=out, in_=in_, **kw)
        self.dval[s] += 16
        ins.then_inc(self.dsem[s], 16)
        self._commit((('d', s), self.dval[s]), r, w)
        self.nins += 1

    def gather(self, out, table, idx_ap, r=(), w=()):
        q = 'pool'
        s = self.dnext
        self.dnext = (s + 1) % len(self.dsem)
        if self.dval[s] > 0:
            self._wait(q, (('d', s), self.dval[s]))
        self._deps(q, r, w)
        ins = self.nc.gpsimd.indirect_dma_start(
            out=out, out_offset=None, in_=table,
            in_offset=bass.IndirectOffsetOnAxis(ap=idx_ap, axis=0))
        self.dval[s] += 16
        ins.then_inc(self.dsem[s], 16)
        self._commit((('d', s), self.dval[s]), r, w)
        self.nins += 1

    def drain(self):
        for s in range(len(self.dsem)):
            if self.dval[s] > 0:
                self._wait('sp', (('d', s), self.dval[s]))
        for e in ('pe', 'act', 'dve', 'pool'):
            if self.cnt[e] > 0:
                self._wait('sp', (('e', e), self.cnt[e]))


class Ring:
    def __init__(self, tiles):
        self.tiles = tiles
        self.i = 0

    def next(self):
        t = self.tiles[self.i]
        self.i = (self.i + 1) % len(self.tiles)
        return t


def kF(name, r0, r1, t0, t1):
    return [(name, rc, tb) for rc in range(r0 // 128, (r1 + 127) // 128) for tb in range(t0 // 128, (t1 + 127) // 128)]


def kT(name, t0, t1, c0, c1):
    return [(name, tb, cb) for tb in range(t0 // 128, (t1 + 127) // 128) for cb in range(c0 // 128, (c1 + 127) // 128)]


def rope_tables(d, seq):
    h = d // 2
    q = h // 2
    t = np.arange(seq)
    row = (t // GRID_W).astype(np.float32)
    col = (t % GRID_W).astype(np.float32)
    freqs = (np.float32(ROPE_BASE) ** (-np.arange(q, dtype=np.float32) / np.float32(q))).astype(np.float32)
    cos = np.zeros((d, seq), np.float32)
    sin = np.zeros((d, seq), np.float32)
    for f in range(d):
        pos = row if f < h else col
        i = f % q
        ang = (pos * freqs[i]).astype(np.float32)
        sgn = -1.0 if (f % h) < q else 1.0
        cos[f] = np.cos(ang).astype(np.float32)
        sin[f] = (sgn * np.sin(ang)).astype(np.float32)
    return cos, sin


def make_consts(seq):
    ident = np.eye(128, dtype=np.float32)
    s = np.arange(128)[:, None]
    l = np.arange(128)[None, :]
    trif = (s <= l).astype(np.float32)
    trib = (s >= l).astype(np.float32)
    mnf = np.where(s <= l, 0.0, NEG).astype(np.float32)
    mnb = np.where(s >= l, 0.0, NEG).astype(np.float32)
    ones = np.ones((128, 128), np.float32)
    cm = np.stack([ident, trif, trib, mnf, mnb, ones], 0)
    c256, s256 = rope_tables(256, seq)
    c64, s64 = rope_tables(64, seq)
    rope_ret = np.stack([c256.reshape(2, 128, seq), s256.reshape(2, 128, seq)], 0)
    rope_dif = np.stack([np.concatenate([c64, c64], 0), np.concatenate([s64, s64], 0)], 0)
    iota16 = np.tile(np.arange(16, dtype=np.float32)[None, :], (128, 1))
    return dict(cm=cm, rope_ret=np.ascontiguousarray(rope_ret), rope_dif=np.ascontiguousarray(rope_dif), iota16=iota16)


class Prog:
    def __init__(self, NB, CTX, SEQ, kinds, layer_ids=None, final_last=True):
        self.NB, self.CTX, self.SEQ = NB, CTX, SEQ
        self.kinds = list(kinds)
        self.L = len(kinds)
        self.layer_ids = list(layer_ids) if layer_ids is not None else list(range(self.L))
        self.final_last = final_last
        self.T = CTX + SEQ
        self.NCH = self.T // 128
        self.NCC = CTX // 128
        self.nc = bass.Bass("TRN2", target_bir_lowering=False)
        self.es = ExitStack()
        self.inputs = {}
        self.build()

    def din(self, name, shape, dtype=F32):
        ap = self.nc.dram_tensor(name, list(shape), dtype, kind="ExternalInput").ap()
        self.inputs[name] = ap
        return ap

    def dscr(self, name, shape, dtype=F32):
        return self.nc.dram_tensor(name, list(shape), dtype, kind="Internal").ap()

    def sb(self, name, shape, dtype=F32):
        return self.es.enter_context(self.nc.sbuf_tensor("sb_" + name, list(shape), dtype))

    def groups(self, tgmax, include_ctx=True):
        gs = []
        if include_ctx:
            t = 0
            while t < self.CTX:
                g = min(tgmax, self.CTX - t)
                gs.append((t, g))
                t += g
        t = self.CTX
        while t < self.T:
            g = min(tgmax, self.T - t)
            gs.append((t, g))
            t += g
        return gs

    def build(self):
        nc, es = self.nc, self.es
        NB, CTX, SEQ, T, NCH = self.NB, self.CTX, self.SEQ, self.T, self.NCH
        kb = KB(nc, es)
        self.kb = kb
        self.x = self.din("x", [NB, SEQ, D])
        self.cx = self.din("ctx", [NB, CTX, D])
        self.cT = self.din("cT", [D, NB + 1])
        self.out = nc.dram_tensor("out", [NB, SEQ, D], F32, kind="ExternalOutput").ap()
        self.c_cm = self.din("cm", [6, 128, 128])
        self.c_rr = self.din("rope_ret", [2, 2, 128, SEQ])
        self.c_rd = self.din("rope_dif", [2, 128, SEQ])
        self.c_i16 = self.din("iota16", [128, 16])
        W = []
        for li, kind in enumerate(self.kinds):
            w = {}
            p = "L%d_" % li
            w['ada_w'] = self.din(p + "ada_w", [D, 6 * D])
            w['ada_b'] = self.din(p + "ada_b", [6 * D])
            w['ln_g'] = self.din(p + "ln_g", [2, D])
            w['ln_b'] = self.din(p + "ln_b", [2, D])
            w['wq'] = self.din(p + "peer_wq", [D, 2048])
            w['keys'] = self.din(p + "peer_keys", [16, 128, 128])
            w['u'] = self.din(p + "peer_u", [16384, D])
            w['v'] = self.din(p + "peer_v", [16384, D])
            if kind == 0:
                w['w_up'] = self.din(p + "w_up", [D, 6144])
                w['conv_w'] = self.din(p + "conv_w", [5, 2048])
                w['conv_b'] = self.din(p + "conv_b", [2048])
                w['w_qk'] = self.din(p + "w_qk", [2048, 4096])
                w['w_v'] = self.din(p + "w_v", [2048, 2048])
                w['w_gate'] = self.din(p + "w_gate", [6144, 16])
                w['b_gate'] = self.din(p + "b_gate", [16])
                w['norm_g'] = self.din(p + "norm_g", [2048])
                w['skip'] = self.din(p + "skip", [2048])
                w['w_down'] = self.din(p + "w_down", [2048, D])
            elif kind == 1:
                w['w_in'] = self.din(p + "w_in", [D, 6208])
                w['conv_w'] = self.din(p + "conv_w", [5, 4096])
                w['conv_b'] = self.din(p + "conv_b", [4096])
                w['dt_bias'] = self.din(p + "dt_bias", [64])
                w['a_log'] = self.din(p + "a_log", [64])
                w['d'] = self.din(p + "d", [32])
                w['norm_g'] = self.din(p + "norm_g", [2048])
                w['w_out'] = self.din(p + "w_out", [2048, D])
            elif kind == 2:
                w['w_qkv'] = self.din(p + "w_qkv", [D, 3072])
                w['lam'] = self.din(p + "lam", [4, 64])
                w['norm_g'] = self.din(p + "norm_g", [128])
                w['w_out'] = self.din(p + "w_out", [D, D])
            else:
                w['w_in'] = self.din(p + "w_in", [D, 6144])
                w['decay'] = self.din(p + "decay", [8])
                w['norm_g'] = self.din(p + "norm_g", [2048])
                w['w_out'] = self.din(p + "w_out", [2048, D])
            W.append(w)
        self.W = W
        self.HT = self.dscr("HT", [D, T])
        self.YT = self.dscr("YT", [D, T])
        self.H1T = self.dscr("H1T", [D, T])
        self.QPT = self.dscr("QPT", [2048, T])
        self.A = [self.dscr("A%d" % i, [2048, T]) for i in range(5)]
        self.A4k = self.dscr("A4k", [4096, T])
        self.Bt = [self.dscr("B%d" % i, [T, 2048]) for i in range(3)]
        self.XC4k = self.dscr("XC4k", [4096, T])
        self.cm = self.sb("cm", [128, 6, 128])
        self.ident = self.cm[:, 0, :]
        self.trif = self.cm[:, 1, :]
        self.trib = self.cm[:, 2, :]
        self.mnf = self.cm[:, 3, :]
        self.mnb = self.cm[:, 4, :]
        self.ones = self.cm[:, 5, :]
        self.onesm = self.sb("onesm", [128, 128])
        self.i16 = self.sb("i16", [128, 16])
        self.mod = self.sb("mod", [128, self.L * 48, NB + 1])
        self.lng = self.sb("lng", [128, self.L * 2, 8])
        self.lnb = self.sb("lnb", [128, self.L * 2, 8])
        self.scT = self.sb("scT", [128, 8, NB + 1])
        self.adab = self.sb("adab", [128, 48])
        self.ringL = Ring([self.sb("bigL%d" % i, [128, 4096]) for i in range(1)])
        self.bigS = [self.sb("bigS%d" % i, [128, 4096]) for i in range(4)]
        self.ringS = Ring(self.bigS)
        self.stg = Ring([self.sb("stg%d" % i, [128, 512]) for i in range(3)])
        self.smt = [self.sb("sm%d" % i, [128, 512]) for i in range(8)]
        self.sm_q = Ring(self.smt[0:2])
        self.sm_k = Ring(self.smt[2:4])
        self.sm_v = Ring(self.smt[4:6])
        self.sm_w = Ring(self.smt[6:8])
        self.t2k = Ring([self.sb("t2k%d" % i, [128, 2048]) for i in range(5)])
        self.t1k = Ring([self.sb("t1k%d" % i, [128, 1024]) for i in range(5)])
        self.tiny = Ring([self.sb("tiny%d" % i, [128, 128]) for i in range(12)])
        self.par = self.sb("par", [128, 1100])
        self.fcol = self.sb("fcol", [128, NCH, 64])
        self.ldall = self.sb("ldall", [128, NCH, 64])
        self.igall = self.sb("igall", [128, NCH, 64])
        self.ccol = self.igall
        self.keysT = self.sb("keysT", [128, 16, 128])
        self.ps = es.enter_context(nc.psum_tensor("ps", [128, 8, 512], F32))
        self.psr = Ring([4, 5, 6, 7])
        self._alt = 0

        kb.dma(self.cm[:], self.c_cm.rearrange("k p n -> p k n"), w=["cm"])
        kb.dma(self.i16[:], self.c_i16, w=["i16"])
        kb.op('dve', lambda: nc.vector.memset(self.onesm[:], 1.0 / 1024.0), w=["onesm"])
        self.preamble_mod()
        for b in range(NB):
            self.load_input(b)
            for li, kind in enumerate(self.kinds):
                last = self.final_last and (li == self.L - 1)
                if b == 0 or True:
                    self.load_layer_params(li)
                [self.mlstm, self.ssd, self.diffattn, self.retention][kind](li, b, last)
                self.post(li, b, last)
        kb.drain()
        self.es.close()

    def pbank(self):
        i = self.psr.next()
        return i, "ps%d" % i

    def evac(self, out, in_, r, w):
        nc = self.nc
        if True:
            self.kb.op('act', lambda: nc.scalar.copy(out=out, in_=in_), r=r, w=w)
        else:
            self.kb.op('dve', lambda: nc.vector.tensor_copy(out=out, in_=in_), r=r, w=w)

    def modcol(self, li, j, b):
        return self.mod[:, li * 48 + j * 8: li * 48 + j * 8 + 8, b:b + 1]

    def modkeys(self, li, j):
        return [("mod", li, j * 8 + c) for c in range(8)]

    def preamble_mod(self):
        nc, kb = self.nc, self.kb
        NB = self.NB
        tmp = self.smt[0]
        tv = tmp[:, :8 * (NB + 1)].rearrange("p (c b) -> p c b", c=8)
        kb.dma(tv, self.cT.rearrange("(c p) b -> p c b", p=128), w=[tmp.name])
        kb.op('act', lambda: nc.scalar.activation(out=self.scT[:], in_=tv, func=AF.Silu), r=[tmp.name], w=["scT"])
        for li in range(self.L):
            w = self.W[li]
            kb.dma(self.adab[:], w['ada_b'].rearrange("(c p) -> p c", p=128), w=["adab"], allow_slow_non_contiguous=True)
            kb.dma(self.lng[:, li * 2:li * 2 + 2, :], w['ln_g'].rearrange("a (c p) -> p a c", p=128), w=[("lng", li)],
                   allow_slow_non_contiguous=True)
            kb.dma(self.lnb[:, li * 2:li * 2 + 2, :], w['ln_b'].rearrange("a (c p) -> p a c", p=128), w=[("lnb", li)],
                   allow_slow_non_contiguous=True)
            for nb in range(12):
                wt = self.ringS.next()
                wv = wt[:, :4096].rearrange("p (k n) -> p k n", k=8)
                kb.dma(wv, w['ada_w'].rearrange("(k p) n -> p k n", p=128)[:, :, nb * 512:(nb + 1) * 512], w=[wt.name])
                for sub in range(4):
                    ch = nb * 4 + sub
                    bi, bk = self.pbank()
                    pv = self.ps[:, bi, 0:NB + 1]
                    for k in range(8):
                        kb.op('pe', lambda k=k: nc.tensor.matmul(pv, wv[:, k, sub * 128:(sub + 1) * 128], self.scT[:, k, :],
                                                                  start=(k == 0), stop=(k == 7)),
                              r=[wt.name, "scT"], w=[bk])
                    add1 = 1.0 if (ch // 8) in (1, 4) else 0.0
                    dst = self.mod[:, li * 48 + ch, :]
                    kb.op('dve', lambda: nc.vector.tensor_scalar(out=dst, in0=pv, scalar1=self.adab[:, ch:ch + 1], scalar2=add1,
                                                                 op0=ALU.add, op1=ALU.add),
                          r=[bk, "adab"], w=[("mod", li, ch)])

    def load_input(self, b):
        kb = self.kb
        for tb in range(self.NCH):
            t0 = tb * 128
            src = self.cx[b, t0:t0 + 128, :] if tb < self.NCC else self.x[b, t0 - self.CTX:t0 - self.CTX + 128, :]
            xt = self.t1k.next()
            kb.dma(xt[:, :D], src, w=[xt.name])
            self.transpose_store(xt, 8, self.HT, "HT", 0, t0)

    def transpose_to(self, src_tile, nchunk, dst_tile):
        nc, kb = self.nc, self.kb
        dv = dst_tile[:, :nchunk * 128].rearrange("p (c t) -> p c t", c=nchunk)
        for c0 in range(0, nchunk, 4):
            bi, bk = self.pbank()
            n = min(4, nchunk - c0)
            for j in range(n):
                c = c0 + j
                kb.op('pe', lambda c=c, j=j: nc.tensor.transpose(self.ps[:, bi, j * 128:(j + 1) * 128],
                                                                   src_tile[:, c * 128:(c + 1) * 128], self.ident),
                      r=[src_tile.name, "cm"], w=[bk])
            self.evac(dv[:, c0:c0 + n, :], self.ps[:, bi, :n * 128].rearrange("p (c t) -> p c t", c=n),
                      r=[bk], w=[dst_tile.name])
        return dv

    def transpose_store(self, src_tile, nchunk, dst_dram, dname, row0, t0, scale_cols=None):
        nc, kb = self.nc, self.kb
        ft = self.t1k.next() if nchunk <= 8 else self.t2k.next()
        dv = self.transpose_to(src_tile, nchunk, ft)
        if scale_cols is not None:
            kb.op('dve', lambda: nc.vector.tensor_tensor(out=dv, in0=dv, in1=scale_cols.unsqueeze(2).to_broadcast([128, nchunk, 128]),
                                                         op=ALU.mult), r=[ft.name, "par"], w=[ft.name])
        kb.dma(dst_dram[row0:row0 + nchunk * 128, t0:t0 + 128].rearrange("(c p) t -> p c t", p=128), dv,
               r=[ft.name], w=kF(dname, row0, row0 + nchunk * 128, t0, t0 + 128))

    def load_layer_params(self, li):
        nc, kb = self.nc, self.kb
        w = self.W[li]
        kind = self.kinds[li]
        par = self.par
        pk = ["par"]
        if kind == 0:
            for j in range(5):
                kb.dma(par[:, j * 16:(j + 1) * 16], w['conv_w'][j].rearrange("(c p) -> p c", p=128), w=pk, allow_slow_non_contiguous=True)
            kb.dma(par[:, 80:96], w['conv_b'].rearrange("(c p) -> p c", p=128), w=pk, allow_slow_non_contiguous=True)
            kb.dma(par[:, 96:112], w['norm_g'].rearrange("(c p) -> p c", p=128), w=pk, allow_slow_non_contiguous=True)
            kb.dma(par[:, 112:128], w['skip'].rearrange("(c p) -> p c", p=128), w=pk, allow_slow_non_contiguous=True)
            kb.dma(par[:, 128:144], w['b_gate'].partition_broadcast(128), w=pk)
            kb.dma(par[:, 256:1024].rearrange("p (c j) -> p c j", c=48), w['w_gate'].rearrange("(c p) j -> p c j", p=128), w=pk)
        elif kind == 1:
            for j in range(5):
                kb.dma(par[:, j * 32:(j + 1) * 32], w['conv_w'][j].rearrange("(c p) -> p c", p=128), w=pk, allow_slow_non_contiguous=True)
            kb.dma(par[:, 160:192], w['conv_b'].rearrange("(c p) -> p c", p=128), w=pk, allow_slow_non_contiguous=True)
            kb.dma(par[:, 192:256], w['dt_bias'].partition_broadcast(128), w=pk)
            kb.dma(par[:, 256:320], w['a_log'].partition_broadcast(128), w=pk)
            kb.op('act', lambda: nc.scalar.activation(out=par[:, 256:320], in_=par[:, 256:320], func=AF.Exp), r=pk, w=pk)
            kb.op('dve', lambda: nc.vector.tensor_scalar(out=par[:, 256:320], in0=par[:, 256:320], scalar1=-1.0, scalar2=None,
                                                         op0=ALU.mult), r=pk, w=pk)
            kb.dma(par[:, 320:352], w['d'].partition_broadcast(128), w=pk)
            kb.dma(par[:, 352:368], w['norm_g'].rearrange("(c p) -> p c", p=128), w=pk, allow_slow_non_contiguous=True)
        elif kind == 2:
            kb.dma(par[:, 0:256], w['lam'].rearrange("a b -> (a b)").partition_broadcast(128), w=pk)
            kb.dma(par[:, 256:384], w['norm_g'].partition_broadcast(128), w=pk)
            lam_init = 0.8 - 0.6 * math.exp(-0.3 * self.layer_ids[li])
            pv = par[:, 0:256].rearrange("p (a b c) -> p a b c", a=2, b=2)
            kb.op('dve', lambda: nc.vector.tensor_tensor(out=par[:, 512:640].rearrange("p (a c) -> p a c", a=2),
                                                         in0=pv[:, :, 0, :], in1=pv[:, :, 1, :], op=ALU.mult), r=pk, w=pk)
            kb.op('dve', lambda: nc.vector.reduce_sum(out=par[:, 402:404], in_=par[:, 512:640].rearrange("p (a c) -> p a c", a=2),
                                                      axis=AX.X), r=pk, w=pk)
            kb.op('act', lambda: nc.scalar.activation(out=par[:, 404:406], in_=par[:, 402:404], func=AF.Exp), r=pk, w=pk)
            kb.op('dve', lambda: nc.vector.tensor_tensor(out=par[:, 400:401], in0=par[:, 405:406], in1=par[:, 404:405],
                                                         op=ALU.subtract), r=pk, w=pk)
            kb.op('dve', lambda: nc.vector.tensor_scalar(out=par[:, 400:401], in0=par[:, 400:401], scalar1=-lam_init, scalar2=None,
                                                         op0=ALU.add), r=pk, w=pk)
        else:
            kb.dma(par[:, 0:8], w['decay'].partition_broadcast(128), w=pk)
            self.log_sigmoid(par[:, 0:8], par[:, 0:8], par[:, 8:16], pk)
            kb.dma(par[:, 16:32], w['norm_g'].rearrange("(c p) -> p c", p=128), w=pk, allow_slow_non_contiguous=True)
        for j0 in range(0, 16, 4):
            kt = self.t2k.next()
            kb.dma(kt[:, :512].rearrange("p (j d) -> p j d", j=4), w['keys'][j0:j0 + 4].rearrange("j n d -> n j d"), w=[kt.name])
            bi, bk = self.pbank()
            for j in range(4):
                kb.op('pe', lambda j=j: nc.tensor.transpose(self.ps[:, bi, j * 128:(j + 1) * 128], kt[:, j * 128:(j + 1) * 128],
                                                             self.ident), r=[kt.name, "cm"], w=[bk])
            self.evac(self.keysT[:, j0:j0 + 4, :], self.ps[:, bi, :].rearrange("p (j n) -> p j n", j=4), r=[bk], w=["keysT"])

    def log_sigmoid(self, out, in_, tmp, keys):
        nc, kb = self.nc, self.kb
        kb.op('act', lambda: nc.scalar.activation(out=tmp, in_=in_, func=AF.Exp, scale=-1.0), r=keys, w=keys)
        kb.op('act', lambda: nc.scalar.activation(out=tmp, in_=tmp, func=AF.Ln, bias=1.0, scale=1.0), r=keys, w=keys)
        kb.op('dve', lambda: nc.vector.tensor_scalar(out=out, in0=tmp, scalar1=-1.0, scalar2=None, op0=ALU.mult), r=keys, w=keys)

    def load_xin(self, src, sname, K, t0, tg, premod=None):
        nc, kb = self.nc, self.kb
        KC = K // 128
        xt = self.ringL.next()
        xv = xt[:, :KC * tg].rearrange("p (k t) -> p k t", k=KC)
        kb.dma(xv, src[0:K, t0:t0 + tg].rearrange("(k p) t -> p k t", p=128), r=kF(sname, 0, K, t0, t0 + tg), w=[xt.name])
        if premod is not None:
            li, js, jt, b = premod
            col = self.NB if t0 < self.CTX else b
            sc = self.modcol(li, js, col)
            sh = self.modcol(li, jt, col)
            for k in range(8):
                kb.op('act', lambda k=k: nc.scalar.activation(out=xv[:, k, :], in_=xv[:, k, :], func=AF.Identity,
                                                              scale=sc[:, k, :], bias=sh[:, k, :]),
                      r=[xt.name, ("mod", li, js * 8 + k), ("mod", li, jt * 8 + k)], w=[xt.name])
        return xt, xv

    def proj(self, src, sname, K, Wap, segs, premod=None, tgmax=None, include_ctx=True, groups=None):
        nc, kb = self.nc, self.kb
        KC = K // 128
        if tgmax is None:
            tgmax = 512 if KC <= 8 else 256
        wb = min(512, 4096 // KC)
        Wv = Wap.rearrange("(k p) n -> p k n", p=128)
        for (t0, tg) in (groups if groups is not None else self.groups(tgmax, include_ctx)):
            xt, xv = self.load_xin(src, sname, K, t0, tg, premod)
            blocks = []
            for (n0, n1, mode, epi) in segs:
                c = n0
                while c < n1:
                    bw = min(wb, n1 - c)
                    blocks.append((c, bw, mode, epi))
                    c += bw
            tiles = {}

            def loadw(i):
                c, bw, mode, epi = blocks[i]
                wt = self.ringS.next()
                wv = wt[:, :KC * bw].rearrange("p (k n) -> p k n", k=KC)
                kb.dma(wv, Wv[:, :, c:c + bw], w=[wt.name])
                tiles[i] = (wt, wv)
            loadw(0)
            for i in range(len(blocks)):
                if i + 1 < len(blocks):
                    loadw(i + 1)
                c, bw, mode, epi = blocks[i]
                wt, wv = tiles.pop(i)
                if mode == 'F':
                    for sub in range((bw + 127) // 128):
                        m = min(128, bw - sub * 128)
                        bi, bk = self.pbank()
                        pv = self.ps[:m, bi, :tg]
                        for k in range(KC):
                            kb.op('pe', lambda k=k: nc.tensor.matmul(pv, wv[:, k, sub * 128:sub * 128 + m], xv[:, k, :],
                                                                      start=(k == 0), stop=(k == KC - 1)),
                                  r=[wt.name, xt.name], w=[bk])
                        epi(pv, bk, (c + sub * 128) // 128, t0, tg)
                else:
                    for tt in range(tg // 128):
                        bi, bk = self.pbank()
                        pv = self.ps[:, bi, :bw]
                        for k in range(KC):
                            kb.op('pe', lambda k=k: nc.tensor.matmul(pv, xv[:, k, tt * 128:(tt + 1) * 128], wv[:, k, :],
                                                                      start=(k == 0), stop=(k == KC - 1)),
                                  r=[wt.name, xt.name], w=[bk])
                        epi(pv, bk, c, bw, t0 + tt * 128)

    def epiF(self, dst, dname, nbase):
        kb = self.kb

        def epi(pv, bk, cc, t0, tg):
            st = self.stg.next()
            sv = st[:, :tg]
            rc = cc - nbase
            self.evac(sv, pv, r=[bk], w=[st.name])
            kb.dma(dst[rc * 128:(rc + 1) * 128, t0:t0 + tg], sv, r=[st.name], w=kF(dname, rc * 128, rc * 128 + 128, t0, t0 + tg))
        return epi

    def epiT(self, dst, dname, cbase):
        kb = self.kb

        def epi(pv, bk, c, bw, tok0):
            st = self.stg.next()
            sv = st[:, :bw]
            self.evac(sv, pv, r=[bk], w=[st.name])
            kb.dma(dst[tok0:tok0 + 128, c - cbase:c - cbase + bw], sv, r=[st.name],
                   w=kT(dname, tok0, tok0 + 128, c - cbase, c - cbase + bw))
        return epi

    def proj_rope(self, src, sname, Wap, col0, ncols, dst, dname, premod, d):
        nc, kb = self.nc, self.kb
        Wv = Wap.rearrange("(k p) n -> p k n", p=128)
        q = d // 4
        nblk = 128 // (2 * q)
        for (t0, tg) in self.groups(512, True):
            lat = t0 >= self.CTX
            xt, xv = self.load_xin(src, sname, D, t0, tg, premod)
            for ch in range(ncols // 128):
                c = col0 + ch * 128
                wt = self.ringS.next()
                wv = wt[:, :2048].rearrange("p (k n) -> p k n", k=8)
                kb.dma(wv[:, :, 0:128], Wv[:, :, c:c + 128], w=[wt.name])
                if lat:
                    src4 = wv[:, :, 0:128].rearrange("p k (b h q) -> p k b h q", b=nblk, h=2)
                    dst4 = wv[:, :, 128:256].rearrange("p k (b h q) -> p k b h q", b=nblk, h=2)
                    if nblk == 1:
                        kb.op('pool', lambda: nc.gpsimd.tensor_copy(out=dst4[:, :, 0, 0, :], in_=src4[:, :, 0, 1, :]),
                              r=[wt.name], w=[(wt.name, 'p')])
                        kb.op('pool', lambda: nc.gpsimd.tensor_copy(out=dst4[:, :, 0, 1, :], in_=src4[:, :, 0, 0, :]),
                              r=[wt.name], w=[(wt.name, 'p')])
                    else:
                        kb.op('pool', lambda: nc.gpsimd.tensor_copy(out=dst4[:, :, :, 0, :], in_=src4[:, :, :, 1, :]),
                              r=[wt.name], w=[(wt.name, 'p')])
                        kb.op('pool', lambda: nc.gpsimd.tensor_copy(out=dst4[:, :, :, 1, :], in_=src4[:, :, :, 0, :]),
                              r=[wt.name], w=[(wt.name, 'p')])
                bi, bk = self.pbank()
                pv = self.ps[:, bi, :tg]
                for k in range(8):
                    kb.op('pe', lambda k=k: nc.tensor.matmul(pv, wv[:, k, 0:128], xv[:, k, :], start=(k == 0), stop=(k == 7)),
                          r=[wt.name, xt.name], w=[bk])
                st = self.stg.next()
                sv = st[:, :tg]
                if not lat:
                    self.evac(sv, pv, r=[bk], w=[st.name])
                else:
                    bi2, bk2 = self.pbank()
                    pv2 = self.ps[:, bi2, :tg]
                    for k in range(8):
                        kb.op('pe', lambda k=k: nc.tensor.matmul(pv2, wv[:, k, 128:256], xv[:, k, :], start=(k == 0), stop=(k == 7)),
                              r=[wt.name, (wt.name, 'p'), xt.name], w=[bk2])
                    ct = self.sm_k.next()
                    sn = self.sm_v.next()
                    s0 = t0 - self.CTX
                    if d == 256:
                        lc = ch % 2
                        kb.dma(ct[:, :tg], self.c_rr[0, lc, :, s0:s0 + tg], w=[ct.name])
                        kb.dma(sn[:, :tg], self.c_rr[1, lc, :, s0:s0 + tg], w=[sn.name])
                    else:
                        kb.dma(ct[:, :tg], self.c_rd[0, :, s0:s0 + tg], w=[ct.name])
                        kb.dma(sn[:, :tg], self.c_rd[1, :, s0:s0 + tg], w=[sn.name])
                    kb.op('dve', lambda: nc.vector.tensor_tensor(out=ct[:, :tg], in0=pv, in1=ct[:, :tg], op=ALU.mult),
                          r=[bk, ct.name], w=[ct.name])
                    kb.op('dve', lambda: nc.vector.tensor_tensor(out=sn[:, :tg], in0=pv2, in1=sn[:, :tg], op=ALU.mult),
                          r=[bk2, sn.name], w=[sn.name])
                    kb.op('dve', lambda: nc.vector.tensor_tensor(out=sv, in0=ct[:, :tg], in1=sn[:, :tg], op=ALU.add),
                          r=[ct.name, sn.name], w=[st.name])
                kb.dma(dst[ch * 128:(ch + 1) * 128, t0:t0 + tg], sv, r=[st.name], w=kF(dname, ch * 128, ch * 128 + 128, t0, t0 + tg))

    def conv(self, src, sname, nch, wcol, bcol, dst, dname):
        nc, kb = self.nc, self.kb
        segs = [(0, self.CTX), (self.CTX, self.T)]
        for c in range(nch):
            for (a, e) in segs:
                n = e - a
                for o in range(0, n, 2048):
                    m = min(2048, n - o)
                    xp = self.ringS.next()
                    ac = self.ringS.next()
                    lo = 2 if o == 0 else 0
                    hi = 2 if o + m == n else 0
                    if lo:
                        kb.op('pool', lambda: nc.gpsimd.memset(xp[:, 0:2], 0.0), w=[xp.name])
                    if hi:
                        kb.op('pool', lambda: nc.gpsimd.memset(xp[:, 2 + m:4 + m], 0.0), w=[xp.name])
                    s0 = a + o - (2 - lo)
                    s1 = a + o + m + (2 - hi)
                    kb.dma(xp[:, lo:lo + (s1 - s0)], src[c * 128:(c + 1) * 128, s0:s1],
                           r=kF(sname, c * 128, c * 128 + 128, s0, s1), w=[xp.name])
                    rk = [xp.name, "par"]
                    kb.op('dve', lambda: nc.vector.tensor_scalar(out=ac[:, :m], in0=xp[:, 0:m], scalar1=wcol(c, 0), scalar2=None,
                                                                 op0=ALU.mult), r=rk, w=[ac.name])
                    for j in range(1, 5):
                        kb.op('dve', lambda j=j: nc.vector.scalar_tensor_tensor(out=ac[:, :m], in0=xp[:, j:j + m], scalar=wcol(c, j),
                                                                                 in1=ac[:, :m], op0=ALU.mult, op1=ALU.add),
                              r=rk + [ac.name], w=[ac.name])
                    bc_ = bcol(c)
                    kb.op('act', lambda: nc.scalar.activation(out=ac[:, :m], in_=ac[:, :m], func=AF.Silu, bias=bc_, scale=1.0),
                          r=[ac.name, "par"], w=[ac.name])
                    kb.dma(dst[c * 128:(c + 1) * 128, a + o:a + o + m], ac[:, :m], r=[ac.name],
                           w=kF(dname, c * 128, c * 128 + 128, a + o, a + o + m))

    def decay_cols(self, NU, have_ig, scale):
        nc, kb = self.nc, self.kb
        NCH, NCC = self.NCH, self.NCC
        N2 = 2 * NU
        lns = math.log(scale)
        for lc in range(NCH):
            bi, bk = self.pbank()
            fs = [sc for sc in range(NCH) if sc <= lc]
            for i, sc in enumerate(fs):
                m = self.trif if sc == lc else self.ones
                kb.op('pe', lambda m=m, sc=sc, i=i: nc.tensor.matmul(self.ps[:, bi, 0:NU], m, self.ldall[:, sc, 0:NU],
                                                                     start=(i == 0), stop=(i == len(fs) - 1)),
                      r=["cm", "ldall"], w=[bk])
            if lc < NCC:
                bs = [sc for sc in range(lc, NCC)]
            else:
                bs = list(range(NCC)) + [sc for sc in range(lc, NCH)]
            for i, sc in enumerate(bs):
                m = self.trib if sc == lc else self.ones
                kb.op('pe', lambda m=m, sc=sc, i=i: nc.tensor.matmul(self.ps[:, bi, NU:N2], m, self.ldall[:, sc, NU:N2],
                                                                     start=(i == 0), stop=(i == len(bs) - 1)),
                      r=["cm", "ldall"], w=[bk])
            kb.op('dve', lambda: nc.vector.tensor_copy(out=self.fcol[:, lc, 0:N2], in_=self.ps[:, bi, 0:N2]), r=[bk], w=["fcol"])
            if have_ig:
                kb.op('dve', lambda: nc.vector.scalar_tensor_tensor(out=self.ccol[:, lc, 0:N2], in0=self.igall[:, lc, 0:N2], scalar=lns,
                                                                     in1=self.fcol[:, lc, 0:N2], op0=ALU.add, op1=ALU.subtract),
                      r=["fcol", "igall"], w=["igall"])
            else:
                kb.op('dve', lambda: nc.vector.tensor_scalar(out=self.ccol[:, lc, 0:N2], in0=self.fcol[:, lc, 0:N2], scalar1=-1.0,
                                                             scalar2=lns, op0=ALU.mult, op1=ALU.add), r=["fcol"], w=["igall"])

    def quad(self, G, R, KC, DV, sep, qsrc, qname, qrow0, ksrc, kname, krow0, vsrc, vname, post, lbs=None):
        nc, kb = self.nc, self.kb
        NCH, NCC = self.NCH, self.NCC
        NU = G * R
        if lbs is None:
            lbs = range(NCH)
        for lb in lbs:
            sbs_f = [sb for sb in range(NCH) if sb <= lb]
            if lb < NCC:
                sbs_b = list(range(lb, NCC))
            else:
                sbs_b = list(range(NCC)) + list(range(lb, NCH))
            union = sorted(set(sbs_f) | set(sbs_b))
            hh = self.t2k.next()
            for g in range(G):
                rowL = self.t1k.next()
                rl = rowL[:, :2 * R * 128].rearrange("p (j l) -> p j l", j=2 * R)
                for j0 in range(0, 2 * R, 4):
                    bi, bk = self.pbank()
                    n = min(4, 2 * R - j0)
                    for j in range(j0, j0 + n):
                        d_, r_ = j // R, j % R
                        ud = d_ * NU + g * R + r_
                        bc = self.tiny.next()
                        kb.op('dve', lambda bc=bc, ud=ud: nc.vector.tensor_copy(out=bc[:, :128],
                                                                                 in_=self.fcol[:, lb, ud:ud + 1].to_broadcast([128, 128])),
                              r=["fcol"], w=[bc.name])
                        kb.op('pe', lambda bc=bc, j=j: nc.tensor.matmul(self.ps[:, bi, (j - j0) * 128:(j - j0 + 1) * 128], bc[:, :128],
                                                                         self.ident, start=True, stop=True),
                              r=[bc.name, "cm"], w=[bk])
                    self.evac(rl[:, j0:j0 + n, :], self.ps[:, bi, :n * 128].rearrange("p (j l) -> p j l", j=n), r=[bk], w=[rowL.name])
                qt = self.sm_q.next()
                qv = qt[:, :KC * 128].rearrange("p (k t) -> p k t", k=KC)
                r0 = qrow0 + g * KC * 128
                kb.dma(qv, qsrc[r0:r0 + KC * 128, lb * 128:(lb + 1) * 128].rearrange("(k p) t -> p k t", p=128),
                       r=kF(qname, r0, r0 + KC * 128, lb * 128, lb * 128 + 128), w=[qt.name])
                steps = []
                for sb in union:
                    for d_ in (0, 1):
                        if sb in (sbs_f if d_ == 0 else sbs_b):
                            steps.append((sb, d_))
                first, lastu = {}, {}
                for i, (sb, d_) in enumerate(steps):
                    a = d_ if sep else 0
                    if a not in first:
                        first[a] = i
                    lastu[a] = i
                loaded = {}

                def load_sb(sb):
                    kt = self.sm_k.next()
                    kv = kt[:, :KC * 128].rearrange("p (k t) -> p k t", k=KC)
                    k0 = krow0 + g * KC * 128
                    kb.dma(kv, ksrc[k0:k0 + KC * 128, sb * 128:(sb + 1) * 128].rearrange("(k p) t -> p k t", p=128),
                           r=kF(kname, k0, k0 + KC * 128, sb * 128, sb * 128 + 128), w=[kt.name])
                    vt = self.sm_v.next()
                    c0 = g * R * DV
                    kb.dma(vt[:, :R * DV], vsrc[sb * 128:(sb + 1) * 128, c0:c0 + R * DV],
                           r=kT(vname, sb * 128, sb * 128 + 128, c0, c0 + R * DV), w=[vt.name])
                    loaded[sb] = (kt, kv, vt)
                load_sb(union[0])
                si_ = 0
                for ui, sb in enumerate(union):
                    if ui + 1 < len(union):
                        load_sb(union[ui + 1])
                    kt, kv, vt = loaded.pop(sb)
                    bi, bk = self.pbank()
                    sraw = self.ps[:, bi, 0:128]
                    for k in range(KC):
                        kb.op('pe', lambda k=k: nc.tensor.matmul(sraw, kv[:, k, :], qv[:, k, :], start=(k == 0), stop=(k == KC - 1)),
                              r=[kt.name, qt.name], w=[bk])
                    while si_ < len(steps) and steps[si_][0] == sb:
                        _, d_ = steps[si_]
                        wt = self.sm_w.next()
                        wv = wt[:, :R * 128].rearrange("p (r l) -> p r l", r=R)
                        u0 = d_ * NU + g * R
                        cc = self.ccol[:, sb, u0:u0 + R]
                        rls = rl[:, d_ * R:(d_ + 1) * R, :]
                        diag = (sb == lb)
                        if not diag:
                            for r_ in range(R):
                                kb.op('act', lambda r_=r_: nc.scalar.activation(out=wv[:, r_, :], in_=rls[:, r_, :], func=AF.Exp,
                                                                                bias=cc[:, r_:r_ + 1], scale=1.0),
                                      r=[rowL.name, "igall"], w=[wt.name])
                        else:
                            kb.op('dve', lambda: nc.vector.tensor_tensor(out=wv, in0=rls, in1=cc.unsqueeze(2).to_broadcast([128, R, 128]),
                                                                         op=ALU.add), r=[rowL.name, "igall"], w=[wt.name])
                            if diag:
                                mk = self.mnf if d_ == 0 else self.mnb
                                kb.op('dve', lambda: nc.vector.tensor_tensor(out=wv, in0=wv, in1=mk.unsqueeze(1).to_broadcast([128, R, 128]),
                                                                             op=ALU.add), r=[wt.name, "cm"], w=[wt.name])
                            kb.op('act', lambda: nc.scalar.activation(out=wv, in_=wv, func=AF.Exp), r=[wt.name], w=[wt.name])
                        kb.op('dve', lambda: nc.vector.tensor_tensor(out=wv, in0=wv, in1=sraw.unsqueeze(1).to_broadcast([128, R, 128]),
                                                                     op=ALU.mult), r=[wt.name, bk], w=[wt.name])
                        a = d_ if sep else 0
                        for r_ in range(R):
                            if sep:
                                acc = self.ps[:, d_, 0:DV]
                            else:
                                acc = self.ps[:, r_, 0:DV]
                            kb.op('pe', lambda r_=r_, acc=acc: nc.tensor.matmul(acc, wv[:, r_, :], vt[:, r_ * DV:(r_ + 1) * DV],
                                                                                 start=(first[a] == si_), stop=(lastu[a] == si_)),
                                  r=[wt.name, vt.name], w=["ps%d" % (d_ if sep else r_)])
                            if sep:
                                kb.op('pe', lambda: nc.tensor.matmul(self.ps[:, 2 + d_, 0:1], wv[:, r_, :], self.ones[:, 0:1],
                                                                      start=(first[a] == si_), stop=(lastu[a] == si_)),
                                      r=[wt.name, "cm"], w=["ps%d" % (2 + d_)])
                        si_ += 1
                c0 = g * R * DV
                if sep:
                    tn = self.tiny.next()
                    for d_ in (0, 1):
                        kb.op('act', lambda d_=d_: nc.scalar.activation(out=tn[:, d_:d_ + 1], in_=self.ps[:, 2 + d_, 0:1], func=AF.Abs),
                              r=["ps%d" % (2 + d_)], w=[tn.name])
                    kb.op('dve', lambda: nc.vector.tensor_scalar(out=tn[:, 0:2], in0=tn[:, 0:2], scalar1=1.0, scalar2=None, op0=ALU.max),
                          r=[tn.name], w=[tn.name])
                    kb.op('dve', lambda: nc.vector.reciprocal(out=tn[:, 2:4], in_=tn[:, 0:2]), r=[tn.name], w=[tn.name])
                    kb.op('dve', lambda: nc.vector.tensor_scalar(out=hh[:, c0:c0 + DV], in0=self.ps[:, 0, 0:DV], scalar1=tn[:, 2:3],
                                                                 scalar2=None, op0=ALU.mult), r=["ps0", tn.name], w=[hh.name])
                    kb.op('dve', lambda: nc.vector.scalar_tensor_tensor(out=hh[:, c0:c0 + DV], in0=self.ps[:, 1, 0:DV], scalar=tn[:, 3:4],
                                                                         in1=hh[:, c0:c0 + DV], op0=ALU.mult, op1=ALU.add),
                          r=["ps1", tn.name, hh.name], w=[hh.name])
                else:
                    self.evac(hh[:, c0:c0 + R * DV].rearrange("p (r d) -> p r d", r=R), self.ps[:, 0:R, 0:DV],
                              r=["ps%d" % r_ for r_ in range(R)], w=[hh.name])
            post(lb, hh)

    def head_norm_tok(self, x, nh, dh, eps, center=True, sq=None):
        nc, kb = self.nc, self.kb
        xv = x[:, :nh * dh].rearrange("p (h d) -> p h d", h=nh)
        tn = self.tiny.next()
        if sq is None:
            sq = self.t2k.next() if nh * dh > 1024 else self.t1k.next()
        sqv = sq[:, :nh * dh].rearrange("p (h d) -> p h d", h=nh)
        if center:
            kb.op('dve', lambda: nc.vector.reduce_sum(out=tn[:, 0:nh], in_=xv, axis=AX.X), r=[x.name], w=[tn.name])
            kb.op('dve', lambda: nc.vector.tensor_scalar(out=tn[:, 0:nh], in0=tn[:, 0:nh], scalar1=1.0 / dh, scalar2=None, op0=ALU.mult),
                  r=[tn.name], w=[tn.name])
            kb.op('dve', lambda: nc.vector.tensor_tensor(out=xv, in0=xv, in1=tn[:, 0:nh].unsqueeze(2).to_broadcast([128, nh, dh]),
                                                         op=ALU.subtract), r=[x.name, tn.name], w=[x.name])
        kb.op('dve', lambda: nc.vector.tensor_tensor(out=sqv, in0=xv, in1=xv, op=ALU.mult), r=[x.name], w=[sq.name])
        kb.op('dve', lambda: nc.vector.reduce_sum(out=tn[:, 16:16 + nh], in_=sqv, axis=AX.X), r=[sq.name], w=[tn.name])
        kb.op('dve', lambda: nc.vector.tensor_scalar(out=tn[:, 32:32 + nh], in0=tn[:, 16:16 + nh], scalar1=1.0 / dh, scalar2=float(eps),
                                                     op0=ALU.mult, op1=ALU.add), r=[tn.name], w=[tn.name])
        kb.op('act', lambda: nc.scalar.activation(out=tn[:, 32:32 + nh], in_=tn[:, 32:32 + nh], func=AF.Sqrt), r=[tn.name], w=[tn.name])
        kb.op('dve', lambda: nc.vector.reciprocal(out=tn[:, 48:48 + nh], in_=tn[:, 32:32 + nh]), r=[tn.name], w=[tn.name])
        kb.op('dve', lambda: nc.vector.tensor_tensor(out=xv, in0=xv, in1=tn[:, 48:48 + nh].unsqueeze(2).to_broadcast([128, nh, dh]),
                                                     op=ALU.mult), r=[x.name, tn.name], w=[x.name])

    def mlstm(self, li, b, last):
        nc, kb = self.nc, self.kb
        w = self.W[li]
        par = self.par
        NCH, NCC = self.NCH, self.NCC
        XM, ZT, XC, QT, KT = self.A[0], self.A[1], self.A[2], self.A[3], self.A[4]
        OP, V, = self.Bt[0], self.Bt[1]
        pm = (li, 1, 0, b)
        self.proj(self.HT, "HT", D, w['w_up'], [
            (0, 2048, 'F', self.epiF(XM, "A0", 0)),
            (2048, 4096, 'F', self.epiF(ZT, "A1", 16)),
            (4096, 6144, 'T', self.epiT(OP, "B0", 4096))], premod=pm)
        self.conv(XM, "A0", 16, lambda c, j: par[:, j * 16 + c:j * 16 + c + 1], lambda c: par[:, 80 + c:81 + c], XC, "A2")
        self.proj(XC, "A2", 2048, w['w_qk'], [
            (0, 2048, 'F', self.epiF(QT, "A3", 0)),
            (2048, 4096, 'F', self.epiF(KT, "A4", 16))])
        self.proj(XM, "A0", 2048, w['w_v'], [(0, 2048, 'T', self.epiT(V, "B1", 0))])
        wg = par[:, 256:1024].rearrange("p (c j) -> p c j", c=48)
        for tb in range(NCH):
            t0 = tb * 128
            qt = self.t2k.next()
            kt = self.t2k.next()
            vt = self.t2k.next()
            qv = qt[:, :2048].rearrange("p (k t) -> p k t", k=16)
            kv = kt[:, :2048].rearrange("p (k t) -> p k t", k=16)
            kb.dma(qv, QT[:, t0:t0 + 128].rearrange("(k p) t -> p k t", p=128), r=kF("A3", 0, 2048, t0, t0 + 128), w=[qt.name])
            kb.dma(kv, KT[:, t0:t0 + 128].rearrange("(k p) t -> p k t", p=128), r=kF("A4", 0, 2048, t0, t0 + 128), w=[kt.name])
            kb.dma(vt[:, :2048], V[t0:t0 + 128, :], r=kT("B1", t0, t0 + 128, 0, 2048), w=[vt.name])
            vT = self.t2k.next()
            vv = self.transpose_to(vt, 16, vT)
            gi, gk = self.pbank()
            gp = self.ps[:, gi, 0:16]
            for c in range(48):
                src, sk = (qv, qt.name) if c < 16 else ((kv, kt.name) if c < 32 else (vv, vT.name))
                kb.op('pe', lambda c=c, src=src: nc.tensor.matmul(gp, src[:, c % 16, :], wg[:, c, :], start=(c == 0), stop=(c == 47)),
                      r=[sk, "par"], w=[gk])
            gt = self.tiny.next()
            kb.op('dve', lambda: nc.vector.tensor_tensor(out=gt[:, 0:16], in0=gp, in1=par[:, 128:144], op=ALU.add),
                  r=[gk, "par"], w=[gt.name])
            g4 = gt[:, 0:16].rearrange("p (a x h) -> p a x h", a=2, x=2)
            kb.op('act', lambda: nc.scalar.activation(out=gt[:, 16:24].rearrange("p (a h) -> p a h", a=2), in_=g4[:, :, 1, :],
                                                      func=AF.Exp, scale=-1.0), r=[gt.name], w=[gt.name])
            kb.op('act', lambda: nc.scalar.activation(out=gt[:, 16:24], in_=gt[:, 16:24], func=AF.Ln, bias=1.0, scale=1.0),
                  r=[gt.name], w=[gt.name])
            kb.op('dve', lambda: nc.vector.tensor_scalar(out=self.ldall[:, tb, 0:8], in0=gt[:, 16:24], scalar1=-1.0, scalar2=None,
                                                         op0=ALU.mult), r=[gt.name], w=["ldall"])
            kb.op('dve', lambda: nc.vector.tensor_copy(out=self.igall[:, tb, 0:8].rearrange("p (a h) -> p a h", a=2), in_=g4[:, :, 0, :]),
                  r=[gt.name], w=["igall"])
        self.decay_cols(4, True, 512.0 ** -0.5)
        YIN = self.A[0]

        def post(lb, hh):
            t0 = lb * 128
            op = self.t2k.next()
            kb.dma(op[:, :2048], OP[t0:t0 + 128, :], r=kT("B0", t0, t0 + 128, 0, 2048), w=[op.name])
            kb.op('act', lambda: nc.scalar.activation(out=op[:, :2048], in_=op[:, :2048], func=AF.Sigmoid), r=[op.name], w=[op.name])
            kb.op('dve', lambda: nc.vector.tensor_tensor(out=hh[:, :2048], in0=hh[:, :2048], in1=op[:, :2048], op=ALU.mult),
                  r=[hh.name, op.name], w=[hh.name])
            self.head_norm_tok(hh, 4, 512, LN_EPS, sq=op)
            hT = self.t2k.next()
            hv = self.transpose_to(hh, 16, hT)
            xc = self.t2k.next()
            zt = self.t2k.next()
            xv = xc[:, :2048].rearrange("p (k t) -> p k t", k=16)
            zv = zt[:, :2048].rearrange("p (k t) -> p k t", k=16)
            kb.dma(xv, XC[:, t0:t0 + 128].rearrange("(k p) t -> p k t", p=128), r=kF("A2", 0, 2048, t0, t0 + 128), w=[xc.name])
            kb.dma(zv, ZT[:, t0:t0 + 128].rearrange("(k p) t -> p k t", p=128), r=kF("A1", 0, 2048, t0, t0 + 128), w=[zt.name])
            ng = par[:, 96:112].unsqueeze(2).to_broadcast([128, 16, 128])
            sk = par[:, 112:128].unsqueeze(2).to_broadcast([128, 16, 128])
            kb.op('dve', lambda: nc.vector.tensor_tensor(out=hv, in0=hv, in1=ng, op=ALU.mult), r=[hT.name, "par"], w=[hT.name])
            kb.op('dve', lambda: nc.vector.tensor_tensor(out=xv, in0=xv, in1=sk, op=ALU.mult), r=[xc.name, "par"], w=[xc.name])
            kb.op('dve', lambda: nc.vector.tensor_tensor(out=hv, in0=hv, in1=xv, op=ALU.add), r=[hT.name, xc.name], w=[hT.name])
            kb.op('act', lambda: nc.scalar.activation(out=zv, in_=zv, func=AF.Silu), r=[zt.name], w=[zt.name])
            kb.op('dve', lambda: nc.vector.tensor_tensor(out=hv, in0=hv, in1=zv, op=ALU.mult), r=[hT.name, zt.name], w=[hT.name])
            kb.dma(YIN[:, t0:t0 + 128].rearrange("(k p) t -> p k t", p=128), hv, r=[hT.name], w=kF("A0", 0, 2048, t0, t0 + 128))

        lbs = range(NCC, NCH) if last else None
        self.quad(4, 1, 4, 512, True, QT, "A3", 0, KT, "A4", 0, V, "B1", post, lbs=lbs)
        self.proj(YIN, "A0", 2048, w['w_down'], [(0, D, 'F', self.epiF(self.YT, "YT", 0))], include_ctx=not last)

    def retention(self, li, b, last):
        nc, kb = self.nc, self.kb
        w = self.W[li]
        par = self.par
        NCH, NCC = self.NCH, self.NCC
        QT, KT, GT, YIN = self.A[0], self.A[1], self.A[2], self.A[3]
        V = self.Bt[0]
        pm = (li, 1, 0, b)
        self.proj_rope(self.HT, "HT", w['w_in'], 0, 1024, QT, "A0", pm, 256)
        self.proj_rope(self.HT, "HT", w['w_in'], 1024, 1024, KT, "A1", pm, 256)
        self.proj(self.HT, "HT", D, w['w_in'], [
            (2048, 4096, 'T', self.epiT(V, "B0", 2048)),
            (4096, 6144, 'F', self.epiF(GT, "A2", 32))], premod=pm)
        kb.op('dve', lambda: nc.vector.tensor_copy(out=self.ldall[:, :, 0:8],
                                                   in_=par[:, 0:8].unsqueeze(1).to_broadcast([128, NCH, 8])),
              r=["par"], w=["ldall"])
        self.decay_cols(4, False, 256.0 ** -0.5)

        def post(lb, hh):
            t0 = lb * 128
            self.head_norm_tok(hh, 4, 512, LN_EPS)
            hT = self.t2k.next()
            hv = self.transpose_to(hh, 16, hT)
            gt = self.t2k.next()
            gv = gt[:, :2048].rearrange("p (k t) -> p k t", k=16)
            kb.dma(gv, GT[:, t0:t0 + 128].rearrange("(k p) t -> p k t", p=128), r=kF("A2", 0, 2048, t0, t0 + 128), w=[gt.name])
            ng = par[:, 16:32].unsqueeze(2).to_broadcast([128, 16, 128])
            kb.op('dve', lambda: nc.vector.tensor_tensor(out=hv, in0=hv, in1=ng, op=ALU.mult), r=[hT.name, "par"], w=[hT.name])
            kb.op('act', lambda: nc.scalar.activation(out=gv, in_=gv, func=AF.Silu), r=[gt.name], w=[gt.name])
            kb.op('dve', lambda: nc.vector.tensor_tensor(out=hv, in0=hv, in1=gv, op=ALU.mult), r=[hT.name, gt.name], w=[hT.name])
            kb.dma(YIN[:, t0:t0 + 128].rearrange("(k p) t -> p k t", p=128), hv, r=[hT.name], w=kF("A3", 0, 2048, t0, t0 + 128))

        lbs = range(NCC, NCH) if last else None
        self.quad(4, 1, 2, 512, False, QT, "A0", 0, KT, "A1", 0, V, "B0", post, lbs=lbs)
        self.proj(YIN, "A3", 2048, w['w_out'], [(0, D, 'F', self.epiF(self.YT, "YT", 0))], include_ctx=not last)

    def ssd(self, li, b, last):
        nc, kb = self.nc, self.kb
        w = self.W[li]
        par = self.par
        NCH, NCC = self.NCH, self.NCC
        XBC, XC = self.A4k, self.XC4k
        ZTOK, XS = self.Bt[0], self.Bt[1]
        pm = (li, 1, 0, b)

        def epi_dt(pv, bk, c, bw, tok0):
            tb = tok0 // 128
            tn = self.tiny.next()
            kb.op('dve', lambda: nc.vector.tensor_tensor(out=tn[:, 0:64], in0=pv, in1=par[:, 192:256], op=ALU.add), r=[bk, "par"], w=[tn.name])
            kb.op('act', lambda: nc.scalar.activation(out=tn[:, 0:64], in_=tn[:, 0:64], func=AF.Exp), r=[tn.name], w=[tn.name])
            kb.op('act', lambda: nc.scalar.activation(out=tn[:, 0:64], in_=tn[:, 0:64], func=AF.Ln, bias=1.0, scale=1.0), r=[tn.name], w=[tn.name])
            kb.op('act', lambda: nc.scalar.activation(out=self.igall[:, tb, 0:64], in_=tn[:, 0:64], func=AF.Ln), r=[tn.name], w=["igall"])
            kb.op('dve', lambda: nc.vector.tensor_tensor(out=self.ldall[:, tb, 0:64], in0=tn[:, 0:64], in1=par[:, 256:320], op=ALU.mult),
                  r=[tn.name, "par"], w=["ldall"])

        self.proj(self.HT, "HT", D, w['w_in'], [
            (0, 2048, 'T', self.epiT(ZTOK, "B0", 0)),
            (2048, 6144, 'F', self.epiF(XBC, "A4k", 16)),
            (6144, 6208, 'T', epi_dt)], premod=pm)
        self.conv(XBC, "A4k", 32, lambda c, j: par[:, j * 32 + c:j * 32 + c + 1], lambda c: par[:, 160 + c:161 + c], XC, "XC4k")
        for tb in range(NCH):
            t0 = tb * 128
            xt = self.t2k.next()
            xv = xt[:, :2048].rearrange("p (k t) -> p k t", k=16)
            kb.dma(xv, XC[0:2048, t0:t0 + 128].rearrange("(k p) t -> p k t", p=128), r=kF("XC4k", 0, 2048, t0, t0 + 128), w=[xt.name])
            xo = self.t2k.next()
            for c0 in range(0, 16, 4):
                bi, bk = self.pbank()
                for j in range(4):
                    c = c0 + j
                    kb.op('pe', lambda c=c, j=j: nc.tensor.transpose(self.ps[:, bi, j * 128:(j + 1) * 128], xv[:, c, :], self.ident),
                          r=[xt.name, "cm"], w=[bk])
                self.evac(xo[:, c0 * 128:(c0 + 4) * 128], self.ps[:, bi, :], r=[bk], w=[xo.name])
            kb.dma(XS[t0:t0 + 128, :], xo[:, :2048], r=[xo.name], w=kT("B1", t0, t0 + 128, 0, 2048))
        self.decay_cols(32, True, 1.0)
        YIN = self.A[0]

        def post(lb, hh):
            t0 = lb * 128
            xs = self.t2k.next()
            zt = self.t2k.next()
            kb.dma(xs[:, :2048], XS[t0:t0 + 128, :], r=kT("B1", t0, t0 + 128, 0, 2048), w=[xs.name])
            kb.dma(zt[:, :2048], ZTOK[t0:t0 + 128, :], r=kT("B0", t0, t0 + 128, 0, 2048), w=[zt.name])
            x3 = xs[:, :2048].rearrange("p (h d) -> p h d", h=32)
            kb.op('dve', lambda: nc.vector.tensor_tensor(out=x3, in0=x3, in1=par[:, 320:352].unsqueeze(2).to_broadcast([128, 32, 64]),
                                                         op=ALU.mult), r=[xs.name, "par"], w=[xs.name])
            kb.op('dve', lambda: nc.vector.tensor_tensor(out=hh[:, :2048], in0=hh[:, :2048], in1=xs[:, :2048], op=ALU.add),
                  r=[hh.name, xs.name], w=[hh.name])
            kb.op('act', lambda: nc.scalar.activation(out=zt[:, :2048], in_=zt[:, :2048], func=AF.Silu), r=[zt.name], w=[zt.name])
            kb.op('dve', lambda: nc.vector.tensor_tensor(out=hh[:, :2048], in0=hh[:, :2048], in1=zt[:, :2048], op=ALU.mult),
                  r=[hh.name, zt.name], w=[hh.name])
            self.head_norm_tok(hh, 8, 256, RMS_EPS, center=False, sq=xs)
            self.transpose_store(hh, 16, YIN, "A0", 0, t0, scale_cols=par[:, 352:368])

        lbs = range(NCC, NCH) if last else None
        self.quad(8, 4, 1, 64, False, XC, "XC4k", 3072, XC, "XC4k", 2048, XS, "B1", post, lbs=lbs)
        self.proj(YIN, "A0", 2048, w['w_out'], [(0, D, 'F', self.epiF(self.YT, "YT", 0))], include_ctx=not last)

    def diffattn(self, li, b, last):
        nc, kb = self.nc, self.kb
        w = self.W[li]
        par = self.par
        T, NCH, NCC, CTX = self.T, self.NCH, self.NCC, self.CTX
        QT, KT, YIN = self.A[0], self.A[1], self.A[2]
        V, O = self.Bt[0], self.Bt[1]
        pm = (li, 1, 0, b)
        lam_init = 0.8 - 0.6 * math.exp(-0.3 * self.layer_ids[li])
        self.proj_rope(self.HT, "HT", w['w_qkv'], 0, 1024, QT, "A0", pm, 64)
        self.proj_rope(self.HT, "HT", w['w_qkv'], 1024, 1024, KT, "A1", pm, 64)
        self.proj(self.HT, "HT", D, w['w_qkv'], [(2048, 3072, 'T', self.epiT(V, "B0", 2048))], premod=pm)
        sc = 64.0 ** -0.5
        kt, vt = self.bigS[0], self.bigS[1]
        ptring = Ring(self.bigS[2:4])
        lbl = range(NCC, NCH) if last else range(NCH)
        for h in range(8):
            kb.dma(kt[:, :T], KT[h * 128:(h + 1) * 128, :], r=kF("A1", h * 128, h * 128 + 128, 0, T), w=[kt.name])
            vv = vt[:, :NCH * 128].rearrange("p (c e) -> p c e", c=NCH)
            kb.dma(vv, V[:, h * 128:(h + 1) * 128].rearrange("(c p) e -> p c e", p=128), r=kT("B0", 0, T, h * 128, h * 128 + 128), w=[vt.name])
            for lb in lbl:
                nk = CTX if lb < NCC else T
                nkb = nk // 128
                nb5 = (nk + 511) // 512
                qt = self.sm_q.next()
                kb.op('dve', lambda: nc.vector.memset(qt[:, 0:256], 0.0), w=[qt.name])
                kb.dma(qt[0:64, 0:128], QT[h * 128:h * 128 + 64, lb * 128:(lb + 1) * 128],
                       r=kF("A0", h * 128, h * 128 + 128, lb * 128, lb * 128 + 128) + [qt.name], w=[(qt.name, 0)])
                kb.dma(qt[64:128, 128:256], QT[h * 128 + 64:h * 128 + 128, lb * 128:(lb + 1) * 128],
                       r=kF("A0", h * 128, h * 128 + 128, lb * 128, lb * 128 + 128) + [qt.name], w=[(qt.name, 1)])
                ot = self.sm_k.next()
                for m in (0, 1):
                    for kbk in range(nb5):
                        n = min(512, nk - kbk * 512)
                        kb.op('pe', lambda kbk=kbk, n=n: nc.tensor.matmul(self.ps[:, kbk, 0:n], qt[:, m * 128:(m + 1) * 128],
                                                                            kt[:, kbk * 512:kbk * 512 + n], start=True, stop=True),
                              r=[qt.name, (qt.name, 0), (qt.name, 1), kt.name], w=["ps%d" % kbk])
                    tn = self.tiny.next()
                    for kbk in range(nb5):
                        n = min(512, nk - kbk * 512)
                        kb.op('dve', lambda kbk=kbk, n=n: nc.vector.reduce_max(out=tn[:, kbk:kbk + 1], in_=self.ps[:, kbk, 0:n], axis=AX.X),
                              r=["ps%d" % kbk], w=[tn.name])
                    kb.op('dve', lambda: nc.vector.reduce_max(out=tn[:, 8:9], in_=tn[:, 0:nb5], axis=AX.X), r=[tn.name], w=[tn.name])
                    kb.op('dve', lambda: nc.vector.tensor_scalar(out=tn[:, 9:10], in0=tn[:, 8:9], scalar1=-sc, scalar2=None, op0=ALU.mult),
                          r=[tn.name], w=[tn.name])
                    pt = ptring.next()
                    for kbk in range(nb5):
                        n = min(512, nk - kbk * 512)
                        kb.op('act', lambda kbk=kbk, n=n: nc.scalar.activation(out=pt[:, kbk * 512:kbk * 512 + n], in_=self.ps[:, kbk, 0:n],
                                                                                func=AF.Exp, bias=tn[:, 9:10], scale=sc),
                              r=["ps%d" % kbk, tn.name], w=[pt.name])
                    kb.op('dve', lambda: nc.vector.reduce_sum(out=tn[:, 10:11], in_=pt[:, 0:nk], axis=AX.X), r=[pt.name], w=[tn.name])
                    kb.op('dve', lambda: nc.vector.reciprocal(out=tn[:, 11:12], in_=tn[:, 10:11]), r=[tn.name], w=[tn.name])
                    if m == 1:
                        kb.op('dve', lambda: nc.vector.tensor_tensor(out=tn[:, 11:12], in0=tn[:, 11:12], in1=par[:, 400:401], op=ALU.mult),
                              r=[tn.name, "par"], w=[tn.name])
                    for s0 in range(0, nkb, 4):
                        n4 = min(4, nkb - s0)
                        tb_ = 5 + ((s0 // 4) % 2)
                        for j in range(n4):
                            kb.op('pe', lambda j=j: nc.tensor.transpose(self.ps[:, tb_, j * 128:(j + 1) * 128],
                                                                         pt[:, (s0 + j) * 128:(s0 + j + 1) * 128], self.ident),
                                  r=[pt.name, "cm"], w=["ps%d" % tb_])
                        ptt = self.sm_w.next()
                        self.evac(ptt[:, :n4 * 128], self.ps[:, tb_, :n4 * 128], r=["ps%d" % tb_], w=[ptt.name])
                        for j in range(n4):
                            sbk = s0 + j
                            kb.op('pe', lambda j=j, sbk=sbk: nc.tensor.matmul(self.ps[:, 7, 0:128], ptt[:, j * 128:(j + 1) * 128], vv[:, sbk, :],
                                                                               start=(sbk == 0), stop=(sbk == nkb - 1)),
                                  r=[ptt.name, vt.name], w=["ps7"])
                    if m == 0:
                        kb.op('dve', lambda: nc.vector.tensor_scalar(out=ot[:, 0:128], in0=self.ps[:, 7, 0:128], scalar1=tn[:, 11:12], scalar2=None,
                                                                     op0=ALU.mult), r=["ps7", tn.name], w=[ot.name])
                    else:
                        kb.op('dve', lambda: nc.vector.scalar_tensor_tensor(out=ot[:, 0:128], in0=self.ps[:, 7, 0:128], scalar=tn[:, 11:12],
                                                                             in1=ot[:, 0:128], op0=ALU.mult, op1=ALU.add),
                              r=["ps7", tn.name, ot.name], w=[ot.name])
                kb.dma(O[lb * 128:(lb + 1) * 128, h * 128:(h + 1) * 128], ot[:, 0:128], r=[ot.name],
                       w=kT("B1", lb * 128, lb * 128 + 128, h * 128, h * 128 + 128))
        for tb in lbl:
            t0 = tb * 128
            o = self.t1k.next()
            kb.dma(o[:, :D], O[t0:t0 + 128, 0:D], r=kT("B1", t0, t0 + 128, 0, D), w=[o.name])
            self.head_norm_tok(o, 8, 128, RMS_EPS, center=False)
            ov = o[:, :D].rearrange("p (h d) -> p h d", h=8)
            kb.op('dve', lambda: nc.vector.tensor_tensor(out=ov, in0=ov, in1=par[:, 256:384].unsqueeze(1).to_broadcast([128, 8, 128]),
                                                         op=ALU.mult), r=[o.name, "par"], w=[o.name])
            kb.op('dve', lambda: nc.vector.tensor_scalar(out=o[:, :D], in0=o[:, :D], scalar1=1.0 - lam_init, scalar2=None, op0=ALU.mult),
                  r=[o.name], w=[o.name])
            self.transpose_store(o, 8, YIN, "A2", 0, t0)
        self.proj(YIN, "A2", D, w['w_out'], [(0, D, 'F', self.epiF(self.YT, "YT", 0))], include_ctx=not last)

    def ln_feat(self, zt, zv, n, li, which):
        nc, kb = self.nc, self.kb
        bi, bk = self.pbank()
        mv = self.ps[:, bi, 0:n]
        for c in range(8):
            kb.op('pe', lambda c=c: nc.tensor.matmul(mv, self.onesm[:], zv[:, c, :], start=(c == 0), stop=(c == 7)),
                  r=["onesm", zt.name], w=[bk])
        kb.op('dve', lambda: nc.vector.tensor_tensor(out=zv, in0=zv, in1=mv.unsqueeze(1).to_broadcast([128, 8, n]), op=ALU.subtract),
              r=[zt.name, bk], w=[zt.name])
        sq = self.t1k.next()
        sv = sq[:, :8 * n].rearrange("p (c t) -> p c t", c=8)
        kb.op('dve', lambda: nc.vector.tensor_tensor(out=sv, in0=zv, in1=zv, op=ALU.mult), r=[zt.name], w=[sq.name])
        bi2, bk2 = self.pbank()
        vv = self.ps[:, bi2, 0:n]
        for c in range(8):
            kb.op('pe', lambda c=c: nc.tensor.matmul(vv, self.onesm[:], sv[:, c, :], start=(c == 0), stop=(c == 7)),
                  r=["onesm", sq.name], w=[bk2])
        rs = self.tiny.next()
        kb.op('dve', lambda: nc.vector.tensor_scalar(out=rs[:, :n], in0=vv, scalar1=LN_EPS, scalar2=None, op0=ALU.add), r=[bk2], w=[rs.name])
        kb.op('act', lambda: nc.scalar.activation(out=rs[:, :n], in_=rs[:, :n], func=AF.Sqrt), r=[rs.name], w=[rs.name])
        kb.op('dve', lambda: nc.vector.reciprocal(out=rs[:, :n], in_=rs[:, :n]), r=[rs.name], w=[rs.name])
        kb.op('dve', lambda: nc.vector.tensor_tensor(out=zv, in0=zv, in1=rs[:, :n].unsqueeze(1).to_broadcast([128, 8, n]), op=ALU.mult),
              r=[zt.name, rs.name], w=[zt.name])
        g = self.lng[:, li * 2 + which, :].unsqueeze(2).to_broadcast([128, 8, n])
        bb = self.lnb[:, li * 2 + which, :].unsqueeze(2).to_broadcast([128, 8, n])
        kb.op('dve', lambda: nc.vector.tensor_tensor(out=zv, in0=zv, in1=g, op=ALU.mult), r=[zt.name, ("lng", li)], w=[zt.name])
        kb.op('dve', lambda: nc.vector.tensor_tensor(out=zv, in0=zv, in1=bb, op=ALU.add), r=[zt.name, ("lnb", li)], w=[zt.name])

    def post(self, li, b, last):
        nc, kb = self.nc, self.kb
        w = self.W[li]
        NB, NCH, NCC = self.NB, self.NCH, self.NCC
        tbs = list(range(NCC, NCH)) if last else list(range(NCH))
        for tb in tbs:
            t0 = tb * 128
            col = NB if tb < NCC else b
            ht = self.t1k.next()
            yt = self.t1k.next()
            hv = ht[:, :1024].rearrange("p (c t) -> p c t", c=8)
            yv = yt[:, :1024].rearrange("p (c t) -> p c t", c=8)
            kb.dma(hv, self.HT[:, t0:t0 + 128].rearrange("(c p) t -> p c t", p=128), r=kF("HT", 0, D, t0, t0 + 128), w=[ht.name])
            kb.dma(yv, self.YT[:, t0:t0 + 128].rearrange("(c p) t -> p c t", p=128), r=kF("YT", 0, D, t0, t0 + 128), w=[yt.name])
            gm = self.modcol(li, 2, col).to_broadcast([128, 8, 128])
            kb.op('dve', lambda: nc.vector.tensor_tensor(out=yv, in0=yv, in1=gm, op=ALU.mult), r=[yt.name] + self.modkeys(li, 2), w=[yt.name])
            kb.op('dve', lambda: nc.vector.scalar_tensor_tensor(out=ht[:, :1024], in0=ht[:, :1024], scalar=ALPHA, in1=yt[:, :1024],
                                                                 op0=ALU.mult, op1=ALU.add), r=[ht.name, yt.name], w=[ht.name])
            self.ln_feat(ht, hv, 128, li, 0)
            kb.dma(self.H1T[:, t0:t0 + 128].rearrange("(c p) t -> p c t", p=128), hv, r=[ht.name], w=kF("H1T", 0, D, t0, t0 + 128))
        grp = self.groups(512, include_ctx=not last)
        self.proj(self.H1T, "H1T", D, w['wq'], [(0, 2048, 'F', self.epiF(self.QPT, "QPT", 0))], premod=(li, 4, 3, b), groups=grp)
        for tb in tbs:
            self.peer_tile(li, b, tb, last)

    def peer_tile(self, li, b, tb, last):
        nc, kb = self.nc, self.kb
        w = self.W[li]
        NB, CTX, NCC = self.NB, self.CTX, self.NCC
        t0 = tb * 128
        col = NB if tb < NCC else b
        h1 = self.t1k.next()
        h1v = h1[:, :1024].rearrange("p (c t) -> p c t", c=8)
        kb.dma(h1v, self.H1T[:, t0:t0 + 128].rearrange("(c p) t -> p c t", p=128), r=kF("H1T", 0, D, t0, t0 + 128), w=[h1.name])
        pT = self.t1k.next()
        pTv = pT[:, :1024].rearrange("p (c t) -> p c t", c=8)
        kb.op('dve', lambda: nc.vector.tensor_tensor(out=pTv, in0=h1v, in1=self.modcol(li, 4, col).to_broadcast([128, 8, 128]), op=ALU.mult),
              r=[h1.name] + self.modkeys(li, 4), w=[pT.name])
        kb.op('dve', lambda: nc.vector.tensor_tensor(out=pTv, in0=pTv, in1=self.modcol(li, 3, col).to_broadcast([128, 8, 128]), op=ALU.add),
              r=[pT.name] + self.modkeys(li, 3), w=[pT.name])
        ptok = self.t1k.next()
        for c0 in (0, 4):
            bi, bk = self.pbank()
            for j in range(4):
                kb.op('pe', lambda j=j: nc.tensor.transpose(self.ps[:, bi, j * 128:(j + 1) * 128], pTv[:, c0 + j, :], self.ident),
                      r=[pT.name, "cm"], w=[bk])
            self.evac(ptok[:, c0 * 128:(c0 + 4) * 128], self.ps[:, bi, :], r=[bk], w=[ptok.name])
        qt = self.t2k.next()
        qv = qt[:, :2048].rearrange("p (j t) -> p j t", j=16)
        kb.dma(qv, self.QPT[:, t0:t0 + 128].rearrange("(j p) t -> p j t", p=128), r=kF("QPT", 0, 2048, t0, t0 + 128), w=[qt.name])
        sc = self.t2k.next()
        scv = sc[:, :2048].rearrange("p (j n) -> p j n", j=16)
        for j0 in range(0, 16, 4):
            bi, bk = self.pbank()
            for j in range(4):
                kb.op('pe', lambda j=j: nc.tensor.matmul(self.ps[:, bi, j * 128:(j + 1) * 128], qv[:, j0 + j, :], self.keysT[:, j0 + j, :],
                                                          start=True, stop=True), r=[qt.name, "keysT"], w=[bk])
            self.evac(scv[:, j0:j0 + 4, :], self.ps[:, bi, :].rearrange("p (j n) -> p j n", j=4), r=[bk], w=[sc.name])
        sc2 = self.t2k.next()
        sc2v = sc2[:, :2048].rearrange("p (j n) -> p j n", j=16)
        tp, ti, bs, bj, ef, ei, gt, act = self.smt
        tiu = ti[:, 0:256].bitcast(U32)
        stop_ = tp[:, 0:256].rearrange("p (j k) -> p j k", j=16)
        for j in range(16):
            kb.op('dve', lambda j=j: nc.vector.max(out=stop_[:, j, 0:8], in_=scv[:, j, :]), r=[sc.name], w=[tp.name])
            kb.op('dve', lambda j=j: nc.vector.match_replace(out=sc2v[:, j, :], in_to_replace=stop_[:, j, 0:8], in_values=scv[:, j, :],
                                                              imm_value=NEG), r=[sc.name, tp.name], w=[sc2.name])
            kb.op('dve', lambda j=j: nc.vector.max(out=stop_[:, j, 8:16], in_=sc2v[:, j, :]), r=[sc2.name], w=[tp.name])
            kb.op('dve', lambda j=j: nc.vector.max_index(out=tiu[:, j * 16:j * 16 + 8], in_max=stop_[:, j, 0:8], in_values=scv[:, j, :]),
                  r=[sc.name, tp.name], w=[ti.name])
            kb.op('dve', lambda j=j: nc.vector.max_index(out=tiu[:, j * 16 + 8:j * 16 + 16], in_max=stop_[:, j, 8:16], in_values=sc2v[:, j, :]),
                  r=[sc2.name, tp.name], w=[ti.name])
        kb.op('dve', lambda: nc.vector.tensor_copy(out=tp[:, 256:512], in_=tiu), r=[ti.name], w=[tp.name])
        st4 = tp[:, 0:256].rearrange("p (h c k) -> p h c k", h=8, c=2)
        it4 = tp[:, 256:512].rearrange("p (h c k) -> p h c k", h=8, c=2)
        cand = self.t2k.next()
        cv = cand[:, :2048].rearrange("p (h a b) -> p h a b", h=8, a=16)
        kb.op('dve', lambda: nc.vector.tensor_tensor(out=cv, in0=st4[:, :, 0, :].unsqueeze(3).to_broadcast([128, 8, 16, 16]),
                                                     in1=st4[:, :, 1, :].unsqueeze(2).to_broadcast([128, 8, 16, 16]), op=ALU.add),
              r=[tp.name], w=[cand.name])
        cand2 = self.t2k.next()
        bju = bj[:, 0:128].bitcast(U32)
        bsv = bs[:, 0:128].rearrange("p (h k) -> p h k", h=8)
        c1 = cand[:, :2048].rearrange("p (h n) -> p h n", h=8)
        c2 = cand2[:, :2048].rearrange("p (h n) -> p h n", h=8)
        for h in range(8):
            kb.op('dve', lambda h=h: nc.vector.max(out=bsv[:, h, 0:8], in_=c1[:, h, :]), r=[cand.name], w=[bs.name])
            kb.op('dve', lambda h=h: nc.vector.match_replace(out=c2[:, h, :], in_to_replace=bsv[:, h, 0:8], in_values=c1[:, h, :],
                                                              imm_value=NEG), r=[cand.name, bs.name], w=[cand2.name])
            kb.op('dve', lambda h=h: nc.vector.max(out=bsv[:, h, 8:16], in_=c2[:, h, :]), r=[cand2.name], w=[bs.name])
            kb.op('dve', lambda h=h: nc.vector.max_index(out=bju[:, h * 16:h * 16 + 8], in_max=bsv[:, h, 0:8], in_values=c1[:, h, :]),
                  r=[cand.name, bs.name], w=[bj.name])
            kb.op('dve', lambda h=h: nc.vector.max_index(out=bju[:, h * 16 + 8:h * 16 + 16], in_max=bsv[:, h, 8:16], in_values=c2[:, h, :]),
                  r=[cand2.name, bs.name], w=[bj.name])
        bau = bj[:, 128:256].bitcast(U32)
        bbu = bj[:, 256:384].bitcast(U32)
        kb.op('dve', lambda: nc.vector.tensor_single_scalar(out=bau, in_=bju, scalar=4, op=ALU.logical_shift_right), r=[bj.name], w=[bj.name])
        kb.op('dve', lambda: nc.vector.tensor_single_scalar(out=bbu, in_=bju, scalar=15, op=ALU.bitwise_and), r=[bj.name], w=[bj.name])
        kb.op('dve', lambda: nc.vector.tensor_copy(out=bs[:, 256:384], in_=bau), r=[bj.name], w=[bs.name])
        kb.op('dve', lambda: nc.vector.tensor_copy(out=bs[:, 384:512], in_=bbu), r=[bj.name], w=[bs.name])
        oh = self.t2k.next()
        ohv = oh[:, :2048].rearrange("p (h k a) -> p h k a", h=8, k=16)
        i16b = self.i16[:, :].unsqueeze(1).unsqueeze(1).to_broadcast([128, 8, 16, 16])
        for which in (0, 1):
            ab = bs[:, 256 + which * 128:384 + which * 128].rearrange("p (h k) -> p h k", h=8)
            kb.op('dve', lambda ab=ab: nc.vector.tensor_tensor(out=ohv, in0=ab.unsqueeze(3).to_broadcast([128, 8, 16, 16]), in1=i16b,
                                                                op=ALU.is_equal), r=[bs.name, "i16"], w=[oh.name])
            kb.op('dve', lambda which=which: nc.vector.tensor_tensor(out=ohv, in0=ohv,
                                                                      in1=it4[:, :, which, :].unsqueeze(2).to_broadcast([128, 8, 16, 16]),
                                                                      op=ALU.mult), r=[oh.name, tp.name], w=[oh.name])
            kb.op('dve', lambda which=which: nc.vector.reduce_sum(out=ef[:, which * 128:(which + 1) * 128].rearrange("p (h k) -> p h k", h=8),
                                                                   in_=ohv, axis=AX.X), r=[oh.name], w=[ef.name])
        kb.op('dve', lambda: nc.vector.scalar_tensor_tensor(out=ef[:, 256:384], in0=ef[:, 0:128], scalar=128.0, in1=ef[:, 128:256],
                                                             op0=ALU.mult, op1=ALU.add), r=[ef.name], w=[ef.name])
        eiv = ei[:, 0:128].bitcast(I32)
        kb.op('dve', lambda: nc.vector.tensor_copy(out=eiv, in_=ef[:, 256:384]), r=[ef.name], w=[ei.name])
        gv = gt[:, 0:128].rearrange("p (h k) -> p h k", h=8)
        kb.op('dve', lambda: nc.vector.tensor_tensor(out=gv, in0=bsv, in1=bsv[:, :, 0:1].to_broadcast([128, 8, 16]), op=ALU.subtract),
              r=[bs.name], w=[gt.name])
        kb.op('act', lambda: nc.scalar.activation(out=gt[:, 0:128], in_=gt[:, 0:128], func=AF.Exp), r=[gt.name], w=[gt.name])
        kb.op('dve', lambda: nc.vector.reduce_sum(out=gt[:, 128:136], in_=gv, axis=AX.X), r=[gt.name], w=[gt.name])
        kb.op('dve', lambda: nc.vector.reciprocal(out=gt[:, 136:144], in_=gt[:, 128:136]), r=[gt.name], w=[gt.name])
        kb.op('dve', lambda: nc.vector.tensor_tensor(out=gv, in0=gv, in1=gt[:, 136:144].unsqueeze(2).to_broadcast([128, 8, 16]), op=ALU.mult),
              r=[gt.name], w=[gt.name])
        for g4 in range(32):
            ut = self.ringS.next()
            uv = ut[:, :4096].rearrange("p (k d) -> p k d", k=4)
            for i in range(4):
                k = g4 * 4 + i
                kb.gather(uv[:, i, :], w['u'], eiv[:, k:k + 1], r=[ei.name], w=([ut.name] if i == 0 else []) + [(ut.name, i)])
            for i in range(4):
                k = g4 * 4 + i
                kb.op('dve', lambda i=i, k=k: nc.vector.scalar_tensor_tensor(out=uv[:, i, :], in0=uv[:, i, :], scalar=1.0, in1=ptok[:, :1024],
                                                                              op0=ALU.mult, op1=ALU.mult, accum_out=act[:, k:k + 1]),
                      r=[(ut.name, i), ut.name, ptok.name], w=[act.name, (ut.name, i)])
        a0 = act[:, 0:128]
        a1 = act[:, 128:256]
        kb.op('dve', lambda: nc.vector.tensor_tensor(out=a1, in0=a0, in1=a0, op=ALU.mult), r=[act.name], w=[act.name])
        kb.op('dve', lambda: nc.vector.tensor_scalar(out=a1, in0=a1, scalar1=0.044715, scalar2=1.0, op0=ALU.mult, op1=ALU.add),
              r=[act.name], w=[act.name])
        kb.op('dve', lambda: nc.vector.tensor_tensor(out=a1, in0=a1, in1=a0, op=ALU.mult), r=[act.name], w=[act.name])
        kb.op('act', lambda: nc.scalar.activation(out=a1, in_=a1, func=AF.Sigmoid, scale=2.0 * math.sqrt(2.0 / math.pi)), r=[act.name], w=[act.name])
        kb.op('dve', lambda: nc.vector.tensor_tensor(out=a1, in0=a1, in1=a0, op=ALU.mult), r=[act.name], w=[act.name])
        kb.op('dve', lambda: nc.vector.tensor_tensor(out=act[:, 256:384], in0=a1, in1=gt[:, 0:128], op=ALU.mult), r=[act.name, gt.name], w=[act.name])
        wts = act[:, 256:384]
        ft = self.t1k.next()
        for g4 in range(32):
            vt = self.ringS.next()
            vv = vt[:, :4096].rearrange("p (k d) -> p k d", k=4)
            for i in range(4):
                k = g4 * 4 + i
                kb.gather(vv[:, i, :], w['v'], eiv[:, k:k + 1], r=[ei.name], w=([vt.name] if i == 0 else []) + [(vt.name, i)])
            for i in range(4):
                k = g4 * 4 + i
                if k == 0:
                    kb.op('dve', lambda: nc.vector.tensor_scalar(out=ft[:, :1024], in0=vv[:, 0, :], scalar1=wts[:, 0:1], scalar2=None,
                                                                 op0=ALU.mult), r=[(vt.name, 0), vt.name, act.name], w=[ft.name])
                else:
                    kb.op('dve', lambda i=i, k=k: nc.vector.scalar_tensor_tensor(out=ft[:, :1024], in0=vv[:, i, :], scalar=wts[:, k:k + 1],
                                                                                  in1=ft[:, :1024], op0=ALU.mult, op1=ALU.add),
                          r=[(vt.name, i), vt.name, act.name, ft.name], w=[ft.name])
        fT = self.t1k.next()
        fv = self.transpose_to(ft, 8, fT)
        kb.op('dve', lambda: nc.vector.tensor_tensor(out=fv, in0=fv, in1=self.modcol(li, 5, col).to_broadcast([128, 8, 128]), op=ALU.mult),
              r=[fT.name] + self.modkeys(li, 5), w=[fT.name])
        kb.op('dve', lambda: nc.vector.scalar_tensor_tensor(out=fT[:, :1024], in0=h1[:, :1024], scalar=ALPHA, in1=fT[:, :1024],
                                                             op0=ALU.mult, op1=ALU.add), r=[h1.name, fT.name], w=[fT.name])
        self.ln_feat(fT, fv, 128, li, 1)
        if last:
            if tb >= NCC:
                ot = self.t1k.next()
                for c0 in (0, 4):
                    bi, bk = self.pbank()
                    for j in range(4):
                        kb.op('pe', lambda j=j: nc.tensor.transpose(self.ps[:, bi, j * 128:(j + 1) * 128], fv[:, c0 + j, :], self.ident),
                              r=[fT.name, "cm"], w=[bk])
                    self.evac(ot[:, c0 * 128:(c0 + 4) * 128], self.ps[:, bi, :], r=[bk], w=[ot.name])
                s0 = t0 - CTX
                kb.dma(self.out[b, s0:s0 + 128, :], ot[:, :1024], r=[ot.name], w=[("out", b, tb)])
        else:
            kb.dma(self.HT[:, t0:t0 + 128].rearrange("(c p) t -> p c t", p=128), fv, r=[fT.name], w=kF("HT", 0, D, t0, t0 + 128))


_CACHE = {}


def layer_weight_maps(inputs, kinds, layer_ids):
    m = {}
    cnt = {0: 0, 1: 0, 2: 0, 3: 0}
    for li, (kind, lid) in enumerate(zip(kinds, layer_ids)):
        p = "L%d_" % li
        j = lid // 4
        f = lambda a: np.ascontiguousarray(np.asarray(a, dtype=np.float32))
        m[p + "ada_w"] = f(inputs['ada_w'][lid])
        m[p + "ada_b"] = f(inputs['ada_b'][lid])
        m[p + "ln_g"] = f(inputs['ln_g'][lid])
        m[p + "ln_b"] = f(inputs['ln_b'][lid])
        m[p + "peer_wq"] = f(inputs['peer_wq'][lid])
        m[p + "peer_keys"] = f(inputs['peer_keys'][lid]).reshape(16, 128, 128)
        m[p + "peer_u"] = f(inputs['peer_u'][lid])
        m[p + "peer_v"] = f(inputs['peer_v'][lid])
        if kind == 0:
            m[p + "w_up"] = f(inputs['mlstm_w_up'][j])
            m[p + "conv_w"] = f(inputs['mlstm_conv_w'][j])
            m[p + "conv_b"] = f(inputs['mlstm_conv_b'][j])
            m[p + "w_qk"] = f(inputs['mlstm_w_qk'][j])
            m[p + "w_v"] = f(inputs['mlstm_w_v'][j])
            m[p + "w_gate"] = f(inputs['mlstm_w_gate'][j])
            m[p + "b_gate"] = f(inputs['mlstm_b_gate'][j])
            m[p + "norm_g"] = f(inputs['mlstm_norm_g'][j])
            m[p + "skip"] = f(inputs['mlstm_skip'][j])
            m[p + "w_down"] = f(inputs['mlstm_w_down'][j])
        elif kind == 1:
            m[p + "w_in"] = f(inputs['ssd_w_in'][j])
            m[p + "conv_w"] = f(inputs['ssd_conv_w'][j])
            m[p + "conv_b"] = f(inputs['ssd_conv_b'][j])
            m[p + "dt_bias"] = f(inputs['ssd_dt_bias'][j]).reshape(64)
            m[p + "a_log"] = f(inputs['ssd_a_log'][j]).reshape(64)
            m[p + "d"] = f(inputs['ssd_d'][j]).reshape(32)
            m[p + "norm_g"] = f(inputs['ssd_norm_g'][j])
            m[p + "w_out"] = f(inputs['ssd_w_out'][j])
        elif kind == 2:
            m[p + "w_qkv"] = f(inputs['diff_w_qkv'][j])
            m[p + "lam"] = f(inputs['diff_lambda'][j])
            m[p + "norm_g"] = f(inputs['diff_norm_g'][j])
            m[p + "w_out"] = f(inputs['diff_w_out'][j])
        else:
            m[p + "w_in"] = f(inputs['ret_w_in'][j])
            m[p + "decay"] = f(inputs['ret_decay_logit'][j]).reshape(8)
            m[p + "norm_g"] = f(inputs['ret_norm_g'][j])
            m[p + "w_out"] = f(inputs['ret_w_out'][j])
    return m


def run(inputs, NB, n_cores, kinds, layer_ids, final_last=True, trace=False):
    x = np.asarray(inputs['x'], np.float32)
    cx = np.asarray(inputs['ctx'], np.float32)
    c = np.asarray(inputs['c'], np.float32)
    c_ctx = np.asarray(inputs['c_ctx'], np.float32)
    B, SEQ, _ = x.shape
    CTX = cx.shape[1]
    assert B == NB * n_cores
    key = (NB, CTX, SEQ, tuple(kinds), tuple(layer_ids), final_last)
    if key not in _CACHE:
        _CACHE[key] = Prog(NB, CTX, SEQ, kinds, layer_ids, final_last)
    prog = _CACHE[key]
    consts = make_consts(SEQ)
    wm = layer_weight_maps(inputs, kinds, layer_ids)
    in_maps = []
    for ci in range(n_cores):
        sl = slice(ci * NB, (ci + 1) * NB)
        m = dict(wm)
        m.update(consts)
        m['x'] = np.ascontiguousarray(x[sl])
        m['ctx'] = np.ascontiguousarray(cx[sl])
        m['cT'] = np.ascontiguousarray(np.concatenate([c[sl].T, c_ctx[:, None]], axis=1))
        in_maps.append(m)
    res = run_bass_kernel_spmd(prog.nc, in_maps, core_ids=list(range(n_cores)), trace=trace)
    out = np.concatenate([np.asarray(r["out"]) for r in res.results], axis=0)
    return out.astype(np.float32), res


N_LAUNCH = 4


def kernel(**inputs):
    if N_LAUNCH == 1:
        out, _ = run(inputs, 4, 8, [0, 1, 2, 3], [0, 1, 2, 3], True)
        return out
    x = np.asarray(inputs['x'])
    outs = []
    for i in range(4):
        sub = dict(inputs)
        sl = slice(i * 8, (i + 1) * 8)
        sub['x'] = x[sl]
        sub['ctx'] = np.asarray(inputs['ctx'])[sl]
        sub['c'] = np.asarray(inputs['c'])[sl]
        o, _ = run(sub, 1, 8, [0, 1, 2, 3], [0, 1, 2, 3], True)
        outs.append(o)
    return np.concatenate(outs, axis=0)
```
